# Optimizing a Trainium2 kernel written in Bass

```python
import math
import jax, jax.numpy as jnp
from jax import lax
import numpy as np

D_MODEL = 1024
BATCH = 16
SEQ = 2048
DEPTH = 4

CTX_LEN = 256
GRID_W = 64
EPS = 1e-6
NEG_INF = -1e30
D_FF = 4 * D_MODEL

GLA_WIDTH = 3 * D_MODEL // 8
ATT_WIDTH = 3 * D_MODEL // 8
S5_WIDTH = D_MODEL - GLA_WIDTH - ATT_WIDTH

GLA_HEADS = 4
GLA_DV = GLA_WIDTH // GLA_HEADS
GLA_DK = GLA_DV // 2
GLA_RANK = 16
GLA_TAU = 16.0
GLA_CHUNK = 64

ATT_HD = 64
ATT_HEADS = ATT_WIDTH // ATT_HD
ATT_KV_HEADS = 2
ATT_GROUP = ATT_HEADS // ATT_KV_HEADS
WINDOW = 128
ATT_BLOCK = 128
ROPE_BASE = 10000.0

S5_GROUP = 16
S5_GROUPS = S5_WIDTH // S5_GROUP
S5_STATE = 64

IN_SPLITS = (GLA_HEADS * GLA_DK, GLA_HEADS * GLA_DK, GLA_WIDTH, GLA_WIDTH, GLA_RANK, GLA_RANK,
             ATT_WIDTH, ATT_KV_HEADS * ATT_HD, ATT_KV_HEADS * ATT_HD, S5_WIDTH)
IN_WIDTH = sum(IN_SPLITS)

kernel_name = 'hybrid_gla_swa_s5_prefix_dit'


def rmsnorm(x, w):
    xf = x.astype(jnp.float32)
    y = xf * lax.rsqrt(jnp.mean(xf * xf, axis=-1, keepdims=True) + EPS)
    return (y * w.astype(jnp.float32)).astype(x.dtype)


def split_columns(p):
    offsets = [int(o) for o in np.cumsum(IN_SPLITS)[:-1]]
    return jnp.split(p, offsets, axis=-1)


def to_heads(t, n_heads):
    b, n, _ = t.shape
    return t.reshape(b, n, n_heads, -1).transpose(0, 2, 1, 3)


def axial_rope_tables(rows):
    row = jnp.repeat(jnp.arange(rows, dtype=jnp.float32), GRID_W)
    col = jnp.tile(jnp.arange(GRID_W, dtype=jnp.float32), rows)
    n_freq = ATT_HD // 4
    inv_freq = ROPE_BASE ** (-jnp.arange(n_freq, dtype=jnp.float32) / n_freq)
    ang_r = row[:, None] * inv_freq
    ang_c = col[:, None] * inv_freq
    ang = jnp.concatenate([ang_r, ang_r, ang_c, ang_c], axis=-1)
    return jnp.cos(ang), jnp.sin(ang)


def apply_axial_rope(t, cos, sin):
    t1, t2, t3, t4 = jnp.split(t, 4, axis=-1)
    rot = jnp.concatenate([-t2, t1, -t4, t3], axis=-1)
    return t * cos.astype(t.dtype) + rot * sin.astype(t.dtype)


def gla_scan(q, k, v, log_a, h0, with_output):
    bsz, nh, t_len, dk = k.shape
    dv = v.shape[-1]
    nc = t_len // GLA_CHUNK
    chunked = lambda t: t.reshape(bsz, nh, nc, GLA_CHUNK, t.shape[-1])
    k, v, log_a = chunked(k), chunked(v), chunked(log_a)
    b = jnp.cumsum(log_a, axis=3)
    b_last = b[:, :, :, -1:, :]
    d_state = jnp.einsum('bhncd,bhnce->nbhde', k * jnp.exp(b_last - b), v)
    chunk_decay = jnp.exp(b_last[:, :, :, 0, :]).transpose(2, 0, 1, 3)

    def step(state, inp):
        decay, ds = inp
        return decay[..., None] * state + ds, state

    h_final, s_in = lax.scan(step, h0, (chunk_decay, d_state))
    if not with_output:
        return None, h_final
    qb = chunked(q) * jnp.exp(b)
    kb = k * jnp.exp(-b)
    o_inter = jnp.einsum('bhncd,nbhde->bhnce', qb, s_in)
    scores = jnp.einsum('bhnid,bhnjd->bhnij', qb, kb)
    lower = jnp.tril(jnp.ones((GLA_CHUNK, GLA_CHUNK), dtype=bool))
    o_intra = jnp.einsum('bhnij,bhnje->bhnie', jnp.where(lower, scores, 0.0), v)
    return (o_inter + o_intra).reshape(bsz, nh, t_len, dv), h_final


def gla_prepare(q, k, v, zf, zb, wa_f, ba_f, wa_b, ba_b):
    f = lambda t: t.astype(jnp.float32)
    q = to_heads(f(q), GLA_HEADS) * (GLA_DK ** -0.5)
    k = to_heads(f(k), GLA_HEADS)
    v = to_heads(f(v), GLA_HEADS)
    la_f = to_heads(jax.nn.log_sigmoid(f(zf) @ f(wa_f) + f(ba_f)) / GLA_TAU, GLA_HEADS)
    la_b = to_heads(jax.nn.log_sigmoid(f(zb) @ f(wa_b) + f(ba_b)) / GLA_TAU, GLA_HEADS)
    return q, k, v, la_f, la_b


def gla_readout(o, g, norm_w):
    bsz, nh, t_len, dv = o.shape
    o = rmsnorm(o.transpose(0, 2, 1, 3), norm_w).reshape(bsz, t_len, nh * dv)
    return (o * jax.nn.silu(g.astype(jnp.float32))).astype(g.dtype)


def gla_mixer(lat, cpx, wa_f, ba_f, wa_b, ba_b, norm_w, ctx_out):
    ql, kl, vl, laf_l, lab_l = gla_prepare(lat[0], lat[1], lat[2], lat[4], lat[5], wa_f, ba_f, wa_b, ba_b)
    qc, kc, vc, laf_c, lab_c = gla_prepare(cpx[0], cpx[1], cpx[2], cpx[4], cpx[5], wa_f, ba_f, wa_b, ba_b)
    flip = lambda t: jnp.flip(t, axis=2)
    h0 = jnp.zeros((kl.shape[0], GLA_HEADS, GLA_DK, GLA_DV), jnp.float32)
    oc_f, hc_f = gla_scan(qc, kc, vc, laf_c, h0, ctx_out)
    oc_b, hc_b = gla_scan(flip(qc), flip(kc), flip(vc), flip(lab_c), h0, ctx_out)
    ol_f, _ = gla_scan(ql, kl, vl, laf_l, hc_f, True)
    ol_b, _ = gla_scan(flip(ql), flip(kl), flip(vl), flip(lab_l), hc_b, True)
    out_lat = gla_readout(ol_f + flip(ol_b), lat[3], norm_w)
    out_ctx = gla_readout(oc_f + flip(oc_b), cpx[3], norm_w) if ctx_out else None
    return out_lat, out_ctx


def swa_latent(q, k, v, kc, vc, sink):
    bsz, t_len = q.shape[:2]
    nb = t_len // ATT_BLOCK
    qb = q.reshape(bsz, nb, ATT_BLOCK, ATT_KV_HEADS, ATT_GROUP, ATT_HD)

    def band(t):
        tp = jnp.pad(t, ((0, 0), (ATT_BLOCK, ATT_BLOCK), (0, 0), (0, 0)))
        tp = tp.reshape(bsz, nb + 2, ATT_BLOCK, ATT_KV_HEADS, ATT_HD)
        return jnp.concatenate([tp[:, :-2], tp[:, 1:-1], tp[:, 2:]], axis=2)

    kw, vw = band(k), band(v)
    scale = ATT_HD ** -0.5
    s_win = jnp.einsum('bnqkgd,bnjkd->bkgnqj', qb, kw).astype(jnp.float32) * scale
    qpos = jnp.arange(nb)[:, None] * ATT_BLOCK + jnp.arange(ATT_BLOCK)[None, :]
    kpos = (jnp.arange(nb)[:, None] - 1) * ATT_BLOCK + jnp.arange(3 * ATT_BLOCK)[None, :]
    valid = ((jnp.abs(qpos[:, :, None] - kpos[:, None, :]) <= WINDOW)
             & (kpos[:, None, :] >= 0) & (kpos[:, None, :] < t_len))
    s_win = jnp.where(valid, s_win, NEG_INF)
    s_ctx = jnp.einsum('bnqkgd,bjkd->bkgnqj', qb, kc).astype(jnp.float32) * scale
    s_sink = jnp.broadcast_to(sink[None, :, :, None, None, None], s_ctx.shape[:-1] + (1,))
    p = jax.nn.softmax(jnp.concatenate([s_win, s_ctx, s_sink], axis=-1), axis=-1).astype(v.dtype)
    n_win, n_ctx = kw.shape[2], kc.shape[1]
    o = (jnp.einsum('bkgnqj,bnjkd->bnqkgd', p[..., :n_win], vw)
         + jnp.einsum('bkgnqj,bjkd->bnqkgd', p[..., n_win:n_win + n_ctx], vc))
    return o.reshape(bsz, t_len, ATT_WIDTH)


def attend_context_only(qc, kc, vc, sink):
    bsz, n_ctx = qc.shape[:2]
    s = jnp.einsum('bqkgd,bjkd->bkgqj', qc, kc).astype(jnp.float32) * (ATT_HD ** -0.5)
    s_sink = jnp.broadcast_to(sink[None, :, :, None, None], s.shape[:-1] + (1,))
    p = jax.nn.softmax(jnp.concatenate([s, s_sink], axis=-1), axis=-1).astype(vc.dtype)
    o = jnp.einsum('bkgqj,bjkd->bqkgd', p[..., :n_ctx], vc)
    return o.reshape(bsz, n_ctx, ATT_WIDTH)


def swa_mixer(ql, kl, vl, qc, kc, vc, sink, cos, sin, ctx_out):
    bsz, t_len, _ = ql.shape
    n_ctx = kc.shape[1]
    ql = ql.reshape(bsz, t_len, ATT_KV_HEADS, ATT_GROUP, ATT_HD)
    kl = kl.reshape(bsz, t_len, ATT_KV_HEADS, ATT_HD)
    vl = vl.reshape(bsz, t_len, ATT_KV_HEADS, ATT_HD)
    kc = kc.reshape(bsz, n_ctx, ATT_KV_HEADS, ATT_HD)
    vc = vc.reshape(bsz, n_ctx, ATT_KV_HEADS, ATT_HD)
    ql = apply_axial_rope(ql, cos[:, None, None, :], sin[:, None, None, :])
    kl = apply_axial_rope(kl, cos[:, None, :], sin[:, None, :])
    sink = sink.astype(jnp.float32).reshape(ATT_KV_HEADS, ATT_GROUP)
    out_lat = swa_latent(ql, kl, vl, kc, vc, sink)
    out_ctx = None
    if ctx_out:
        qc = qc.reshape(bsz, n_ctx, ATT_KV_HEADS, ATT_GROUP, ATT_HD)
        out_ctx = attend_context_only(qc, kc, vc, sink)
    return out_lat, out_ctx


def s5_discretize(lam_re, lam_im, log_step, b_re, b_im):
    f = lambda t: t.astype(jnp.float32)
    lr = jnp.minimum(f(lam_re), -1e-4)
    li = f(lam_im)
    dt = jnp.exp(f(log_step))[:, None]
    mag = jnp.exp(lr * dt)
    ar, ai = mag * jnp.cos(li * dt), mag * jnp.sin(li * dt)
    den = lr * lr + li * li
    fr = ((ar - 1.0) * lr + ai * li) / den
    fi = (ai * lr - (ar - 1.0) * li) / den
    bbr = fr[..., None] * f(b_re) - fi[..., None] * f(b_im)
    bbi = fr[..., None] * f(b_im) + fi[..., None] * f(b_re)
    return ar, ai, bbr, bbi


def complex_linear_combine(e1, e2):
    a1r, a1i, b1r, b1i = e1
    a2r, a2i, b2r, b2i = e2
    return (a1r * a2r - a1i * a2i, a1r * a2i + a1i * a2r,
            a2r * b1r - a2i * b1i + b2r, a2r * b1i + a2i * b1r + b2i)


def s5_scan(u, ar, ai, bbr, bbi, h0=None):
    t_len = u.shape[1]
    ur = jnp.einsum('gph,btgh->btgp', bbr, u)
    ui = jnp.einsum('gph,btgh->btgp', bbi, u)
    a_r = jnp.broadcast_to(ar[None, None], (1, t_len) + ar.shape)
    a_i = jnp.broadcast_to(ai[None, None], (1, t_len) + ai.shape)
    pr, pi, xr, xi = lax.associative_scan(complex_linear_combine, (a_r, a_i, ur, ui), axis=1)
    if h0 is not None:
        h0r, h0i = h0
        xr = xr + pr * h0r[:, None] - pi * h0i[:, None]
        xi = xi + pr * h0i[:, None] + pi * h0r[:, None]
    return xr, xi


def s5_readout(c_re, c_im, xr, xi):
    return (jnp.einsum('ghp,btgp->btgh', c_re.astype(jnp.float32), xr)
            - jnp.einsum('ghp,btgp->btgh', c_im.astype(jnp.float32), xi))


def s5_direction(uc, ul, params, ctx_out):
    lam_re, lam_im, log_step, b_re, b_im, c_re, c_im = params
    ar, ai, bbr, bbi = s5_discretize(lam_re, lam_im, log_step, b_re, b_im)
    xcr, xci = s5_scan(uc, ar, ai, bbr, bbi)
    xlr, xli = s5_scan(ul, ar, ai, bbr, bbi, (xcr[:, -1], xci[:, -1]))
    y_lat = s5_readout(c_re, c_im, xlr, xli)
    y_ctx = s5_readout(c_re, c_im, xcr, xci) if ctx_out else None
    return y_lat, y_ctx


def s5_glu(y, glu_w, glu_b):
    bsz, t_len = y.shape[:2]
    y = jax.nn.gelu(y.reshape(bsz, t_len, S5_WIDTH))
    z = y @ glu_w.astype(jnp.float32) + glu_b.astype(jnp.float32)
    a, gate = jnp.split(z, 2, axis=-1)
    return a * jax.nn.sigmoid(gate)


def s5_mixer(ul, uc, fwd, bwd, d, glu_w, glu_b, ctx_out):
    groups = lambda t: t.astype(jnp.float32).reshape(t.shape[0], t.shape[1], S5_GROUPS, S5_GROUP)
    ul4, uc4 = groups(ul), groups(uc)
    flip = lambda t: jnp.flip(t, axis=1)
    yl_f, yc_f = s5_direction(uc4, ul4, fwd, ctx_out)
    yl_b, yc_b = s5_direction(flip(uc4), flip(ul4), bwd, ctx_out)
    d4 = d.astype(jnp.float32).reshape(S5_GROUPS, S5_GROUP)
    out_lat = s5_glu(yl_f + flip(yl_b) + d4 * ul4, glu_w, glu_b).astype(ul.dtype)
    out_ctx = s5_glu(yc_f + flip(yc_b) + d4 * uc4, glu_w, glu_b).astype(uc.dtype) if ctx_out else None
    return out_lat, out_ctx


def sq_relu_mlp(h, w1, w2):
    return jnp.square(jax.nn.relu(h @ w1)) @ w2


def setup_inputs(seed: int = 0) -> dict:
    key = jax.random.key(seed)
    keys = iter(jax.random.split(key, 48))
    f32 = jnp.float32

    def normal(shape, std):
        return jax.random.normal(next(keys), shape, f32) * std

    def gain(shape):
        return 1.0 + normal(shape, 0.02)

    L, G, P, H = DEPTH, S5_GROUPS, S5_STATE, S5_GROUP
    inputs = {}
    inputs['x'] = normal((BATCH, SEQ, D_MODEL), 1.0)
    inputs['c'] = normal((BATCH, D_MODEL), 1.0)
    inputs['ctx'] = normal((BATCH, CTX_LEN, D_MODEL), 1.0)
    inputs['c_ctx'] = normal((D_MODEL,), 1.0)
    inputs['w_mod'] = normal((L, D_MODEL, 6 * D_MODEL), 0.5 * D_MODEL ** -0.5)
    inputs['b_mod'] = normal((L, 6 * D_MODEL), 0.01)
    inputs['norm1_w'] = gain((L, D_MODEL))
    inputs['norm2_w'] = gain((L, D_MODEL))
    inputs['w_in'] = normal((L, D_MODEL, IN_WIDTH), D_MODEL ** -0.5)
    inputs['gla_wa_f'] = normal((L, GLA_RANK, GLA_HEADS * GLA_DK), GLA_RANK ** -0.5)
    inputs['gla_ba_f'] = normal((L, GLA_HEADS * GLA_DK), 0.1)
    inputs['gla_wa_b'] = normal((L, GLA_RANK, GLA_HEADS * GLA_DK), GLA_RANK ** -0.5)
    inputs['gla_ba_b'] = normal((L, GLA_HEADS * GLA_DK), 0.1)
    inputs['gla_norm_w'] = gain((L, GLA_DV))
    inputs['attn_sink'] = normal((L, ATT_HEADS), 0.5)
    for tag in ('f', 'b'):
        inputs['s5_lam_re_' + tag] = -0.5 + normal((L, G, P), 0.01)
        inputs['s5_lam_im_' + tag] = jnp.pi * jnp.arange(P, dtype=f32) + normal((L, G, P), 0.01)
        inputs['s5_log_step_' + tag] = jax.random.uniform(next(keys), (L, G), f32, math.log(1e-3), math.log(1e-1))
        inputs['s5_b_re_' + tag] = normal((L, G, P, H), 0.5)
        inputs['s5_b_im_' + tag] = normal((L, G, P, H), 0.5)
        inputs['s5_c_re_' + tag] = normal((L, G, H, P), P ** -0.5)
        inputs['s5_c_im_' + tag] = normal((L, G, H, P), P ** -0.5)
    inputs['s5_d'] = normal((L, S5_WIDTH), 0.5)
    inputs['glu_w'] = normal((L, S5_WIDTH, 2 * S5_WIDTH), S5_WIDTH ** -0.5)
    inputs['glu_b'] = normal((L, 2 * S5_WIDTH), 0.01)
    inputs['w_out'] = normal((L, D_MODEL, D_MODEL), D_MODEL ** -0.5)
    inputs['mlp_w1'] = normal((L, D_MODEL, D_FF), D_MODEL ** -0.5)
    inputs['mlp_w2'] = normal((L, D_FF, D_MODEL), D_FF ** -0.5)
    inputs['final_norm_w'] = gain((D_MODEL,))
    return inputs


def reference(x, c, ctx, c_ctx, w_mod, b_mod, norm1_w, norm2_w, w_in,
              gla_wa_f, gla_ba_f, gla_wa_b, gla_ba_b, gla_norm_w, attn_sink,
              s5_lam_re_f, s5_lam_im_f, s5_log_step_f, s5_b_re_f, s5_b_im_f, s5_c_re_f, s5_c_im_f,
              s5_lam_re_b, s5_lam_im_b, s5_log_step_b, s5_b_re_b, s5_b_im_b, s5_c_re_b, s5_c_im_b,
              s5_d, glu_w, glu_b, w_out, mlp_w1, mlp_w2, final_norm_w):
    rows = x.shape[1] // GRID_W
    cos, sin = axial_rope_tables(rows)
    c_act = jax.nn.silu(c)
    cc_act = jax.nn.silu(c_ctx)
    xc = ctx
    for l in range(DEPTH):
        last = l == DEPTH - 1
        mod = (c_act @ w_mod[l] + b_mod[l])[:, None, :]
        mod_c = (cc_act @ w_mod[l] + b_mod[l])[None, None, :]
        sh1, sc1, g1, sh2, sc2, g2 = jnp.split(mod, 6, axis=-1)
        csh1, csc1, cg1, csh2, csc2, cg2 = jnp.split(mod_c, 6, axis=-1)

        h = rmsnorm(x, norm1_w[l]) * (1 + sc1) + sh1
        hc = rmsnorm(xc, norm1_w[l]) * (1 + csc1) + csh1
        lat = split_columns(h @ w_in[l])
        cpx = split_columns(hc @ w_in[l])

        a_lat, a_ctx = gla_mixer(lat[0:6], cpx[0:6], gla_wa_f[l], gla_ba_f[l], gla_wa_b[l], gla_ba_b[l],
                                 gla_norm_w[l], not last)
        b_lat, b_ctx = swa_mixer(lat[6], lat[7], lat[8], cpx[6], cpx[7], cpx[8], attn_sink[l], cos, sin, not last)
        fwd = (s5_lam_re_f[l], s5_lam_im_f[l], s5_log_step_f[l], s5_b_re_f[l], s5_b_im_f[l], s5_c_re_f[l], s5_c_im_f[l])
        bwd = (s5_lam_re_b[l], s5_lam_im_b[l], s5_log_step_b[l], s5_b_re_b[l], s5_b_im_b[l], s5_c_re_b[l], s5_c_im_b[l])
        s_lat, s_ctx = s5_mixer(lat[9], cpx[9], fwd, bwd, s5_d[l], glu_w[l], glu_b[l], not last)

        x = x + g1 * (jnp.concatenate([a_lat, b_lat, s_lat], axis=-1) @ w_out[l])
        x = x + g2 * sq_relu_mlp(rmsnorm(x, norm2_w[l]) * (1 + sc2) + sh2, mlp_w1[l], mlp_w2[l])
        if not last:
            xc = xc + cg1 * (jnp.concatenate([a_ctx, b_ctx, s_ctx], axis=-1) @ w_out[l])
            xc = xc + cg2 * sq_relu_mlp(rmsnorm(xc, norm2_w[l]) * (1 + csc2) + csh2, mlp_w1[l], mlp_w2[l])
    return rmsnorm(x, final_norm_w)
```

```python
import numpy as np
import concourse.bass as bass
import concourse.mybir as mybir
from concourse.bass_utils import run_bass_kernel_spmd
from contextlib import ExitStack

F32 = mybir.dt.float32
BF16 = mybir.dt.bfloat16
I32 = mybir.dt.int32
U8 = mybir.dt.uint8
AF = mybir.ActivationFunctionType
ALU = mybir.AluOpType
AX = mybir.AxisListType

D = 1024
KT = 8
T = 2304
TCX = 256
TLAT = 2048
NCH = 18
L = 4
NBC = 2
BLKS = [(0, 256), (256, 512), (768, 512), (1280, 512), (1792, 512)]
EPS = 1e-6
NFM_G, NTM_G = 640, 1024
NFM_A, NTM_A = 1024, 128
NFM_S = 256
TWO_PI = 2.0 * np.pi


class Res:
    __slots__ = ("name", "w", "r", "excl")

    def __init__(self, name="", excl=False):
        self.name = name
        self.w = None
        self.r = {}
        self.excl = excl


class Rec:
    EPOCH = 30000
    SAME_ENGINE_SYNC = True

    def __init__(self, nc, es, n_dma_sems=12):
        self.nc = nc
        self.es = es
        self.prog = {k: [] for k in ("pe", "act", "dve", "pool", "sp")}
        self.sems = []
        self.csem = {}
        self.ccnt = {}
        self.waited = {k: {} for k in self.prog}
        for k in ("pe", "act", "dve", "pool"):
            self._new_csem(k)
        self.dsem = {}
        self.drr = {}
        for q in ("sp", "pool", "act"):
            self.dsem[q] = [[self._new_sem(f"d_{q}_{i}"), 0] for i in range(n_dma_sems)]
            self.drr[q] = 0
        self.n_ops = 0
        self.n_waits = 0

    def _new_sem(self, name):
        s = self.es.enter_context(self.nc.semaphore(name))
        self.sems.append(s)
        return len(self.sems) - 1

    def _new_csem(self, e):
        self.csem[e] = self._new_sem(f"c_{e}_{len(self.sems)}")
        self.ccnt[e] = 0

    def _waits(self, e, deps):
        wd = self.waited[e]
        best = {}
        for (si, val, src) in deps:
            if src == e and (e == "pe" or not self.SAME_ENGINE_SYNC):
                continue
            if wd.get(si, 0) >= val:
                continue
            if best.get(si, 0) < val:
                best[si] = val
        for si, val in best.items():
            wd[si] = val
            self.prog[e].append(("wait", si, val))
            self.n_waits += 1

    def _deps(self, reads, writes, e=None):
        deps = []
        for r in reads:
            if r.w is not None:
                deps.append(r.w)
            if r.excl:
                for si, (val, src) in r.r.items():
                    if src != e:
                        deps.append((si, val, src))
        for w in writes:
            if w.w is not None:
                deps.append(w.w)
            for si, (val, src) in w.r.items():
                deps.append((si, val, src))
        return deps

    def _mark(self, tok, reads, writes):
        si, val, src = tok
        for r in reads:
            r.r[si] = (val, src)
        for w in writes:
            w.w = tok
            w.r = {}

    def op(self, e, fn, reads=(), writes=(), sig=True):
        self._waits(e, self._deps(reads, writes, e))
        if not sig and self.ccnt[e] >= self.EPOCH - 1:
            sig = True
        if sig and self.ccnt[e] >= self.EPOCH:
            self._new_csem(e)
        si = self.csem[e]
        if sig:
            self.ccnt[e] += 1
            tok = (si, self.ccnt[e], e)
            self.prog[e].append(("op", fn, si, 1))
        else:
            tok = (si, self.ccnt[e] + 1, e)
            self.prog[e].append(("op", fn, None, 0))
        self._mark(tok, reads, writes)
        self.n_ops += 1
        return tok

    def dma(self, q, fn, reads=(), writes=()):
        deps = self._deps(reads, writes)
        slot = self.dsem[q][self.drr[q]]
        self.drr[q] = (self.drr[q] + 1) % len(self.dsem[q])
        if slot[1] > 0:
            deps.append((slot[0], slot[1], "dma"))
        self._waits(q, deps)
        slot[1] += 16
        tok = (slot[0], slot[1], "dma")
        self.prog[q].append(("op", fn, slot[0], 16))
        self._mark(tok, reads, writes)
        self.n_ops += 1
        return tok

    def all_tokens(self):
        toks = [(self.csem[e], self.ccnt[e], e) for e in ("pe", "act", "dve", "pool") if self.ccnt[e] > 0]
        for q in self.dsem:
            toks += [(s[0], s[1], "dma") for s in self.dsem[q] if s[1] > 0]
        return toks

    def barrier(self):
        toks = self.all_tokens()
        for e in self.prog:
            self._waits(e, [t for t in toks if not (t[2] == e and e == "pe")])

    def emit(self):
        nc = self.nc
        sems = self.sems
        prog = self.prog

        def replay(name, eng):
            for it in prog[name]:
                if it[0] == "wait":
                    eng.wait_ge(sems[it[1]], it[2])
                else:
                    ins = it[1](eng)
                    if it[3]:
                        ins.then_inc(sems[it[2]], it[3])

        with nc.Block() as block:
            @block.sync
            def _(e):
                replay("sp", e)

            @block.tensor
            def _(e):
                replay("pe", e)

            @block.scalar
            def _(e):
                replay("act", e)

            @block.vector
            def _(e):
                replay("dve", e)

            @block.gpsimd
            def _(e):
                replay("pool", e)


class Tl:
    def __init__(self, ap, name=""):
        self.ap = ap
        self.r = Res(name)

    def __getitem__(self, k):
        return self.ap[k]


class Arena:
    def __init__(self, nc, es, nbytes):
        self.t = es.enter_context(nc.sbuf_tensor("arena", [128, nbytes], U8))
        self.off = 0
        self.cap = nbytes
        self.peak = 0

    def alloc(self, shape, dtype, name=""):
        sz = mybir.dt.size(dtype)
        n = int(np.prod(shape))
        nb = (n * sz + 63) // 64 * 64
        assert self.off + nb <= self.cap, ("SBUF arena overflow", name, self.off, nb, self.cap)
        ap = self.t[:, self.off:self.off + n * sz].bitcast(dtype)
        self.off += nb
        self.peak = max(self.peak, self.off)
        if len(shape) == 2:
            ap = ap.rearrange("p (a b) -> p a b", b=shape[1])
        elif len(shape) == 3:
            ap = ap.rearrange("p (a b c) -> p a b c", b=shape[1], c=shape[2])
        elif len(shape) == 4:
            ap = ap.rearrange("p (a b c d) -> p a b c d", b=shape[1], c=shape[2], d=shape[3])
        return Tl(ap, name)

    def mark(self):
        return self.off

    def release(self, m):
        self.off = m


def build_program(n_layers=L, n_b=NBC, dbg=False, stages="ngaswm", prologue=True):
    nc = bass.Bass("TRN2", target_bir_lowering=False)

    def din(name, shape, dt=F32):
        return nc.dram_tensor(name, list(shape), dt, kind="ExternalInput").ap()

    xin = din("xin", [NBC, 128, KT, T])
    cT_d = din("cT", [128, KT, 3])
    wmod_d = din("w_mod", [L, D, 6 * D])
    bmod_d = din("bmod", [L, 128, 48])
    nrm_d = din("nrm", [128, 9, KT])
    wfm_d = din("wfm", [L, D, 1920])
    wtm_d = din("wtm", [L, D, 1152])
    wa_d = din("wa", [L, 48, 512])
    ba_d = din("ba", [L, 512])
    gnw_d = din("gnw", [L, 384])
    sink_d = din("sink", [L, 6])
    wout_d = din("w_out", [L, D, D])
    w1_d = din("mlp_w1", [L, D, 4 * D])
    w2_d = din("mlp_w2", [L, 4 * D, D])
    gluw_d = din("glu_w", [L, 256, 512])
    glub_d = din("glub", [L, 128, 4])
    s5lam_d = din("s5lam", [L, 2, 128, 2, 8])
    s5step_d = din("s5step", [L, 2, 128, 8])
    s5B_d = din("s5B", [L, 2, 128, 2, 8, 16])
    s5C_d = din("s5C", [L, 2, 128, 2, 8, 16])
    s5d_d = din("s5d", [L, 128, 2])
    ctri_d = din("c_tri", [128, 4, 128])
    cmask_d = din("c_mask", [128, 2, 128])
    cident_d = din("c_ident", [128, 128])
    crope_d = din("c_rope", [128, 2, 2048])
    cbd32_d = din("c_bd32", [128, 128])
    cpp_d = din("c_pp", [128, 2])
    cm_d = din("c_m", [128, 288])
    out_d = nc.dram_tensor("out", [NBC, 128, KT, TLAT], F32, kind="ExternalOutput").ap()
    h_scr = nc.dram_tensor("dbg_h" if dbg else "h_scr", [128, KT, T], BF16, kind="ExternalOutput" if dbg else "Internal").ap()
    m_scr = nc.dram_tensor("dbg_m" if dbg else "m_scr", [128, KT, T], BF16, kind="ExternalOutput" if dbg else "Internal").ap()
    g_scr = nc.dram_tensor("g_scr", [128, NCH, 384], BF16, kind="Internal").ap()
    of_scr = nc.dram_tensor("of_scr", [128, NCH, 384], F32, kind="Internal").ap()
    qb_scr = nc.dram_tensor("qb_scr", [128, 2, NCH, 256], BF16, kind="Internal").ap()
    ds_scr = nc.dram_tensor("ds_scr", [128, 2, NCH, 192], F32, kind="Internal").ap()
    if dbg:
        dbg_xa = nc.dram_tensor("dbg_xa", [128, KT, T], F32, kind="ExternalOutput").ap()
        dbg_xb = nc.dram_tensor("dbg_xb", [128, KT, T], F32, kind="ExternalOutput").ap()

    es = ExitStack()
    with es:
        R = Rec(nc, es)
        AR = Arena(nc, es, 196000)
        pst = [es.enter_context(nc.psum_tensor(f"ps{i}", [128, 512], F32)) for i in range(8)]
        psr = [Res(f"ps{i}", excl=True) for i in range(8)]
        psi = [0]

        def PS():
            k = psi[0]
            psi[0] = (k + 1) % 8
            return pst[k], psr[k]

        def mm(out, lhsT, rhs, start, stop, reads, wres, sig=None, **kw):
            if sig is None:
                sig = stop
            R.op("pe", lambda e: e.matmul(out, lhsT=lhsT, rhs=rhs, start=start, stop=stop, **kw),
                 reads=reads, writes=[wres], sig=sig)

        def act(out, in_, func, reads, writes, **kw):
            R.op("act", lambda e: e.activation(out=out, in_=in_, func=func, **kw), reads=reads, writes=writes)

        def tt(eng, out, in0, in1, op, reads, writes):
            R.op(eng, lambda e: e.tensor_tensor(out=out, in0=in0, in1=in1, op=op), reads=reads, writes=writes)

        def stt(eng, out, in0, scalar, in1, op0, op1, reads, writes):
            R.op(eng, lambda e: e.scalar_tensor_tensor(out=out, in0=in0, scalar=scalar, in1=in1, op0=op0, op1=op1),
                 reads=reads, writes=writes)

        def ts(eng, out, in0, s1, s2, op0, op1, reads, writes):
            if s2 is None:
                R.op(eng, lambda e: e.tensor_scalar(out=out, in0=in0, scalar1=s1, scalar2=None, op0=op0),
                     reads=reads, writes=writes)
            else:
                R.op(eng, lambda e: e.tensor_scalar(out=out, in0=in0, scalar1=s1, scalar2=s2, op0=op0, op1=op1),
                     reads=reads, writes=writes)

        def cp(eng, out, in_, reads, writes):
            if eng == "act":
                act(out, in_, AF.Copy, reads, writes)
            else:
                R.op(eng, lambda e: e.tensor_copy(out=out, in_=in_), reads=reads, writes=writes)

        def memset(eng, out, val, writes):
            R.op(eng, lambda e: e.memset(out, val), writes=writes)

        def dma(q, out, in_, reads, writes):
            R.dma(q, lambda e: e.dma_start(out=out, in_=in_), reads=reads, writes=writes)

        def recip(out, in_, reads, writes):
            R.op("dve", lambda e: e.reciprocal(out=out, in_=in_), reads=reads, writes=writes)

        def treduce(out, in_, reads, writes):
            R.op("dve", lambda e: e.tensor_reduce(out=out, in_=in_, axis=AX.X, op=ALU.add), reads=reads, writes=writes)

        def petr(out, in_, idn, reads, wres):
            R.op("pe", lambda e: e.transpose(out=out, in_=in_, identity=idn), reads=reads, writes=[wres])

        def tscan(out, d0, d1, reads, writes):
            R.op("dve", lambda e: e.tensor_tensor_scan(out=out, data0=d0, data1=d1, initial=0.0, op0=ALU.mult, op1=ALU.add),
                 reads=reads, writes=writes)

        x = AR.alloc([KT, T], F32, "x")
        xres = [Res(f"x{b}") for b in range(len(BLKS))]
        ident = AR.alloc([128], F32, "ident")
        identb = AR.alloc([128], BF16, "identb")
        onesb = AR.alloc([128], BF16, "onesb")
        maskb = AR.alloc([2, 128], BF16, "maskb")
        nrm = AR.alloc([9, KT], F32, "nrm")
        modt = AR.alloc([L, 48, 3], F32, "mod")
        A1 = AR.alloc([L, KT, 3], F32, "A1")
        A2 = AR.alloc([L, KT, 3], F32, "A2")
        dma("sp", ident[:], cident_d, [], [ident.r])
        dma("pool", identb[:], cident_d, [], [identb.r])
        dma("pool", maskb[:], cmask_d, [], [maskb.r])
        dma("sp", nrm[:], nrm_d, [], [nrm.r])
        memset("dve", onesb[:], 1.0, [onesb.r])

        mk = AR.mark()
        c32 = AR.alloc([KT, 3], F32, "c32")
        cact = AR.alloc([KT, 3], BF16, "cact")
        bmod = AR.alloc([L, 48], F32, "bmod")
        wm = [AR.alloc([KT, 768], BF16, f"wm{i}") for i in range(2)]
        dma("sp", c32[:], cT_d, [], [c32.r])
        dma("sp", bmod[:], bmod_d.rearrange("l p m -> p l m"), [], [bmod.r])
        act(cact[:], c32[:], AF.Silu, [c32.r], [cact.r])
        for l in range(n_layers if prologue else 0):
            ps, pr = PS()
            wv = wmod_d[l].rearrange("(kt p) c -> p kt c", p=128)
            for ch in range(8):
                w_ = wm[ch % 2]
                dma("pool", w_[:], wv[:, :, ch * 768:(ch + 1) * 768], [], [w_.r])
                for m in range(6):
                    col = (ch * 6 + m) * 3
                    for kt in range(KT):
                        mm(ps[:, col:col + 3], w_[:, kt, m * 128:(m + 1) * 128], cact[:, kt, :], kt == 0, kt == KT - 1,
                           [w_.r, cact.r], pr)
            tt("dve", modt[:, l], ps[:, 0:144].rearrange("p (m j) -> p m j", j=3),
               bmod[:, l, :].unsqueeze(2).to_broadcast([128, 48, 3]), ALU.add, [pr, bmod.r], [modt.r])
            stt("dve", A1[:, l], modt[:, l, 8:16, :], 1.0, nrm[:, l, :].unsqueeze(2).to_broadcast([128, KT, 3]),
                ALU.add, ALU.mult, [modt.r, nrm.r], [A1.r])
            stt("dve", A2[:, l], modt[:, l, 32:40, :], 1.0, nrm[:, 4 + l, :].unsqueeze(2).to_broadcast([128, KT, 3]),
                ALU.add, ALU.mult, [modt.r, nrm.r], [A2.r])
        R.barrier()
        AR.release(mk)

        def SH1(l, kt, j): return modt[:, l, 0 + kt, j:j + 1]
        def G1(l, kt, j): return modt[:, l, 16 + kt, j:j + 1]
        def SH2(l, kt, j): return modt[:, l, 24 + kt, j:j + 1]
        def G2(l, kt, j): return modt[:, l, 40 + kt, j:j + 1]

        def rms_block(bi, Asc, Bsh, hb, tmp_sq, rstd, tmpf):
            t0, N = BLKS[bi]
            xr = xres[bi]
            act(tmp_sq[:, :, 0:N], x[:, :, t0:t0 + N], AF.Square, [xr], [tmp_sq.r])
            ps, pr = PS()
            for kt in range(KT):
                mm(ps[:, 0:N], onesb[:], tmp_sq[:, kt, 0:N], kt == 0, kt == KT - 1, [onesb.r, tmp_sq.r], pr)
            act(rstd[:, 0:N], ps[:, 0:N], AF.Sqrt, [pr], [rstd.r], scale=1.0 / D, bias=EPS)
            recip(rstd[:, 0:N], rstd[:, 0:N], [rstd.r], [rstd.r])
            for kt in range(KT):
                tf = tmpf[kt % 2]
                tt("dve", tf[:, 0:N], x[:, kt, t0:t0 + N], rstd[:, 0:N], ALU.mult, [xr, rstd.r], [tf.r])
                act(hb[:, kt, 0:N], tf[:, 0:N], AF.Identity, [tf.r, modt.r, A1.r, A2.r], [hb.r],
                    scale=Asc(kt), bias=Bsh(kt))

        def load_w(tile, src_view, ncols, step=512):
            for c0 in range(0, ncols, step):
                c1 = min(ncols, c0 + step)
                dma("pool", tile[:, :, c0:c1], src_view[:, :, c0:c1], [], [tile.r])

        def stage_norm1(b, l):
            mk = AR.mark()
            hb = [AR.alloc([KT, 512], BF16, f"hb{i}") for i in range(2)]
            sq = AR.alloc([KT, 512], BF16, "sq")
            rstd = AR.alloc([512], F32, "rstd")
            tmpf = [AR.alloc([512], F32, f"tf{i}") for i in range(2)]
            for bi, (t0, N) in enumerate(BLKS):
                j = 2 if bi == 0 else b
                h_ = hb[bi % 2]
                rms_block(bi, lambda kt: A1[:, l, kt, j:j + 1], lambda kt: SH1(l, kt, j), h_, sq, rstd, tmpf)
                dma("sp", h_scr[:, :, t0:t0 + N], h_[:, :, 0:N], [h_.r], [hscr_r])
            R.barrier()
            AR.release(mk)

        hscr_r = Res("h_scr")
        mscr_r = Res("m_scr")

        gscr_r = Res("g_scr")

        def stage_gla(b, l):
            mk = AR.mark()
            qT = AR.alloc([2, T], BF16, "qT")
            kT = AR.alloc([2, T], BF16, "kT")
            ktok = AR.alloc([NCH, 256], BF16, "ktok")
            vtok = AR.alloc([NCH, 384], BF16, "vtok")
            la = AR.alloc([NCH, 2, 256], F32, "la")
            tri = AR.alloc([4, 128], F32, "tri")
            nwb = AR.alloc([384], F32, "nwb")
            dma("sp", tri[:], ctri_d, [], [tri.r])
            dma("sp", nwb[:], gnw_d[l].partition_broadcast(128), [], [nwb.r])
            mk2 = AR.mark()
            zT = AR.alloc([T], BF16, "zT")
            wa = AR.alloc([512], BF16, "wa")
            bab = AR.alloc([512], F32, "bab")
            dma("pool", wa[0:48, :], wa_d[l], [], [wa.r])
            dma("sp", bab[:], ba_d[l].partition_broadcast(128), [], [bab.r])
            hb = [AR.alloc([KT, 512], BF16, "hb0")] * 2
            mk3 = AR.mark()
            Wg = AR.alloc([KT, NFM_G], BF16, "Wgf")
            load_w(Wg, wfm_d[l].rearrange("(kt p) c -> p kt c", p=128)[:, :, 0:NFM_G], NFM_G, 640)
            for bi, (t0, N) in enumerate(BLKS):
                h_ = hb[bi % 2]
                dma("sp", h_[:, :, 0:N], h_scr[:, :, t0:t0 + N], [hscr_r], [h_.r])
                for m in range(5):
                    ps, pr = PS()
                    for kt in range(KT):
                        mm(ps[:, 0:N], Wg[:, kt, m * 128:(m + 1) * 128], h_[:, kt, 0:N], kt == 0, kt == KT - 1,
                           [Wg.r, h_.r], pr)
                    if m < 2:
                        cp("act", qT[:, m, t0:t0 + N], ps[:, 0:N], [pr], [qT.r])
                    elif m < 4:
                        cp("dve", kT[:, m - 2, t0:t0 + N], ps[:, 0:N], [pr], [kT.r])
                    else:
                        cp("act", zT[:, t0:t0 + N], ps[:, 0:N], [pr], [zT.r])
            R.barrier()
            AR.release(mk3)
            import os
            if int(os.environ.get("GSTEP", "9")) == 1:
                AR.release(mk)
                return
            Wg = AR.alloc([KT, 512], BF16, "Wgt")
            tg = AR.alloc([384], F32, "tg")
            tl = AR.alloc([512], F32, "tl")
            gst = [AR.alloc([384], BF16, f"gst{i}") for i in range(2)]
            wtv = wtm_d[l].rearrange("(kt p) c -> p kt c", p=128)
            import os
            TMW = int(os.environ.get("TMW", "15"))
            TMCH = int(os.environ.get("TMCH", "99"))
            for half in range(2 if TMW & 8 else 1):
                dma("pool", Wg[:], wtv[:, :, half * 512:(half + 1) * 512], [], [Wg.r])
                for bi, (t0, N) in enumerate(BLKS):
                    h_ = hb[bi % 2]
                    dma("sp", h_[:, :, 0:N], h_scr[:, :, t0:t0 + N], [hscr_r], [h_.r])
                    for cc in range(N // 128):
                        ch = t0 // 128 + cc
                        if ch >= TMCH:
                            continue
                        tok = slice(cc * 128, (cc + 1) * 128)
                        psA, prA = PS()
                        for kt in range(KT):
                            mm(psA[:, :], h_[:, kt, tok], Wg[:, kt, :], kt == 0, kt == KT - 1, [Wg.r, h_.r], prA)
                        if half == 0:
                            if TMW & 1:
                                cp("dve", ktok[:, ch, :], psA[:, 0:256], [prA], [ktok.r])
                            if TMW & 2:
                                cp(os.environ.get("VENG", "act"), vtok[:, ch, 0:256], psA[:, 256:512], [prA], [vtok.r])
                            if not (TMW & 4):
                                continue
                            psL, prL = PS()
                            gt = slice(t0 + cc * 128, t0 + (cc + 1) * 128)
                            mm(psL[:, :], zT[0:48, gt], wa[0:48, :], True, True, [zT.r, wa.r], prL)
                            tt("dve", tl[:], psL[:, :], bab[:], ALU.add, [prL, bab.r], [tl.r])
                            act(tl[:], tl[:], AF.Exp, [tl.r], [tl.r], scale=-1.0)
                            act(la[:, ch].rearrange("p a b -> p (a b)"), tl[:], AF.Ln, [tl.r], [la.r], bias=1.0)
                        else:
                            cp("dve", vtok[:, ch, 256:384], psA[:, 0:128], [prA], [vtok.r])
                            act(tg[:], psA[:, 128:512], AF.Silu, [prA], [tg.r])
                            g_ = gst[ch % 2]
                            tt("dve", g_[:], tg[:], nwb[:], ALU.mult, [tg.r, nwb.r], [g_.r])
                            dma("sp", g_scr[:, ch, :], g_[:], [g_.r], [gscr_r])
            R.barrier()
            AR.release(mk2)
            import os
            GLIM = int(os.environ.get("GLIM", "99"))
            if GLIM == 0:
                AR.release(mk)
                return
            order = [list(range(NCH)), [1, 0] + list(range(17, 1, -1))]
            step_of = [{ch: s_ for s_, ch in enumerate(order[d_])} for d_ in range(2)]
            QS = 48 ** -0.5
            ofscr_r = Res("of_scr")
            qbscr_r = Res("qb_scr")
            dsscr_r = Res("ds_scr")
            decs = AR.alloc([2, NCH, 2], F32, "decs")
            mkA = AR.mark()
            NSET = 4
            E1 = [AR.alloc([2, 128], F32, f"E1{d}") for d in range(NSET)]
            E2 = [AR.alloc([2, 128], F32, f"E2{d}") for d in range(NSET)]
            Ek = [AR.alloc([256], F32, f"Ek{d}") for d in range(NSET)]
            qb = [AR.alloc([2, 128], BF16, f"qb{d}") for d in range(NSET)]
            kb = [AR.alloc([2, 128], BF16, f"kb{d}") for d in range(NSET)]
            kd = [AR.alloc([256], BF16, f"kd{d}") for d in range(NSET)]
            scm = [AR.alloc([4, 128], BF16, f"scm{d}") for d in range(NSET)]
            dst = [AR.alloc([2, 96], F32, f"dst{d}") for d in range(NSET)]
            ofs = [AR.alloc([384], F32, f"ofs{i}") for i in range(2)]

            def front(ch):
                tok = slice(ch * 128, (ch + 1) * 128)
                for dr in range(2):
                    k_ = (ch % 2) * 2 + dr
                    s_ = step_of[dr][ch]
                    last = 127 if dr == 0 else 0
                    psb, prb = PS()
                    for p in range(2):
                        mm(psb[:, p * 128:(p + 1) * 128], la[:, ch, dr, p * 128:(p + 1) * 128], tri[:, dr, :], True, True,
                           [la.r, tri.r], prb)
                    mm(psb[:, 256:512], tri[:, 2 + dr, :], la[:, ch, dr, :], True, True, [la.r, tri.r], prb)
                    pb3 = psb[:, 0:256].rearrange("p (a b) -> p a b", b=128)
                    act(E1[k_][:], pb3, AF.Exp, [prb], [E1[k_].r])
                    act(E2[k_][:], pb3, AF.Exp, [prb], [E2[k_].r], scale=-1.0)
                    act(Ek[k_][:], psb[:, 256:512], AF.Exp, [prb], [Ek[k_].r])
                    act(decs[:, dr, s_, :], pb3[:, :, last], AF.Exp, [prb], [decs.r])
                    stt("dve", qb[k_][:], qT[:, :, tok], QS, E1[k_][:], ALU.mult, ALU.mult, [qT.r, E1[k_].r], [qb[k_].r])
                    tt("dve", kb[k_][:], kT[:, :, tok], E2[k_][:], ALU.mult, [kT.r, E2[k_].r], [kb[k_].r])
                    tt("dve", kd[k_][:], ktok[:, ch, :], Ek[k_][:], ALU.mult, [ktok.r, Ek[k_].r], [kd[k_].r])
                    dma("sp", qb_scr[:, dr, ch, :], qb[k_][:].rearrange("p a b -> p (a b)"), [qb[k_].r], [qbscr_r])

            def back(ch):
                banks = []
                for dr in range(2):
                    k_ = (ch % 2) * 2 + dr
                    pssE, prsE = PS()
                    pssO, prsO = PS()
                    for h in range(4):
                        p, base = h // 2, 64 * (h % 2)
                        pss_, prs_ = (pssE, prsE) if h % 2 == 0 else (pssO, prsO)
                        mm(pss_[:, p * 128:(p + 1) * 128], kb[k_][base:base + 64, p, :], qb[k_][base:base + 64, p, :],
                           True, True, [kb[k_].r, qb[k_].r], prs_)
                    banks.append(((pssE, prsE), (pssO, prsO)))
                    psd, prd = PS()
                    for h in range(4):
                        p, base = h // 2, 64 * (h % 2)
                        mm(psd[base:base + 64, p * 96:(p + 1) * 96], kd[k_][:, h * 64:(h + 1) * 64],
                           vtok[:, ch, h * 96:(h + 1) * 96], True, True, [kd[k_].r, vtok.r], prd)
                    cp("act", dst[k_][:], psd[:, 0:192].rearrange("p (a b) -> p a b", b=96), [prd], [dst[k_].r])
                    dma("sp", ds_scr[:, dr, step_of[dr][ch], :], dst[k_][:].rearrange("p a b -> p (a b)"), [dst[k_].r], [dsscr_r])
                for dr in range(2):
                    k_ = (ch % 2) * 2 + dr
                    scm4 = scm[k_][:].rearrange("p (a e) b -> p a e b", e=2)
                    for e_, (pss_, prs_) in enumerate(banks[dr]):
                        tt("dve", scm4[:, :, e_, :], pss_[:, 0:256].rearrange("p (a b) -> p a b", b=128),
                           maskb[:, dr, :].unsqueeze(1).to_broadcast([128, 2, 128]), ALU.mult, [prs_, maskb.r], [scm[k_].r])
                pso, pro = PS()
                n_ = 0
                for dr in range(2):
                    k_ = (ch % 2) * 2 + dr
                    for h in range(4):
                        mm(pso[:, h * 96:(h + 1) * 96], scm[k_][:, h, :], vtok[:, ch, h * 96:(h + 1) * 96], n_ == 0, n_ == 7,
                           [scm[k_].r, vtok.r], pro, sig=(n_ == 7))
                        n_ += 1
                os_ = ofs[ch % 2]
                cp("act", os_[:], pso[:, 0:384], [pro], [os_.r])
                dma("sp", of_scr[:, ch, :], os_[:], [os_.r], [ofscr_r])

            for c_ in range(NCH + 1):
                if c_ < NCH:
                    front(c_)
                if c_ >= 1:
                    back(c_ - 1)
            R.barrier()
            AR.release(mk)
            mkB = AR.mark()
            decs2 = AR.alloc([2, NCH, 2], F32, "decs2")
            dS = AR.alloc([2, NCH, 192], F32, "dS")
            Sst = AR.alloc([2, NCH, 192], BF16, "Sst")
            S32 = AR.alloc([2, 2, 96], F32, "S32")
            St = AR.alloc([2, 2, 96], F32, "St")
            cp("dve", decs2[:], decs[:], [decs.r], [decs2.r])
            dma("sp", dS[:], ds_scr, [dsscr_r], [dS.r])
            memset("dve", S32[:], 0.0, [S32.r])
            for s_ in range(NCH):
                cp("act", Sst[:, :, s_, :], S32[:].rearrange("p d a b -> p d (a b)"), [S32.r], [Sst.r])
                if s_ < NCH - 1:
                    tt("dve", St[:], S32[:], decs2[:, :, s_, :].unsqueeze(3).to_broadcast([128, 2, 2, 96]), ALU.mult,
                       [S32.r, decs2.r], [St.r])
                    tt("dve", S32[:], St[:], dS[:, :, s_, :].rearrange("p d (a b) -> p d a b", b=96), ALU.add,
                       [St.r, dS.r], [S32.r])
            G = 3
            ol = [AR.alloc([384], F32, f"ol{i}") for i in range(G)]
            gsb = [AR.alloc([384], BF16, f"gsb{i}") for i in range(G)]
            qbl = [AR.alloc([2, 2, 128], BF16, f"qbl{i}") for i in range(G)]
            osum = [AR.alloc([384], F32, f"osum{i}") for i in range(G)]
            osq = [AR.alloc([384], F32, f"osq{i}") for i in range(G)]
            ss = [AR.alloc([4], F32, f"ss{i}") for i in range(G)]
            otm = [AR.alloc([384], BF16, f"otm{i}") for i in range(G)]
            mst = [AR.alloc([3, 128], BF16, f"mst{i}") for i in range(G)]
            for g0 in range(0, NCH, G):
                chs = list(range(g0, g0 + G))
                for i_, ch in enumerate(chs):
                    dma("sp", qbl[i_][:], qb_scr[:, :, ch, :].rearrange("p d (a b) -> p d a b", b=128), [qbscr_r], [qbl[i_].r])
                    dma("sp", ol[i_][:], of_scr[:, ch, :], [ofscr_r], [ol[i_].r])
                    dma("sp", gsb[i_][:], g_scr[:, ch, :], [gscr_r], [gsb[i_].r])
                pbanks = []
                for i_, ch in enumerate(chs):
                    psE, prE = PS()
                    psO, prO = PS()
                    for e_, (ps_, pr_) in enumerate(((psE, prE), (psO, prO))):
                        n_ = 0
                        for dr in range(2):
                            for p in range(2):
                                base = 64 * e_
                                mm(ps_[:, p * 96:(p + 1) * 96], qbl[i_][base:base + 64, dr, p, :],
                                   Sst[base:base + 64, dr, step_of[dr][ch], p * 96:(p + 1) * 96], n_ == 0, n_ == 3,
                                   [qbl[i_].r, Sst.r], pr_, sig=(n_ == 3))
                                n_ += 1
                    pbanks.append(((psE, prE), (psO, prO)))
                for i_, ch in enumerate(chs):
                    o4 = osum[i_][:].rearrange("p (a e b) -> p a e b", e=2, b=96)
                    l4 = ol[i_][:].rearrange("p (a e b) -> p a e b", e=2, b=96)
                    for e_, (ps_, pr_) in enumerate(pbanks[i_]):
                        tt("dve", o4[:, :, e_, :], l4[:, :, e_, :], ps_[:, 0:192].rearrange("p (a b) -> p a b", b=96), ALU.add,
                           [ol[i_].r, pr_], [osum[i_].r])
                for i_, ch in enumerate(chs):
                    act(osq[i_][:], osum[i_][:], AF.Square, [osum[i_].r], [osq[i_].r])
                for i_, ch in enumerate(chs):
                    treduce(ss[i_][:], osq[i_][:].rearrange("p (a b) -> p a b", b=96), [osq[i_].r], [ss[i_].r])
                for i_, ch in enumerate(chs):
                    act(ss[i_][:], ss[i_][:], AF.Sqrt, [ss[i_].r], [ss[i_].r], scale=1.0 / 96, bias=EPS)
                for i_, ch in enumerate(chs):
                    recip(ss[i_][:], ss[i_][:], [ss[i_].r], [ss[i_].r])
                for i_, ch in enumerate(chs):
                    o3 = osum[i_][:].rearrange("p (a b) -> p a b", b=96)
                    tt("dve", o3, o3, ss[i_][:].unsqueeze(2).to_broadcast([128, 4, 96]), ALU.mult, [osum[i_].r, ss[i_].r], [osum[i_].r])
                for i_, ch in enumerate(chs):
                    tt("dve", otm[i_][:], osum[i_][:], gsb[i_][:], ALU.mult, [osum[i_].r, gsb[i_].r], [otm[i_].r])
                tb = []
                for i_, ch in enumerate(chs):
                    pst_, prt = PS()
                    pstb = pst_[:, :].bitcast(BF16)
                    for k in range(3):
                        petr(pstb[:, k * 128:(k + 1) * 128], otm[i_][:, k * 128:(k + 1) * 128], identb[:], [otm[i_].r, identb.r], prt)
                    tb.append((pstb, prt))
                for i_, ch in enumerate(chs):
                    pstb, prt = tb[i_]
                    cp("act", mst[i_][:], pstb[:, 0:384].rearrange("p (a b) -> p a b", b=128), [prt], [mst[i_].r])
                    dma("sp", m_scr[:, 0:3, ch * 128:(ch + 1) * 128], mst[i_][:], [mst[i_].r], [mscr_r])
            R.barrier()
            AR.release(mkB)

        def stage_att(b, l):
            mk = AR.mark()
            qr = AR.alloc([3, T], BF16, "qr")
            kr = AR.alloc([T], BF16, "kr")
            vaug = AR.alloc([NCH, 2, 66], BF16, "vaug")
            esk = AR.alloc([6], F32, "esk")
            memset("dve", vaug[:], 1.0, [vaug.r])
            dma("sp", esk[:], sink_d[l].partition_broadcast(128), [], [esk.r])
            act(esk[:], esk[:], AF.Exp, [esk.r], [esk.r])
            mk2 = AR.mark()
            Wa = AR.alloc([KT, NFM_A + NTM_A], BF16, "Wa")
            rope = AR.alloc([2, 2048], F32, "rope")
            hb = [AR.alloc([KT, 512], BF16, f"hb{i}") for i in range(2)]
            t1 = AR.alloc([512], F32, "t1")
            t2 = AR.alloc([512], F32, "t2")
            dma("sp", rope[:], crope_d, [], [rope.r])
            load_w(Wa, wfm_d[l].rearrange("(kt p) c -> p kt c", p=128)[:, :, NFM_G:NFM_G + NFM_A], NFM_A)
            dma("pool", Wa[:, :, NFM_A:NFM_A + NTM_A], wtm_d[l].rearrange("(kt p) c -> p kt c", p=128)[:, :, NTM_G:NTM_G + NTM_A],
                [], [Wa.r])
            for bi, (t0, N) in enumerate(BLKS):
                h_ = hb[bi % 2]
                dma("sp", h_[:, :, 0:N], h_scr[:, :, t0:t0 + N], [hscr_r], [h_.r])
                for m in range(4):
                    dst = qr[:, m, t0:t0 + N] if m < 3 else kr[:, t0:t0 + N]
                    dres = qr.r if m < 3 else kr.r
                    wc = m * 128 if m < 3 else 768
                    wcp = (3 + m) * 128 if m < 3 else 896
                    ps, pr = PS()
                    for kt in range(KT):
                        mm(ps[:, 0:N], Wa[:, kt, wc:wc + 128], h_[:, kt, 0:N], kt == 0, kt == KT - 1, [Wa.r, h_.r], pr)
                    if bi == 0:
                        cp("act", dst, ps[:, 0:N], [pr], [dres])
                    else:
                        ps2, pr2 = PS()
                        for kt in range(KT):
                            mm(ps2[:, 0:N], Wa[:, kt, wcp:wcp + 128], h_[:, kt, 0:N], kt == 0, kt == KT - 1, [Wa.r, h_.r], pr2)
                        tl0 = t0 - TCX
                        tt("dve", t1[:, 0:N], ps[:, 0:N], rope[:, 0, tl0:tl0 + N], ALU.mult, [pr, rope.r], [t1.r])
                        tt("dve", t2[:, 0:N], ps2[:, 0:N], rope[:, 1, tl0:tl0 + N], ALU.mult, [pr2, rope.r], [t2.r])
                        tt("pool", dst, t1[:, 0:N], t2[:, 0:N], ALU.add, [t1.r, t2.r], [dres])
                for cc in range(N // 128):
                    ch = t0 // 128 + cc
                    ps, pr = PS()
                    for kt in range(KT):
                        mm(ps[:, 0:128], h_[:, kt, cc * 128:(cc + 1) * 128], Wa[:, kt, NFM_A:NFM_A + 128], kt == 0, kt == KT - 1,
                           [Wa.r, h_.r], pr)
                    cp("act", vaug[:, ch, :, 0:64], ps[:, 0:128].rearrange("p (a b) -> p a b", b=64), [pr], [vaug.r])
            R.barrier()
            AR.release(mk2)
            pt = [AR.alloc([384], BF16, f"pt{i}") for i in range(4)]
            den = AR.alloc([2, 3], F32, "den")
            otm = AR.alloc([384], BF16, "otm")
            mst = AR.alloc([3, 128], BF16, "mst")
            pti = 0
            for qc in range(NCH):
                qtok = slice(qc * 128, (qc + 1) * 128)
                if qc < 2:
                    keys = [(0, None), (1, None)]
                else:
                    keys = [(0, None), (1, None)]
                    if qc - 1 >= 2:
                        keys.append((qc - 1, 1))
                    keys.append((qc, None))
                    if qc + 1 <= NCH - 1:
                        keys.append((qc + 1, 0))
                for kv in range(2):
                    base = 64 * kv
                    pso, pro = PS()
                    for ki, (kc, mk_) in enumerate(keys):
                        ktok_ = slice(kc * 128, (kc + 1) * 128)
                        pss, prs = PS()
                        for hh in range(3):
                            mm(pss[:, hh * 128:(hh + 1) * 128], kr[base:base + 64, ktok_], qr[base:base + 64, hh, qtok],
                               True, True, [kr.r, qr.r], prs)
                        p_ = pt[pti % 4]
                        pti += 1
                        act(p_[:], pss[:, 0:384], AF.Exp, [prs], [p_.r], scale=0.125)
                        if mk_ is not None:
                            tt("dve", p_[:].rearrange("p (a b) -> p a b", b=128), p_[:].rearrange("p (a b) -> p a b", b=128),
                               maskb[:, mk_, :].unsqueeze(1).to_broadcast([128, 3, 128]), ALU.mult, [p_.r, maskb.r], [p_.r])
                        for hh in range(3):
                            mm(pso[:, hh * 65:(hh + 1) * 65], p_[:, hh * 128:(hh + 1) * 128], vaug[:, kc, kv, 0:65],
                               ki == 0 and hh == 0, ki == len(keys) - 1 and hh == 2, [p_.r, vaug.r], pro, sig=(hh == 2))
                    po3 = pso[:, 0:195].rearrange("p (a b) -> p a b", b=65)
                    tt("dve", den[:, kv, :], po3[:, :, 64], esk[:, 3 * kv:3 * kv + 3], ALU.add, [pro, esk.r], [den.r])
                    recip(den[:, kv, :], den[:, kv, :], [den.r], [den.r])
                    tt("dve", otm[:, kv * 192:(kv + 1) * 192].rearrange("p (a b) -> p a b", b=64), po3[:, :, 0:64],
                       den[:, kv, :].unsqueeze(2).to_broadcast([128, 3, 64]), ALU.mult, [pro, den.r], [otm.r])
                pst_, prt = PS()
                pstb = pst_[:, :].bitcast(BF16)
                for k in range(3):
                    petr(pstb[:, k * 128:(k + 1) * 128], otm[:, k * 128:(k + 1) * 128], identb[:], [otm.r, identb.r], prt)
                cp("act", mst[:], pstb[:, 0:384].rearrange("p (a b) -> p a b", b=128), [prt], [mst.r])
                dma("sp", m_scr[:, 3:6, qtok], mst[:], [mst.r], [mscr_r])
            R.barrier()
            AR.release(mk)

        def sincos(src, K_, F_, dst_s, dst_c):
            cp("dve", K_[:], src[:], [src.r], [K_.r])
            cp("dve", F_[:], K_[:], [K_.r], [F_.r])
            tt("dve", F_[:], src[:], F_[:], ALU.subtract, [src.r, F_.r], [F_.r])
            ts("dve", src[:], F_[:], 0.49999, -0.49999, ALU.min, ALU.max, [F_.r], [src.r])
            act(dst_s[:], src[:], AF.Sin, [src.r], [dst_s.r], scale=TWO_PI)
            ts("dve", F_[:], F_[:], 0.25, None, ALU.add, None, [F_.r], [F_.r])
            cp("dve", K_[:], F_[:], [F_.r], [K_.r])
            cp("dve", src[:], K_[:], [K_.r, dst_s.r], [src.r])
            tt("dve", F_[:], F_[:], src[:], ALU.subtract, [F_.r, src.r], [F_.r])
            ts("dve", F_[:], F_[:], 0.49999, -0.49999, ALU.min, ALU.max, [F_.r], [F_.r])
            act(dst_c[:], F_[:], AF.Sin, [F_.r], [dst_c.r], scale=TWO_PI)

        def stage_s5(b, l):
            NS = 288
            mk = AR.mark()
            uT = AR.alloc([2, T], BF16, "uT")
            yT = AR.alloc([2, T], BF16, "yT")
            mk3 = AR.mark()
            Ws = AR.alloc([KT, NFM_S], BF16, "Ws")
            hb = [AR.alloc([KT, 512], BF16, f"hb{i}") for i in range(2)]
            load_w(Ws, wfm_d[l].rearrange("(kt p) c -> p kt c", p=128)[:, :, NFM_G + NFM_A:NFM_G + NFM_A + NFM_S], NFM_S)
            for bi, (t0, N) in enumerate(BLKS):
                h_ = hb[bi % 2]
                dma("sp", h_[:, :, 0:N], h_scr[:, :, t0:t0 + N], [hscr_r], [h_.r])
                for ct in range(2):
                    ps, pr = PS()
                    for kt in range(KT):
                        mm(ps[:, 0:N], Ws[:, kt, ct * 128:(ct + 1) * 128], h_[:, kt, 0:N], kt == 0, kt == KT - 1, [Ws.r, h_.r], pr)
                    cp("act" if ct == 0 else "dve", uT[:, ct, t0:t0 + N], ps[:, 0:N], [pr], [uT.r])
            R.barrier()
            AR.release(mk3)

            lam = AR.alloc([2, 2, 8], F32, "lam")
            dtt = AR.alloc([2, 8], F32, "dt")
            Bp = AR.alloc([2, 2, 8, 16], F32, "Bp")
            Cp = AR.alloc([2, 2, 8, 16], F32, "Cp")
            dsk = AR.alloc([2], F32, "dsk")
            bd32 = AR.alloc([128], F32, "bd32")
            cpp = AR.alloc([2], F32, "cpp")
            mI = AR.alloc([NS], F32, "mI")
            r8 = AR.alloc([2, 8], F32, "r8")
            phi = AR.alloc([2, 8], F32, "phi")
            dma("sp", lam[:], s5lam_d[l].rearrange("d p c g -> p d c g"), [], [lam.r])
            dma("sp", dtt[:], s5step_d[l].rearrange("d p g -> p d g"), [], [dtt.r])
            dma("sp", Bp[:], s5B_d[l].rearrange("d p c g h -> p d c g h"), [], [Bp.r])
            dma("sp", Cp[:], s5C_d[l].rearrange("d p c g h -> p d c g h"), [], [Cp.r])
            dma("sp", dsk[:], s5d_d[l], [], [dsk.r])
            dma("sp", bd32[:], cbd32_d, [], [bd32.r])
            dma("sp", cpp[:], cpp_d, [], [cpp.r])
            dma("sp", mI[:], cm_d, [], [mI.r])
            ldt = AR.alloc([2, 8], F32, "ldt")
            wdt = AR.alloc([2, 8], F32, "wdt")
            arp = AR.alloc([2, 9, 8], F32, "arp")
            aip = AR.alloc([2, 9, 8], F32, "aip")
            ang = AR.alloc([2, 9, 8], F32, "ang")
            ki = AR.alloc([2, 9, 8], I32, "ki")
            kf = AR.alloc([2, 9, 8], F32, "kf")
            mag = AR.alloc([2, 9, 8], F32, "mag")
            tA = AR.alloc([2, 8], F32, "tA")
            tB = AR.alloc([2, 8], F32, "tB")
            tC = AR.alloc([2, 8], F32, "tC")
            fr = AR.alloc([2, 8], F32, "fr")
            fi = AR.alloc([2, 8], F32, "fi")
            Bbr = AR.alloc([2, 8, 16], F32, "Bbr")
            Bbi = AR.alloc([2, 8, 16], F32, "Bbi")
            u1 = AR.alloc([8, 16], F32, "u1")
            u2 = AR.alloc([8, 16], F32, "u2")
            u3 = AR.alloc([8, 16], F32, "u3")
            tbd = AR.alloc([128], F32, "tbd")
            act(dtt[:], dtt[:], AF.Exp, [dtt.r], [dtt.r])
            ts("dve", lam[:, :, 0, :], lam[:, :, 0, :], -1e-4, None, ALU.min, None, [lam.r], [lam.r])
            tt("dve", ldt[:], lam[:, :, 0, :], dtt[:], ALU.mult, [lam.r, dtt.r], [ldt.r])
            tt("dve", wdt[:], lam[:, :, 1, :], dtt[:], ALU.mult, [lam.r, dtt.r], [wdt.r])
            for tau in range(9):
                act(mag[:, :, tau, :], ldt[:], AF.Exp, [ldt.r], [mag.r], scale=float(tau))
                ts("dve", ang[:, :, tau, :], wdt[:], float(tau) / TWO_PI, None, ALU.mult, None, [wdt.r], [ang.r])
            sincos(ang, ki, kf, aip, arp)
            tt("dve", arp[:], arp[:], mag[:], ALU.mult, [arp.r, mag.r], [arp.r])
            tt("dve", aip[:], aip[:], mag[:], ALU.mult, [aip.r, mag.r], [aip.r])
            cp("dve", r8[:], mag[:, :, 8, :], [mag.r], [r8.r])
            ts("dve", phi[:], wdt[:], 8.0 / TWO_PI, None, ALU.mult, None, [wdt.r], [phi.r])
            cp("dve", ki[:, :, 0, :], phi[:], [phi.r], [ki.r])
            cp("dve", kf[:, :, 0, :], ki[:, :, 0, :], [ki.r], [kf.r])
            tt("dve", phi[:], phi[:], kf[:, :, 0, :], ALU.subtract, [phi.r, kf.r], [phi.r])
            lr_, li_ = lam[:, :, 0, :], lam[:, :, 1, :]
            tt("dve", tA[:], lr_, lr_, ALU.mult, [lam.r], [tA.r])
            tt("dve", tB[:], li_, li_, ALU.mult, [lam.r], [tB.r])
            tt("dve", tA[:], tA[:], tB[:], ALU.add, [tA.r, tB.r], [tA.r])
            recip(tA[:], tA[:], [tA.r], [tA.r])
            ts("dve", tB[:], arp[:, :, 1, :], -1.0, None, ALU.add, None, [arp.r], [tB.r])
            tt("dve", fr[:], tB[:], lr_, ALU.mult, [tB.r, lam.r], [fr.r])
            tt("dve", tC[:], aip[:, :, 1, :], li_, ALU.mult, [aip.r, lam.r], [tC.r])
            tt("dve", fr[:], fr[:], tC[:], ALU.add, [fr.r, tC.r], [fr.r])
            tt("dve", fr[:], fr[:], tA[:], ALU.mult, [fr.r, tA.r], [fr.r])
            tt("dve", fi[:], aip[:, :, 1, :], lr_, ALU.mult, [aip.r, lam.r], [fi.r])
            tt("dve", tC[:], tB[:], li_, ALU.mult, [tB.r, lam.r], [tC.r])
            tt("dve", fi[:], fi[:], tC[:], ALU.subtract, [fi.r, tC.r], [fi.r])
            tt("dve", fi[:], fi[:], tA[:], ALU.mult, [fi.r, tA.r], [fi.r])

            def bc16(ap2):
                return ap2.unsqueeze(2).to_broadcast([128, 8, 16])

            for dr in range(2):
                tt("dve", u1[:], Bp[:, dr, 0], bc16(fr[:, dr, :]), ALU.mult, [Bp.r, fr.r], [u1.r])
                tt("dve", u2[:], Bp[:, dr, 1], bc16(fi[:, dr, :]), ALU.mult, [Bp.r, fi.r], [u2.r])
                tt("dve", Bbr[:, dr], u1[:], u2[:], ALU.subtract, [u1.r, u2.r], [Bbr.r])
                tt("dve", u1[:], Bp[:, dr, 1], bc16(fr[:, dr, :]), ALU.mult, [Bp.r, fr.r], [u1.r])
                tt("dve", u2[:], Bp[:, dr, 0], bc16(fi[:, dr, :]), ALU.mult, [Bp.r, fi.r], [u2.r])
                tt("dve", Bbi[:, dr], u1[:], u2[:], ALU.add, [u1.r, u2.r], [Bbi.r])

            for ct in range(2):
                mkc = AR.mark()
                pq = slice(ct * 4, ct * 4 + 4)
                BDT = AR.alloc([2, 8, 128], BF16, "BDT")
                CmI = AR.alloc([4, 32, 32], BF16, "CmI")
                Dst = AR.alloc([16, NS], F32, "Dst")
                memset("dve", CmI[:], 0.0, [CmI.r])
                mkb = AR.mark()
                BmJ = AR.alloc([64, 128], BF16, "BmJ")
                CAMa = AR.alloc([9, 2, 128], F32, "CAMa")
                BbM = AR.alloc([2, 2, 128], F32, "BbM")
                WMa = AR.alloc([8, 2, 128], F32, "WMa")
                V1 = AR.alloc([9, 4, 16], F32, "V1")
                V2 = AR.alloc([9, 4, 16], F32, "V2")
                V3 = AR.alloc([9, 4, 16], F32, "V3")
                memset("dve", CAMa[:], 0.0, [CAMa.r])
                memset("dve", BbM[:], 0.0, [BbM.r])
                memset("dve", WMa[:], 0.0, [WMa.r])

                def bmj(pp, dr, j, ri):
                    return BmJ[:, ((pp * 2 + dr) * 8 + j) * 2 + ri, :]

                cam6 = CAMa[:].rearrange("p t r (g a h) -> p t r g a h", a=2, h=16)
                wm6 = WMa[:].rearrange("p t r (g a h) -> p t r g a h", a=2, h=16)
                for dr in range(2):
                    bm4 = BbM[:, dr].rearrange("p r (g a h) -> p r g a h", a=2, h=16)
                    for g2 in range(2):
                        hs = slice(64 * g2, 64 * g2 + 64)
                        cp("dve", bm4[hs, 0, :, g2, :], Bbr[hs, dr, pq, :], [Bbr.r], [BbM.r])
                        cp("dve", bm4[hs, 1, :, g2, :], Bbi[hs, dr, pq, :], [Bbi.r], [BbM.r])

                def b_h(ap3, n):
                    return ap3.unsqueeze(3).to_broadcast([128, n, 4, 16])

                def b_t(ap3, n):
                    return ap3.unsqueeze(1).to_broadcast([128, n, 4, 16])

                for dr in range(2):
                    ar9, ai9 = b_h(arp[:, dr, :, pq], 9), b_h(aip[:, dr, :, pq], 9)
                    cr9, ci9 = b_t(Cp[:, dr, 0, pq, :], 9), b_t(Cp[:, dr, 1, pq, :], 9)
                    tt("dve", V1[:], cr9, ar9, ALU.mult, [Cp.r, arp.r], [V1.r])
                    tt("dve", V2[:], ci9, ai9, ALU.mult, [Cp.r, aip.r], [V2.r])
                    tt("dve", V3[:], V1[:], V2[:], ALU.subtract, [V1.r, V2.r], [V3.r])
                    for g2 in range(2):
                        hs = slice(64 * g2, 64 * g2 + 64)
                        cp("dve", cam6[hs, :, 0, :, g2, :], V3[hs], [V3.r], [CAMa.r])
                    tt("dve", V1[:], cr9, ai9, ALU.mult, [Cp.r, aip.r], [V1.r])
                    tt("dve", V2[:], ci9, ar9, ALU.mult, [Cp.r, arp.r], [V2.r])
                    stt("dve", V3[:], V1[:], -1.0, V2[:], ALU.mult, ALU.subtract, [V1.r, V2.r], [V3.r])
                    for g2 in range(2):
                        hs = slice(64 * g2, 64 * g2 + 64)
                        cp("dve", cam6[hs, :, 1, :, g2, :], V3[hs], [V3.r], [CAMa.r])
                    for ri in range(2):
                        src = CAMa[:, 1:9, ri, :] if dr == 0 else CAMa[:, 8:0:-1, ri, :]
                        cp("act", CmI[:, :, dr * 16 + ri:dr * 16 + 16:2, :], src.rearrange("p t (g c) -> p g t c", c=32),
                           [CAMa.r], [CmI.r])
                    for k4 in range(2):
                        ps, pr = PS()
                        for t4 in range(4):
                            tau = k4 * 4 + t4
                            for ri in range(2):
                                mm(ps[:, t4 * 128:(t4 + 1) * 128], BbM[:, dr, ri, :], CAMa[:, tau, ri, :], ri == 0, ri == 1,
                                   [BbM.r, CAMa.r], pr)
                        tt("dve", BDT[:, dr, k4 * 4:k4 * 4 + 4, :], ps[:, :].rearrange("p (t c) -> p t c", c=128),
                           bd32[:].unsqueeze(1).to_broadcast([128, 4, 128]), ALU.mult, [pr, bd32.r], [BDT.r])
                    if dr == 0:
                        stt("dve", BDT[:, 0, 0, :], ident[:], dsk[:, ct:ct + 1], BDT[:, 0, 0, :], ALU.mult, ALU.add,
                            [ident.r, dsk.r, BDT.r], [BDT.r])
                    if dr == 0:
                        ar8, ai8 = b_h(arp[:, dr, 7::-1, pq], 8), b_h(aip[:, dr, 7::-1, pq], 8)
                    else:
                        ar8, ai8 = b_h(arp[:, dr, 0:8, pq], 8), b_h(aip[:, dr, 0:8, pq], 8)
                    br8, bi8 = b_t(Bbr[:, dr, pq, :], 8), b_t(Bbi[:, dr, pq, :], 8)
                    tt("dve", V1[:, 0:8], br8, ar8, ALU.mult, [Bbr.r, arp.r], [V1.r])
                    tt("dve", V2[:, 0:8], bi8, ai8, ALU.mult, [Bbi.r, aip.r], [V2.r])
                    tt("dve", V3[:, 0:8], V1[:, 0:8], V2[:, 0:8], ALU.subtract, [V1.r, V2.r], [V3.r])
                    for g2 in range(2):
                        hs = slice(64 * g2, 64 * g2 + 64)
                        cp("dve", wm6[hs, :, 0, :, g2, :], V3[hs, 0:8], [V3.r], [WMa.r])
                    tt("dve", V1[:, 0:8], bi8, ar8, ALU.mult, [Bbi.r, arp.r], [V1.r])
                    tt("dve", V2[:, 0:8], br8, ai8, ALU.mult, [Bbr.r, aip.r], [V2.r])
                    tt("dve", V3[:, 0:8], V1[:, 0:8], V2[:, 0:8], ALU.add, [V1.r, V2.r], [V3.r])
                    for g2 in range(2):
                        hs = slice(64 * g2, 64 * g2 + 64)
                        cp("dve", wm6[hs, :, 1, :, g2, :], V3[hs, 0:8], [V3.r], [WMa.r])
                    for k4 in range(4):
                        ps, pr = PS()
                        for jj in range(2):
                            for ri in range(2):
                                c_ = (jj * 2 + ri) * 128
                                petr(ps[:, c_:c_ + 128], WMa[:, k4 * 2 + jj, ri, :], ident[:], [WMa.r, ident.r], pr)
                        for pp in range(2):
                            s0 = ((pp * 2 + dr) * 8 + k4 * 2) * 2
                            ts("dve", BmJ[:, s0:s0 + 4, :], ps[:, :].rearrange("p (t c) -> p t c", c=128), cpp[:, pp:pp + 1], None,
                               ALU.mult, None, [pr, cpp.r], [BmJ.r])
                for p4 in range(4):
                    half, pp = p4 // 2, p4 % 2
                    hs = slice(64 * half, 64 * half + 64)
                    for dr in range(2):
                        for ri in range(2):
                            ps, pr = PS()
                            for j in range(8):
                                mm(ps[:, 0:NS], bmj(pp, dr, j, ri)[hs, :], uT[hs, ct, j:T:8], j == 0, j == 7, [BmJ.r, uT.r], pr)
                            slot = dr * 8 + ri * 4 + p4
                            if dr == 0:
                                cp("act", Dst[:, slot, :], ps[:, 0:NS], [pr], [Dst.r])
                            else:
                                cp("act", Dst[:, slot, 0:32], ps[:, 0:32][:, ::-1], [pr], [Dst.r])
                                cp("dve", Dst[:, slot, 32:NS], ps[:, 32:NS][:, ::-1], [pr], [Dst.r])
                R.barrier()
                AR.release(mkb)
                Xbf = AR.alloc([16, NS], BF16, "Xbf")
                Ec = AR.alloc([4, NS], F32, "Ec")
                Es = AR.alloc([4, NS], F32, "Es")
                pk = AR.alloc([4, NS], I32, "pk")
                pf = AR.alloc([4, NS], F32, "pf")
                pa = AR.alloc([4, NS], F32, "pa")
                w1_ = AR.alloc([4, NS], F32, "w1_")
                w2_ = AR.alloc([4, NS], F32, "w2_")
                w3_ = AR.alloc([4, NS], F32, "w3_")
                w4_ = AR.alloc([4, NS], F32, "w4_")
                memset("dve", Xbf[:], 0.0, [Xbf.r])
                for dr in range(2):
                    tt("dve", pa[:], mI[:].unsqueeze(1).to_broadcast([128, 4, NS]),
                       phi[:, dr, pq].unsqueeze(2).to_broadcast([128, 4, NS]), ALU.mult, [mI.r, phi.r], [pa.r])
                    sincos(pa, pk, pf, Es, Ec)
                    Dr_ = Dst[:, dr * 8:dr * 8 + 4, :]
                    Di_ = Dst[:, dr * 8 + 4:dr * 8 + 8, :]
                    tt("dve", w1_[:], Dr_, Ec[:], ALU.mult, [Dst.r, Ec.r], [w1_.r])
                    tt("pool", w2_[:], Di_, Es[:], ALU.mult, [Dst.r, Es.r], [w2_.r])
                    tt("dve", w3_[:], Di_, Ec[:], ALU.mult, [Dst.r, Ec.r], [w3_.r])
                    tt("pool", w4_[:], Dr_, Es[:], ALU.mult, [Dst.r, Es.r], [w4_.r])
                    tt("dve", Dr_, w1_[:], w2_[:], ALU.add, [w1_.r, w2_.r], [Dst.r])
                    tt("dve", Di_, w3_[:], w4_[:], ALU.subtract, [w3_.r, w4_.r], [Dst.r])
                    for ri in range(2):
                        for p4 in range(4):
                            sl = dr * 8 + ri * 4 + p4
                            tscan(Dst[:, sl, :], r8[:, dr, ct * 4 + p4:ct * 4 + p4 + 1].to_broadcast([128, NS]), Dst[:, sl, :],
                                  [Dst.r, r8.r], [Dst.r])
                    M1 = NS - 1
                    Sr, Si = Dst[:, dr * 8:dr * 8 + 4, 0:M1], Dst[:, dr * 8 + 4:dr * 8 + 8, 0:M1]
                    cc_, ss_ = Ec[:, :, 0:M1], Es[:, :, 0:M1]
                    tt("dve", w1_[:, :, 0:M1], Sr, cc_, ALU.mult, [Dst.r, Ec.r], [w1_.r])
                    tt("pool", w2_[:, :, 0:M1], Si, ss_, ALU.mult, [Dst.r, Es.r], [w2_.r])
                    tt("dve", w3_[:, :, 0:M1], Si, cc_, ALU.mult, [Dst.r, Ec.r], [w3_.r])
                    tt("pool", w4_[:, :, 0:M1], Sr, ss_, ALU.mult, [Dst.r, Es.r], [w4_.r])
                    for (xo, a_, b_, op) in ((Xbf[:, dr * 8:dr * 8 + 4, :], w1_, w2_, ALU.subtract),
                                             (Xbf[:, dr * 8 + 4:dr * 8 + 8, :], w3_, w4_, ALU.add)):
                        if dr == 0:
                            tt("dve", xo[:, :, 1:NS], a_[:, :, 0:M1], b_[:, :, 0:M1], op, [a_.r, b_.r], [Xbf.r])
                        else:
                            tt("dve", xo[:, :, 0:31][:, :, ::-1], a_[:, :, 0:31], b_[:, :, 0:31], op, [a_.r, b_.r], [Xbf.r])
                            tt("dve", xo[:, :, 32:NS][:, :, ::-1], a_[:, :, 31:M1], b_[:, :, 31:M1], op, [a_.r, b_.r], [Xbf.r])
                for i in range(8):
                    ps, pr = PS()
                    for p4 in range(4):
                        cnt = 0
                        for dr in range(2):
                            for ri in range(2):
                                slot = dr * 8 + ri * 4 + p4
                                kw = dict(tile_position=(0, 96)) if p4 == 3 else {}
                                mm(ps[32 * p4:32 * p4 + 32, 0:NS], CmI[:, p4, (dr * 8 + i) * 2 + ri, :], Xbf[:, slot, :],
                                   cnt == 0, False, [CmI.r, Xbf.r], pr, sig=False, **kw)
                                cnt += 1
                    terms = []
                    for dr in range(2):
                        js = range(0, i + 1) if dr == 0 else range(i, 8)
                        for j in js:
                            terms.append((dr, j, i - j if dr == 0 else j - i))
                    for n_, (dr, j, tau) in enumerate(terms):
                        lastt = n_ == len(terms) - 1
                        mm(ps[:, 0:NS], BDT[:, dr, tau, :], uT[:, ct, j:T:8], False, lastt, [BDT.r, uT.r], pr, sig=lastt)
                    act(yT[:, ct, i:T:8], ps[:, 0:NS], AF.Gelu, [pr], [yT.r])
                R.barrier()
                AR.release(mkc)

            Wgl = AR.alloc([2, 512], BF16, "Wgl")
            gb = AR.alloc([4], F32, "gb")
            sg = [AR.alloc([512], F32, f"sg{i}") for i in range(2)]
            mst = [AR.alloc([2, 512], BF16, f"mst{i}") for i in range(2)]
            dma("pool", Wgl[:], gluw_d[l].rearrange("(k p) c -> p k c", p=128), [], [Wgl.r])
            dma("sp", gb[:], glub_d[l], [], [gb.r])
            for bi, (t0, N) in enumerate(BLKS):
                ms_ = mst[bi % 2]
                for mt in range(2):
                    psa, pra = PS()
                    psg, prg = PS()
                    for k in range(2):
                        mm(psa[:, 0:N], Wgl[:, k, mt * 128:(mt + 1) * 128], yT[:, k, t0:t0 + N], k == 0, k == 1, [Wgl.r, yT.r], pra)
                    for k in range(2):
                        mm(psg[:, 0:N], Wgl[:, k, 256 + mt * 128:256 + (mt + 1) * 128], yT[:, k, t0:t0 + N], k == 0, k == 1,
                           [Wgl.r, yT.r], prg)
                    s_ = sg[mt]
                    act(s_[:, 0:N], psg[:, 0:N], AF.Sigmoid, [prg, gb.r], [s_.r], bias=gb[:, 2 + mt:3 + mt])
                    stt("dve", ms_[:, mt, 0:N], psa[:, 0:N], gb[:, mt:mt + 1], s_[:, 0:N], ALU.add, ALU.mult, [pra, gb.r, s_.r], [ms_.r])
                dma("sp", m_scr[:, 6:8, t0:t0 + N], ms_[:, :, 0:N], [ms_.r], [mscr_r])
            R.barrier()
            AR.release(mk)

        def stage_wout(b, l):
            mk = AR.mark()
            Wo = AR.alloc([KT, D], BF16, "Wo")
            mb = [AR.alloc([KT, 512], BF16, f"mb{i}") for i in range(2)]
            load_w(Wo, wout_d[l].rearrange("(kt p) c -> p kt c", p=128), D)
            for bi, (t0, N) in enumerate(BLKS):
                j = 2 if bi == 0 else b
                m_ = mb[bi % 2]
                dma("sp", m_[:, :, 0:N], m_scr[:, :, t0:t0 + N], [mscr_r], [m_.r])
                for mt in range(KT):
                    ps, pr = PS()
                    for kt in range(KT):
                        mm(ps[:, 0:N], Wo[:, kt, mt * 128:(mt + 1) * 128], m_[:, kt, 0:N], kt == 0, kt == KT - 1, [Wo.r, m_.r], pr)
                    stt("dve", x[:, mt, t0:t0 + N], ps[:, 0:N], G1(l, mt, j), x[:, mt, t0:t0 + N], ALU.mult, ALU.add,
                        [pr, modt.r, xres[bi]], [xres[bi]])
            R.barrier()
            AR.release(mk)

        def stage_mlp(b, l):
            mk = AR.mark()
            hb = [AR.alloc([KT, 512], BF16, f"hb{i}") for i in range(2)]
            sq = AR.alloc([KT, 512], BF16, "sq")
            rstd = AR.alloc([512], F32, "rstd")
            tmpf = [AR.alloc([512], F32, f"tf{i}") for i in range(2)]
            for bi, (t0, N) in enumerate(BLKS):
                j = 2 if bi == 0 else b
                h_ = hb[bi % 2]
                rms_block(bi, lambda kt: A2[:, l, kt, j:j + 1], lambda kt: SH2(l, kt, j), h_, sq, rstd, tmpf)
                dma("sp", h_scr[:, :, t0:t0 + N], h_[:, :, 0:N], [h_.r], [hscr_r])
            R.barrier()
            AR.release(mk)
            W1 = [AR.alloc([KT, 1024], BF16, f"W1{i}") for i in range(2)]
            W2 = [AR.alloc([KT, 1024], BF16, f"W2{i}") for i in range(2)]
            hb = [AR.alloc([KT, 512], BF16, f"hb{i}") for i in range(2)]
            hid = [AR.alloc([KT, 512], BF16, f"hid{i}") for i in range(2)]
            rl = [AR.alloc([512], F32, f"rl{i}") for i in range(2)]
            w1v = w1_d[l].rearrange("(kt p) c -> p kt c", p=128)
            w2v = w2_d[l].rearrange("(kt p) c -> p kt c", p=128)
            for q in range(4):
                W1_, W2_ = W1[q % 2], W2[q % 2]
                for c0 in range(0, 1024, 512):
                    dma("pool", W1_[:, :, c0:c0 + 512], w1v[:, :, q * 1024 + c0:q * 1024 + c0 + 512], [], [W1_.r])
                for c0 in range(0, 1024, 512):
                    dma("pool", W2_[:, :, c0:c0 + 512], w2v[:, q * 8:(q + 1) * 8, c0:c0 + 512], [], [W2_.r])
                for bi, (t0, N) in enumerate(BLKS):
                    j = 2 if bi == 0 else b
                    h_ = hb[bi % 2]
                    hd = hid[bi % 2]
                    dma("sp", h_[:, :, 0:N], h_scr[:, :, t0:t0 + N], [hscr_r], [h_.r])
                    for mt in range(8):
                        ps, pr = PS()
                        for kt in range(KT):
                            mm(ps[:, 0:N], W1_[:, kt, mt * 128:(mt + 1) * 128], h_[:, kt, 0:N], kt == 0, kt == KT - 1, [W1_.r, h_.r], pr)
                        r_ = rl[mt % 2]
                        act(r_[:, 0:N], ps[:, 0:N], AF.Relu, [pr], [r_.r])
                        tt("pool" if mt % 2 else "dve", hd[:, mt, 0:N], r_[:, 0:N], r_[:, 0:N], ALU.mult, [r_.r], [hd.r])
                    for mt in range(KT):
                        ps, pr = PS()
                        for kt in range(8):
                            mm(ps[:, 0:N], W2_[:, kt, mt * 128:(mt + 1) * 128], hd[:, kt, 0:N], kt == 0, kt == 7, [W2_.r, hd.r], pr)
                        stt("dve", x[:, mt, t0:t0 + N], ps[:, 0:N], G2(l, mt, j), x[:, mt, t0:t0 + N], ALU.mult, ALU.add,
                            [pr, modt.r, xres[bi]], [xres[bi]])
            R.barrier()
            AR.release(mk)

        def stage_final(b):
            mk = AR.mark()
            sq = AR.alloc([KT, 512], BF16, "sq")
            rstd = AR.alloc([512], F32, "rstd")
            ob = [AR.alloc([KT, 512], F32, f"ob{i}") for i in range(2)]
            for bi, (t0, N) in enumerate(BLKS):
                if bi == 0:
                    continue
                o_ = ob[bi % 2]
                xr = xres[bi]
                act(sq[:, :, 0:N], x[:, :, t0:t0 + N], AF.Square, [xr], [sq.r])
                ps, pr = PS()
                for kt in range(KT):
                    mm(ps[:, 0:N], onesb[:], sq[:, kt, 0:N], kt == 0, kt == KT - 1, [onesb.r, sq.r], pr)
                act(rstd[:, 0:N], ps[:, 0:N], AF.Sqrt, [pr], [rstd.r], scale=1.0 / D, bias=EPS)
                recip(rstd[:, 0:N], rstd[:, 0:N], [rstd.r], [rstd.r])
                for kt in range(KT):
                    stt("dve", o_[:, kt, 0:N], x[:, kt, t0:t0 + N], nrm[:, 8, kt:kt + 1], rstd[:, 0:N], ALU.mult, ALU.mult,
                        [xr, nrm.r, rstd.r], [o_.r])
                dma("sp", out_d[b, :, :, t0 - TCX:t0 - TCX + N], o_[:, :, 0:N], [o_.r], [])
            R.barrier()
            AR.release(mk)

        for b in range(n_b):
            for bi, (t0, N) in enumerate(BLKS):
                dma("sp", x[:, :, t0:t0 + N], xin[b, :, :, t0:t0 + N], [], [xres[bi]])
            for l in range(n_layers):
                if "n" in stages:
                    stage_norm1(b, l)
                if "g" in stages:
                    stage_gla(b, l)
                if "a" in stages:
                    stage_att(b, l)
                if "s" in stages:
                    stage_s5(b, l)
                if "w" in stages:
                    stage_wout(b, l)
                if dbg and b == 0 and l == 0:
                    R.barrier()
                    for bi_, (t0_, N_) in enumerate(BLKS):
                        dma("sp", dbg_xa[:, :, t0_:t0_ + N_], x[:, :, t0_:t0_ + N_], [xres[bi_]], [])
                if "m" in stages:
                    stage_mlp(b, l)
                if dbg and b == 0 and l == 0:
                    R.barrier()
                    for bi_, (t0_, N_) in enumerate(BLKS):
                        dma("sp", dbg_xb[:, :, t0_:t0_ + N_], x[:, :, t0_:t0_ + N_], [xres[bi_]], [])
            stage_final(b)
        R.barrier()
        print("ops", R.n_ops, "waits", R.n_waits, "sems", len(R.sems), "arena peak", AR.peak, {k: len(v) for k, v in R.prog.items()}, flush=True)
        R.emit()
    return nc


def _perm64():
    p = np.zeros(64, np.int64)
    for d in range(64):
        q = d // 16
        p[d] = d + 16 if q % 2 == 0 else d - 16
    return p


def _w_in_layouts(w_in):
    Lc = w_in.shape[0]
    oQ, oK, oV, oG, oZF, oZB, oAQ, oAK, oAV, oU = 0, 192, 384, 768, 1152, 1168, 1184, 1568, 1696, 1824
    fm = np.full(1920, -1, np.int64)
    for pr in range(2):
        for hh in range(2):
            h = 2 * pr + hh
            fm[pr * 128 + hh * 64: pr * 128 + hh * 64 + 48] = oQ + 48 * h + np.arange(48)
            fm[256 + pr * 128 + hh * 64: 256 + pr * 128 + hh * 64 + 48] = oK + 48 * h + np.arange(48)
    fm[512:528] = oZF + np.arange(16)
    fm[544:560] = oZB + np.arange(16)
    perm = _perm64()
    base = 640
    for m, (ha, hb) in enumerate([(0, 3), (1, 4), (2, 5)]):
        for s, h in enumerate((ha, hb)):
            fm[base + m * 128 + s * 64: base + m * 128 + (s + 1) * 64] = oAQ + 64 * h + np.arange(64)
            fm[base + (3 + m) * 128 + s * 64: base + (3 + m) * 128 + (s + 1) * 64] = oAQ + 64 * h + perm
    for s in range(2):
        fm[base + 768 + s * 64: base + 768 + (s + 1) * 64] = oAK + 64 * s + np.arange(64)
        fm[base + 896 + s * 64: base + 896 + (s + 1) * 64] = oAK + 64 * s + perm
    fm[1664:1920] = oU + np.arange(256)
    tm = np.full(1152, -1, np.int64)
    for h in range(4):
        tm[h * 64:h * 64 + 48] = oK + 48 * h + np.arange(48)
    tm[256:640] = oV + np.arange(384)
    tm[640:1024] = oG + np.arange(384)
    tm[1024:1152] = oAV + np.arange(128)

    def gather(idx):
        out = np.zeros((Lc, w_in.shape[1], idx.size), np.float32)
        sel = idx >= 0
        out[:, :, sel] = w_in[:, :, idx[sel]]
        return out
    return gather(fm), gather(tm)


def _constants():
    j = np.arange(128)[:, None]
    i = np.arange(128)[None, :]
    c = {}
    tri = np.zeros((128, 4, 128), np.float32)
    tri[:, 0, :] = (j <= i) * (-1.0 / 16)
    tri[:, 1, :] = (j >= i) * (-1.0 / 16)
    tri[:, 2, :] = (j > i) * (-1.0 / 16)
    tri[:, 3, :] = (j < i) * (-1.0 / 16)
    c["c_tri"] = tri
    msk = np.zeros((128, 2, 128), np.float32)
    msk[:, 0, :] = (j <= i)
    msk[:, 1, :] = (j >= i)
    c["c_mask"] = msk
    c["c_ident"] = np.eye(128, dtype=np.float32)
    rows = TLAT // 64
    row = np.repeat(np.arange(rows, dtype=np.float32), 64)
    col = np.tile(np.arange(64, dtype=np.float32), rows)
    inv = (10000.0 ** (-np.arange(16, dtype=np.float32) / 16)).astype(np.float32)
    ang = np.concatenate([row[:, None] * inv, row[:, None] * inv, col[:, None] * inv, col[:, None] * inv], axis=-1)
    sign = np.concatenate([-np.ones(16), np.ones(16), -np.ones(16), np.ones(16)]).astype(np.float32)
    cosT = np.cos(ang).T.astype(np.float32)
    sinT = (np.sin(ang) * sign[None, :]).T.astype(np.float32)
    rope = np.zeros((128, 2, TLAT), np.float32)
    rope[:, 0, :] = np.concatenate([cosT, cosT], 0)
    rope[:, 1, :] = np.concatenate([sinT, sinT], 0)
    c["c_rope"] = rope
    r = np.arange(128)
    c["c_bd32"] = (r[:, None] // 32 == r[None, :] // 32).astype(np.float32)
    cpp = np.zeros((128, 2), np.float32)
    for pp in range(2):
        cpp[:, pp] = ((r % 64) // 32 == pp)
    c["c_pp"] = cpp
    c["c_m"] = np.broadcast_to(np.arange(1, 289, dtype=np.float32)[None, :], (128, 288)).copy()
    return c


def _col_layout(v):
    m = v.shape[-1] // 128
    return np.ascontiguousarray(np.swapaxes(v.reshape(v.shape[:-1] + (m, 128)), -1, -2))


def _prep_shared(inp):
    f = lambda k: np.asarray(inp[k], np.float32)
    sh = {}
    sh["w_mod"] = f("w_mod")
    sh["bmod"] = _col_layout(f("b_mod"))
    nr = np.zeros((128, 9, KT), np.float32)
    n1, n2 = _col_layout(f("norm1_w")), _col_layout(f("norm2_w"))
    for l in range(L):
        nr[:, l, :] = n1[l]
        nr[:, 4 + l, :] = n2[l]
    nr[:, 8, :] = _col_layout(f("final_norm_w"))
    sh["nrm"] = nr
    sh["wfm"], sh["wtm"] = _w_in_layouts(f("w_in"))
    wa = np.zeros((L, 48, 512), np.float32)
    ba = np.zeros((L, 512), np.float32)
    for dr, (wk, bk) in enumerate((("gla_wa_f", "gla_ba_f"), ("gla_wa_b", "gla_ba_b"))):
        w_, b_ = f(wk), f(bk)
        for h in range(4):
            wa[:, 32 * dr:32 * dr + 16, dr * 256 + h * 64:dr * 256 + h * 64 + 48] = w_[:, :, h * 48:(h + 1) * 48]
            ba[:, dr * 256 + h * 64: dr * 256 + h * 64 + 48] = b_[:, h * 48:(h + 1) * 48]
    sh["wa"], sh["ba"] = wa, ba
    sh["gnw"] = np.tile(f("gla_norm_w"), (1, 4))
    sh["sink"] = f("attn_sink")
    sh["w_out"], sh["mlp_w1"], sh["mlp_w2"] = f("w_out"), f("mlp_w1"), f("mlp_w2")
    sh["glu_w"] = f("glu_w")
    sh["glub"] = _col_layout(f("glu_b"))
    lam = np.zeros((L, 2, 128, 2, 8), np.float32)
    stp = np.zeros((L, 2, 128, 8), np.float32)
    Bm = np.zeros((L, 2, 128, 2, 8, 16), np.float32)
    Cm = np.zeros((L, 2, 128, 2, 8, 16), np.float32)
    for dr, tg in enumerate(("f", "b")):
        lre, lim, ls = f("s5_lam_re_" + tg), f("s5_lam_im_" + tg), f("s5_log_step_" + tg)
        bre, bim, cre, cim = f("s5_b_re_" + tg), f("s5_b_im_" + tg), f("s5_c_re_" + tg), f("s5_c_im_" + tg)
        for g2 in range(2):
            ps_ = slice(64 * g2, 64 * g2 + 64)
            gsel = np.arange(8) * 2 + g2
            lam[:, dr, ps_, 0, :] = np.transpose(lre[:, gsel, :], (0, 2, 1))
            lam[:, dr, ps_, 1, :] = np.transpose(lim[:, gsel, :], (0, 2, 1))
            stp[:, dr, ps_, :] = ls[:, None, gsel]
            Bm[:, dr, ps_, 0] = np.transpose(bre[:, gsel], (0, 2, 1, 3))
            Bm[:, dr, ps_, 1] = np.transpose(bim[:, gsel], (0, 2, 1, 3))
            Cm[:, dr, ps_, 0] = np.transpose(cre[:, gsel], (0, 3, 1, 2))
            Cm[:, dr, ps_, 1] = np.transpose(cim[:, gsel], (0, 3, 1, 2))
    sh["s5lam"], sh["s5step"], sh["s5B"], sh["s5C"] = lam, stp, Bm, Cm
    sh["s5d"] = _col_layout(f("s5_d"))
    sh.update(_constants())
    return sh


def _prep_core(inp, core):
    x, ctx, c, c_ctx = (np.asarray(inp[k], np.float32) for k in ("x", "ctx", "c", "c_ctx"))
    xin = np.zeros((NBC, 128, KT, T), np.float32)
    cT = np.zeros((128, KT, 3), np.float32)
    for bb in range(NBC):
        b = core * NBC + bb
        seq = np.concatenate([ctx[b], x[b]], axis=0)
        xin[bb] = np.transpose(seq.T.reshape(KT, 128, T), (1, 0, 2))
        cT[:, :, bb] = c[b].reshape(KT, 128).T
    cT[:, :, 2] = c_ctx.reshape(KT, 128).T
    return {"xin": xin, "cT": cT}


_NC_CACHE = {}


def kernel(**inputs):
    n = 8
    shared = _prep_shared(inputs)
    in_maps = []
    for core in range(n):
        m = dict(shared)
        m.update(_prep_core(inputs, core))
        in_maps.append(m)
    if "nc" not in _NC_CACHE:
        _NC_CACHE["nc"] = build_program()
    res = run_bass_kernel_spmd(_NC_CACHE["nc"], in_maps, core_ids=list(range(n)))
    B = np.asarray(inputs["x"]).shape[0]
    out = np.zeros((B, TLAT, D), np.float32)
    for core in range(n):
        o = np.asarray(res.results[core]["out"])
        for bb in range(NBC):
            out[core * NBC + bb] = np.transpose(o[bb], (2, 1, 0)).reshape(TLAT, D)
    return out
```

```python
import numpy as np
import concourse.bass as bass
import concourse.mybir as mybir
from concourse.bass_utils import run_bass_kernel_spmd
from contextlib import ExitStack

F32 = mybir.dt.float32
BF16 = mybir.dt.bfloat16
I32 = mybir.dt.int32
U8 = mybir.dt.uint8
AF = mybir.ActivationFunctionType
ALU = mybir.AluOpType
AX = mybir.AxisListType

D = 1024
KT = 8
T = 2304
TCX = 256
TLAT = 2048
NCH = 18
L = 4
NBC = 2
BLKS = [(0, 256), (256, 512), (768, 512), (1280, 512), (1792, 512)]
EPS = 1e-6
NFM_G, NTM_G = 640, 1024
NFM_A, NTM_A = 1024, 128
NFM_S = 256
TWO_PI = 2.0 * np.pi


class Res:
    __slots__ = ("name", "w", "r", "excl")

    def __init__(self, name="", excl=False):
        self.name = name
        self.w = None
        self.r = {}
        self.excl = excl


class Rec:
    EPOCH = 30000
    SAME_ENGINE_SYNC = True

    def __init__(self, nc, es, n_dma_sems=12):
        self.nc = nc
        self.es = es
        self.prog = {k: [] for k in ("pe", "act", "dve", "pool", "sp")}
        self.sems = []
        self.csem = {}
        self.ccnt = {}
        self.waited = {k: {} for k in self.prog}
        for k in ("pe", "act", "dve", "pool"):
            self._new_csem(k)
        self.dsem = {}
        self.drr = {}
        for q in ("sp", "pool", "act"):
            self.dsem[q] = [[self._new_sem(f"d_{q}_{i}"), 0] for i in range(n_dma_sems)]
            self.drr[q] = 0
        self.n_ops = 0
        self.n_waits = 0

    def _new_sem(self, name):
        s = self.es.enter_context(self.nc.semaphore(name))
        self.sems.append(s)
        return len(self.sems) - 1

    def _new_csem(self, e):
        self.csem[e] = self._new_sem(f"c_{e}_{len(self.sems)}")
        self.ccnt[e] = 0

    def _waits(self, e, deps):
        wd = self.waited[e]
        best = {}
        for (si, val, src) in deps:
            if src == e and (e == "pe" or not self.SAME_ENGINE_SYNC):
                continue
            if wd.get(si, 0) >= val:
                continue
            if best.get(si, 0) < val:
                best[si] = val
        for si, val in best.items():
            wd[si] = val
            self.prog[e].append(("wait", si, val))
            self.n_waits += 1

    def _deps(self, reads, writes, e=None):
        deps = []
        for r in reads:
            if r.w is not None:
                deps.append(r.w)
            if r.excl:
                for si, (val, src) in r.r.items():
                    if src != e:
                        deps.append((si, val, src))
        for w in writes:
            if w.w is not None:
                deps.append(w.w)
            for si, (val, src) in w.r.items():
                deps.append((si, val, src))
        return deps

    def _mark(self, tok, reads, writes):
        si, val, src = tok
        for r in reads:
            r.r[si] = (val, src)
        for w in writes:
            w.w = tok
            w.r = {}

    def op(self, e, fn, reads=(), writes=(), sig=True):
        self._waits(e, self._deps(reads, writes, e))
        if not sig and self.ccnt[e] >= self.EPOCH - 1:
            sig = True
        if sig and self.ccnt[e] >= self.EPOCH:
            self._new_csem(e)
        si = self.csem[e]
        if sig:
            self.ccnt[e] += 1
            tok = (si, self.ccnt[e], e)
            self.prog[e].append(("op", fn, si, 1))
        else:
            tok = (si, self.ccnt[e] + 1, e)
            self.prog[e].append(("op", fn, None, 0))
        self._mark(tok, reads, writes)
        self.n_ops += 1
        return tok

    def dma(self, q, fn, reads=(), writes=()):
        deps = self._deps(reads, writes)
        slot = self.dsem[q][self.drr[q]]
        self.drr[q] = (self.drr[q] + 1) % len(self.dsem[q])
        if slot[1] > 0:
            deps.append((slot[0], slot[1], "dma"))
        self._waits(q, deps)
        slot[1] += 16
        tok = (slot[0], slot[1], "dma")
        self.prog[q].append(("op", fn, slot[0], 16))
        self._mark(tok, reads, writes)
        self.n_ops += 1
        return tok

    def all_tokens(self):
        toks = [(self.csem[e], self.ccnt[e], e) for e in ("pe", "act", "dve", "pool") if self.ccnt[e] > 0]
        for q in self.dsem:
            toks += [(s[0], s[1], "dma") for s in self.dsem[q] if s[1] > 0]
        return toks

    def barrier(self):
        toks = self.all_tokens()
        for e in self.prog:
            self._waits(e, [t for t in toks if not (t[2] == e and e == "pe")])

    def emit(self):
        nc = self.nc
        sems = self.sems
        prog = self.prog

        def replay(name, eng):
            for it in prog[name]:
                if it[0] == "wait":
                    eng.wait_ge(sems[it[1]], it[2])
                else:
                    ins = it[1](eng)
                    if it[3]:
                        ins.then_inc(sems[it[2]], it[3])

        with nc.Block() as block:
            @block.sync
            def _(e):
                replay("sp", e)

            @block.tensor
            def _(e):
                replay("pe", e)

            @block.scalar
            def _(e):
                replay("act", e)

            @block.vector
            def _(e):
                replay("dve", e)

            @block.gpsimd
            def _(e):
                replay("pool", e)


class Tl:
    def __init__(self, ap, name=""):
        self.ap = ap
        self.r = Res(name)

    def __getitem__(self, k):
        return self.ap[k]


class Arena:
    def __init__(self, nc, es, nbytes):
        self.t = es.enter_context(nc.sbuf_tensor("arena", [128, nbytes], U8))
        self.off = 0
        self.cap = nbytes
        self.peak = 0

    def alloc(self, shape, dtype, name=""):
        sz = mybir.dt.size(dtype)
        n = int(np.prod(shape))
        nb = (n * sz + 63) // 64 * 64
        assert self.off + nb <= self.cap, ("SBUF arena overflow", name, self.off, nb, self.cap)
        ap = self.t[:, self.off:self.off + n * sz].bitcast(dtype)
        self.off += nb
        self.peak = max(self.peak, self.off)
        if len(shape) == 2:
            ap = ap.rearrange("p (a b) -> p a b", b=shape[1])
        elif len(shape) == 3:
            ap = ap.rearrange("p (a b c) -> p a b c", b=shape[1], c=shape[2])
        elif len(shape) == 4:
            ap = ap.rearrange("p (a b c d) -> p a b c d", b=shape[1], c=shape[2], d=shape[3])
        return Tl(ap, name)

    def mark(self):
        return self.off

    def release(self, m):
        self.off = m


def build_program(n_layers=L, n_b=NBC, dbg=False, stages="ngaswm", prologue=True):
    nc = bass.Bass("TRN2", target_bir_lowering=False)

    def din(name, shape, dt=F32):
        return nc.dram_tensor(name, list(shape), dt, kind="ExternalInput").ap()

    xin = din("xin", [NBC, 128, KT, T])
    cT_d = din("cT", [128, KT, 3])
    wmod_d = din("w_mod", [L, D, 6 * D])
    bmod_d = din("bmod", [L, 128, 48])
    nrm_d = din("nrm", [128, 9, KT])
    wfm_d = din("wfm", [L, D, 1920])
    wtm_d = din("wtm", [L, D, 1152])
    wa_d = din("wa", [L, 48, 512])
    ba_d = din("ba", [L, 512])
    gnw_d = din("gnw", [L, 384])
    sink_d = din("sink", [L, 6])
    wout_d = din("w_out", [L, D, D])
    w1_d = din("mlp_w1", [L, D, 4 * D])
    w2_d = din("mlp_w2", [L, 4 * D, D])
    gluw_d = din("glu_w", [L, 256, 512])
    glub_d = din("glub", [L, 128, 4])
    s5lam_d = din("s5lam", [L, 2, 128, 2, 8])
    s5step_d = din("s5step", [L, 2, 128, 8])
    s5B_d = din("s5B", [L, 2, 128, 2, 8, 16])
    s5C_d = din("s5C", [L, 2, 128, 2, 8, 16])
    s5d_d = din("s5d", [L, 128, 2])
    ctri_d = din("c_tri", [128, 4, 128])
    cmask_d = din("c_mask", [128, 2, 128])
    cident_d = din("c_ident", [128, 128])
    crope_d = din("c_rope", [128, 2, 2048])
    cbd32_d = din("c_bd32", [128, 128])
    cpp_d = din("c_pp", [128, 2])
    cm_d = din("c_m", [128, 288])
    out_d = nc.dram_tensor("out", [NBC, 128, KT, TLAT], F32, kind="ExternalOutput").ap()
    h_scr = nc.dram_tensor("dbg_h" if dbg else "h_scr", [128, KT, T], BF16, kind="ExternalOutput" if dbg else "Internal").ap()
    m_scr = nc.dram_tensor("dbg_m" if dbg else "m_scr", [128, KT, T], BF16, kind="ExternalOutput" if dbg else "Internal").ap()
    g_scr = nc.dram_tensor("g_scr", [128, NCH, 384], BF16, kind="Internal").ap()
    of_scr = nc.dram_tensor("of_scr", [128, NCH, 384], F32, kind="Internal").ap()
    qb_scr = nc.dram_tensor("qb_scr", [128, 2, NCH, 256], BF16, kind="Internal").ap()
    ds_scr = nc.dram_tensor("ds_scr", [128, 2, NCH, 192], F32, kind="Internal").ap()
    if dbg:
        dbg_xa = nc.dram_tensor("dbg_xa", [128, KT, T], F32, kind="ExternalOutput").ap()
        dbg_xb = nc.dram_tensor("dbg_xb", [128, KT, T], F32, kind="ExternalOutput").ap()

    es = ExitStack()
    with es:
        R = Rec(nc, es)
        AR = Arena(nc, es, 196000)
        pst = [es.enter_context(nc.psum_tensor(f"ps{i}", [128, 512], F32)) for i in range(8)]
        psr = [Res(f"ps{i}", excl=True) for i in range(8)]
        psi = [0]

        def PS():
            k = psi[0]
            psi[0] = (k + 1) % 8
            return pst[k], psr[k]

        def mm(out, lhsT, rhs, start, stop, reads, wres, sig=None, **kw):
            if sig is None:
                sig = stop
            R.op("pe", lambda e: e.matmul(out, lhsT=lhsT, rhs=rhs, start=start, stop=stop, **kw),
                 reads=reads, writes=[wres], sig=sig)

        def act(out, in_, func, reads, writes, **kw):
            R.op("act", lambda e: e.activation(out=out, in_=in_, func=func, **kw), reads=reads, writes=writes)

        def tt(eng, out, in0, in1, op, reads, writes):
            R.op(eng, lambda e: e.tensor_tensor(out=out, in0=in0, in1=in1, op=op), reads=reads, writes=writes)

        def stt(eng, out, in0, scalar, in1, op0, op1, reads, writes):
            R.op(eng, lambda e: e.scalar_tensor_tensor(out=out, in0=in0, scalar=scalar, in1=in1, op0=op0, op1=op1),
                 reads=reads, writes=writes)

        def ts(eng, out, in0, s1, s2, op0, op1, reads, writes):
            if s2 is None:
                R.op(eng, lambda e: e.tensor_scalar(out=out, in0=in0, scalar1=s1, scalar2=None, op0=op0),
                     reads=reads, writes=writes)
            else:
                R.op(eng, lambda e: e.tensor_scalar(out=out, in0=in0, scalar1=s1, scalar2=s2, op0=op0, op1=op1),
                     reads=reads, writes=writes)

        def cp(eng, out, in_, reads, writes):
            if eng == "act":
                act(out, in_, AF.Copy, reads, writes)
            else:
                R.op(eng, lambda e: e.tensor_copy(out=out, in_=in_), reads=reads, writes=writes)

        def memset(eng, out, val, writes):
            R.op(eng, lambda e: e.memset(out, val), writes=writes)

        def dma(q, out, in_, reads, writes):
            R.dma(q, lambda e: e.dma_start(out=out, in_=in_), reads=reads, writes=writes)

        def recip(out, in_, reads, writes):
            R.op("dve", lambda e: e.reciprocal(out=out, in_=in_), reads=reads, writes=writes)

        def treduce(out, in_, reads, writes):
            R.op("dve", lambda e: e.tensor_reduce(out=out, in_=in_, axis=AX.X, op=ALU.add), reads=reads, writes=writes)

        def petr(out, in_, idn, reads, wres):
            R.op("pe", lambda e: e.transpose(out=out, in_=in_, identity=idn), reads=reads, writes=[wres])

        def tscan(out, d0, d1, reads, writes):
            R.op("dve", lambda e: e.tensor_tensor_scan(out=out, data0=d0, data1=d1, initial=0.0, op0=ALU.mult, op1=ALU.add),
                 reads=reads, writes=writes)

        x = AR.alloc([KT, T], F32, "x")
        xres = [Res(f"x{b}") for b in range(len(BLKS))]
        ident = AR.alloc([128], F32, "ident")
        identb = AR.alloc([128], BF16, "identb")
        onesb = AR.alloc([128], BF16, "onesb")
        maskb = AR.alloc([2, 128], BF16, "maskb")
        nrm = AR.alloc([9, KT], F32, "nrm")
        modt = AR.alloc([L, 48, 3], F32, "mod")
        A1 = AR.alloc([L, KT, 3], F32, "A1")
        A2 = AR.alloc([L, KT, 3], F32, "A2")
        dma("sp", ident[:], cident_d, [], [ident.r])
        dma("pool", identb[:], cident_d, [], [identb.r])
        dma("pool", maskb[:], cmask_d, [], [maskb.r])
        dma("sp", nrm[:], nrm_d, [], [nrm.r])
        memset("dve", onesb[:], 1.0, [onesb.r])

        mk = AR.mark()
        c32 = AR.alloc([KT, 3], F32, "c32")
        cact = AR.alloc([KT, 3], BF16, "cact")
        bmod = AR.alloc([L, 48], F32, "bmod")
        wm = [AR.alloc([KT, 768], BF16, f"wm{i}") for i in range(2)]
        dma("sp", c32[:], cT_d, [], [c32.r])
        dma("sp", bmod[:], bmod_d.rearrange("l p m -> p l m"), [], [bmod.r])
        act(cact[:], c32[:], AF.Silu, [c32.r], [cact.r])
        for l in range(n_layers if prologue else 0):
            ps, pr = PS()
            wv = wmod_d[l].rearrange("(kt p) c -> p kt c", p=128)
            for ch in range(8):
                w_ = wm[ch % 2]
                dma("pool", w_[:], wv[:, :, ch * 768:(ch + 1) * 768], [], [w_.r])
                for m in range(6):
                    col = (ch * 6 + m) * 3
                    for kt in range(KT):
                        mm(ps[:, col:col + 3], w_[:, kt, m * 128:(m + 1) * 128], cact[:, kt, :], kt == 0, kt == KT - 1,
                           [w_.r, cact.r], pr)
            tt("dve", modt[:, l], ps[:, 0:144].rearrange("p (m j) -> p m j", j=3),
               bmod[:, l, :].unsqueeze(2).to_broadcast([128, 48, 3]), ALU.add, [pr, bmod.r], [modt.r])
            stt("dve", A1[:, l], modt[:, l, 8:16, :], 1.0, nrm[:, l, :].unsqueeze(2).to_broadcast([128, KT, 3]),
                ALU.add, ALU.mult, [modt.r, nrm.r], [A1.r])
            stt("dve", A2[:, l], modt[:, l, 32:40, :], 1.0, nrm[:, 4 + l, :].unsqueeze(2).to_broadcast([128, KT, 3]),
                ALU.add, ALU.mult, [modt.r, nrm.r], [A2.r])
        R.barrier()
        AR.release(mk)

        def SH1(l, kt, j): return modt[:, l, 0 + kt, j:j + 1]
        def G1(l, kt, j): return modt[:, l, 16 + kt, j:j + 1]
        def SH2(l, kt, j): return modt[:, l, 24 + kt, j:j + 1]
        def G2(l, kt, j): return modt[:, l, 40 + kt, j:j + 1]

        def rms_block(bi, Asc, Bsh, hb, tmp_sq, rstd, tmpf):
            t0, N = BLKS[bi]
            xr = xres[bi]
            act(tmp_sq[:, :, 0:N], x[:, :, t0:t0 + N], AF.Square, [xr], [tmp_sq.r])
            ps, pr = PS()
            for kt in range(KT):
                mm(ps[:, 0:N], onesb[:], tmp_sq[:, kt, 0:N], kt == 0, kt == KT - 1, [onesb.r, tmp_sq.r], pr)
            act(rstd[:, 0:N], ps[:, 0:N], AF.Sqrt, [pr], [rstd.r], scale=1.0 / D, bias=EPS)
            recip(rstd[:, 0:N], rstd[:, 0:N], [rstd.r], [rstd.r])
            for kt in range(KT):
                tf = tmpf[kt % 2]
                tt("dve", tf[:, 0:N], x[:, kt, t0:t0 + N], rstd[:, 0:N], ALU.mult, [xr, rstd.r], [tf.r])
                act(hb[:, kt, 0:N], tf[:, 0:N], AF.Identity, [tf.r, modt.r, A1.r, A2.r], [hb.r],
                    scale=Asc(kt), bias=Bsh(kt))

        def load_w(tile, src_view, ncols, step=512):
            for c0 in range(0, ncols, step):
                c1 = min(ncols, c0 + step)
                dma("pool", tile[:, :, c0:c1], src_view[:, :, c0:c1], [], [tile.r])

        def stage_norm1(b, l):
            mk = AR.mark()
            hb = [AR.alloc([KT, 512], BF16, f"hb{i}") for i in range(2)]
            sq = AR.alloc([KT, 512], BF16, "sq")
            rstd = AR.alloc([512], F32, "rstd")
            tmpf = [AR.alloc([512], F32, f"tf{i}") for i in range(2)]
            for bi, (t0, N) in enumerate(BLKS):
                j = 2 if bi == 0 else b
                h_ = hb[bi % 2]
                rms_block(bi, lambda kt: A1[:, l, kt, j:j + 1], lambda kt: SH1(l, kt, j), h_, sq, rstd, tmpf)
                dma("sp", h_scr[:, :, t0:t0 + N], h_[:, :, 0:N], [h_.r], [hscr_r])
            R.barrier()
            AR.release(mk)

        hscr_r = Res("h_scr")
        mscr_r = Res("m_scr")

        gscr_r = Res("g_scr")

        def stage_gla(b, l):
            mk = AR.mark()
            qT = AR.alloc([2, T], BF16, "qT")
            kT = AR.alloc([2, T], BF16, "kT")
            ktok = AR.alloc([NCH, 256], BF16, "ktok")
            vtok = AR.alloc([NCH, 384], BF16, "vtok")
            la = AR.alloc([NCH, 2, 256], F32, "la")
            tri = AR.alloc([4, 128], F32, "tri")
            nwb = AR.alloc([384], F32, "nwb")
            dma("sp", tri[:], ctri_d, [], [tri.r])
            dma("sp", nwb[:], gnw_d[l].partition_broadcast(128), [], [nwb.r])
            mk2 = AR.mark()
            zT = AR.alloc([T], BF16, "zT")
            wa = AR.alloc([512], BF16, "wa")
            bab = AR.alloc([512], F32, "bab")
            dma("pool", wa[0:48, :], wa_d[l], [], [wa.r])
            dma("sp", bab[:], ba_d[l].partition_broadcast(128), [], [bab.r])
            hb = [AR.alloc([KT, 256], BF16, f"hb{i}") for i in range(2)]
            GBL = [(t_, 256) for t_ in range(0, T, 256)]
            mk3 = AR.mark()
            Wg = AR.alloc([KT, NFM_G], BF16, "Wgf")
            load_w(Wg, wfm_d[l].rearrange("(kt p) c -> p kt c", p=128)[:, :, 0:NFM_G], NFM_G, 640)
            for bi, (t0, N) in enumerate(GBL):
                h_ = hb[bi % 2]
                dma("sp", h_[:, :, 0:N], h_scr[:, :, t0:t0 + N], [hscr_r], [h_.r])
                for m in range(5):
                    ps, pr = PS()
                    for kt in range(KT):
                        mm(ps[:, 0:N], Wg[:, kt, m * 128:(m + 1) * 128], h_[:, kt, 0:N], kt == 0, kt == KT - 1,
                           [Wg.r, h_.r], pr)
                    if m < 2:
                        cp("act", qT[:, m, t0:t0 + N], ps[:, 0:N], [pr], [qT.r])
                    elif m < 4:
                        cp("dve", kT[:, m - 2, t0:t0 + N], ps[:, 0:N], [pr], [kT.r])
                    else:
                        cp("act", zT[:, t0:t0 + N], ps[:, 0:N], [pr], [zT.r])
            R.barrier()
            AR.release(mk3)
            import os
            if int(os.environ.get("GSTEP", "9")) == 1:
                AR.release(mk)
                return
            Wg = AR.alloc([KT, 512], BF16, "Wgt")
            tg = AR.alloc([384], F32, "tg")
            tl = AR.alloc([512], F32, "tl")
            gst = [AR.alloc([384], BF16, f"gst{i}") for i in range(2)]
            wtv = wtm_d[l].rearrange("(kt p) c -> p kt c", p=128)
            import os
            TMW = int(os.environ.get("TMW", "15"))
            TMCH = int(os.environ.get("TMCH", "99"))
            for half in range(2 if TMW & 8 else 1):
                dma("pool", Wg[:], wtv[:, :, half * 512:(half + 1) * 512], [], [Wg.r])
                for bi, (t0, N) in enumerate(GBL):
                    h_ = hb[bi % 2]
                    dma("sp", h_[:, :, 0:N], h_scr[:, :, t0:t0 + N], [hscr_r], [h_.r])
                    for cc in range(N // 128):
                        ch = t0 // 128 + cc
                        if ch >= TMCH:
                            continue
                        tok = slice(cc * 128, (cc + 1) * 128)
                        psA, prA = PS()
                        for kt in range(KT):
                            mm(psA[:, :], h_[:, kt, tok], Wg[:, kt, :], kt == 0, kt == KT - 1, [Wg.r, h_.r], prA)
                        if half == 0:
                            if TMW & 1:
                                cp("dve", ktok[:, ch, :], psA[:, 0:256], [prA], [ktok.r])
                            if TMW & 2:
                                cp(os.environ.get("VENG", "act"), vtok[:, ch, 0:256], psA[:, 256:512], [prA], [vtok.r])
                            if not (TMW & 4):
                                continue
                            psL, prL = PS()
                            gt = slice(t0 + cc * 128, t0 + (cc + 1) * 128)
                            mm(psL[:, :], zT[0:48, gt], wa[0:48, :], True, True, [zT.r, wa.r], prL)
                            tt("dve", tl[:], psL[:, :], bab[:], ALU.add, [prL, bab.r], [tl.r])
                            act(tl[:], tl[:], AF.Exp, [tl.r], [tl.r], scale=-1.0)
                            act(la[:, ch].rearrange("p a b -> p (a b)"), tl[:], AF.Ln, [tl.r], [la.r], bias=1.0)
                        else:
                            cp("dve", vtok[:, ch, 256:384], psA[:, 0:128], [prA], [vtok.r])
                            act(tg[:], psA[:, 128:512], AF.Silu, [prA], [tg.r])
                            g_ = gst[ch % 2]
                            tt("dve", g_[:], tg[:], nwb[:], ALU.mult, [tg.r, nwb.r], [g_.r])
                            dma("sp", g_scr[:, ch, :], g_[:], [g_.r], [gscr_r])
            R.barrier()
            AR.release(mk2)
            import os
            GLIM = int(os.environ.get("GLIM", "99"))
            if GLIM == 0:
                AR.release(mk)
                return
            order = [list(range(NCH)), [1, 0] + list(range(17, 1, -1))]
            step_of = [{ch: s_ for s_, ch in enumerate(order[d_])} for d_ in range(2)]
            QS = 48 ** -0.5
            ofscr_r = Res("of_scr")
            qbscr_r = Res("qb_scr")
            dsscr_r = Res("ds_scr")
            decs = AR.alloc([2, NCH, 2], F32, "decs")
            mkA = AR.mark()
            NSET = 4
            E1 = [AR.alloc([2, 128], F32, f"E1{d}") for d in range(NSET)]
            E2 = [AR.alloc([2, 128], F32, f"E2{d}") for d in range(NSET)]
            Ek = [AR.alloc([256], F32, f"Ek{d}") for d in range(NSET)]
            qb = [AR.alloc([2, 128], BF16, f"qb{d}") for d in range(NSET)]
            kb = [AR.alloc([2, 128], BF16, f"kb{d}") for d in range(NSET)]
            kd = [AR.alloc([256], BF16, f"kd{d}") for d in range(NSET)]
            scm = [AR.alloc([4, 128], BF16, f"scm{d}") for d in range(NSET)]
            dst = [AR.alloc([2, 96], F32, f"dst{d}") for d in range(NSET)]
            ofs = [AR.alloc([384], F32, f"ofs{i}") for i in range(2)]

            def front(ch):
                tok = slice(ch * 128, (ch + 1) * 128)
                for dr in range(2):
                    k_ = (ch % 2) * 2 + dr
                    s_ = step_of[dr][ch]
                    last = 127 if dr == 0 else 0
                    psb, prb = PS()
                    for p in range(2):
                        mm(psb[:, p * 128:(p + 1) * 128], la[:, ch, dr, p * 128:(p + 1) * 128], tri[:, dr, :], True, True,
                           [la.r, tri.r], prb)
                    mm(psb[:, 256:512], tri[:, 2 + dr, :], la[:, ch, dr, :], True, True, [la.r, tri.r], prb)
                    pb3 = psb[:, 0:256].rearrange("p (a b) -> p a b", b=128)
                    act(E1[k_][:], pb3, AF.Exp, [prb], [E1[k_].r])
                    act(E2[k_][:], pb3, AF.Exp, [prb], [E2[k_].r], scale=-1.0)
                    act(Ek[k_][:], psb[:, 256:512], AF.Exp, [prb], [Ek[k_].r])
                    act(decs[:, dr, s_, :], pb3[:, :, last], AF.Exp, [prb], [decs.r])
                    stt("dve", qb[k_][:], qT[:, :, tok], QS, E1[k_][:], ALU.mult, ALU.mult, [qT.r, E1[k_].r], [qb[k_].r])
                    tt("dve", kb[k_][:], kT[:, :, tok], E2[k_][:], ALU.mult, [kT.r, E2[k_].r], [kb[k_].r])
                    tt("dve", kd[k_][:], ktok[:, ch, :], Ek[k_][:], ALU.mult, [ktok.r, Ek[k_].r], [kd[k_].r])
                    dma("sp", qb_scr[:, dr, ch, :], qb[k_][:].rearrange("p a b -> p (a b)"), [qb[k_].r], [qbscr_r])

            def back(ch):
                banks = []
                for dr in range(2):
                    k_ = (ch % 2) * 2 + dr
                    pssE, prsE = PS()
                    pssO, prsO = PS()
                    for h in range(4):
                        p, base = h // 2, 64 * (h % 2)
                        pss_, prs_ = (pssE, prsE) if h % 2 == 0 else (pssO, prsO)
                        mm(pss_[:, p * 128:(p + 1) * 128], kb[k_][base:base + 64, p, :], qb[k_][base:base + 64, p, :],
                           True, True, [kb[k_].r, qb[k_].r], prs_)
                    banks.append(((pssE, prsE), (pssO, prsO)))
                    psd, prd = PS()
                    for h in range(4):
                        p, base = h // 2, 64 * (h % 2)
                        mm(psd[base:base + 64, p * 96:(p + 1) * 96], kd[k_][:, h * 64:(h + 1) * 64],
                           vtok[:, ch, h * 96:(h + 1) * 96], True, True, [kd[k_].r, vtok.r], prd)
                    cp("act", dst[k_][:], psd[:, 0:192].rearrange("p (a b) -> p a b", b=96), [prd], [dst[k_].r])
                    dma("sp", ds_scr[:, dr, step_of[dr][ch], :], dst[k_][:].rearrange("p a b -> p (a b)"), [dst[k_].r], [dsscr_r])
                for dr in range(2):
                    k_ = (ch % 2) * 2 + dr
                    scm4 = scm[k_][:].rearrange("p (a e) b -> p a e b", e=2)
                    for e_, (pss_, prs_) in enumerate(banks[dr]):
                        tt("dve", scm4[:, :, e_, :], pss_[:, 0:256].rearrange("p (a b) -> p a b", b=128),
                           maskb[:, dr, :].unsqueeze(1).to_broadcast([128, 2, 128]), ALU.mult, [prs_, maskb.r], [scm[k_].r])
                pso, pro = PS()
                n_ = 0
                for dr in range(2):
                    k_ = (ch % 2) * 2 + dr
                    for h in range(4):
                        mm(pso[:, h * 96:(h + 1) * 96], scm[k_][:, h, :], vtok[:, ch, h * 96:(h + 1) * 96], n_ == 0, n_ == 7,
                           [scm[k_].r, vtok.r], pro, sig=(n_ == 7))
                        n_ += 1
                os_ = ofs[ch % 2]
                cp("act", os_[:], pso[:, 0:384], [pro], [os_.r])
                dma("sp", of_scr[:, ch, :], os_[:], [os_.r], [ofscr_r])

            for c_ in range(NCH + 1):
                if c_ < NCH:
                    front(c_)
                if c_ >= 1:
                    back(c_ - 1)
            R.barrier()
            AR.release(mk)
            mkB = AR.mark()
            decs2 = AR.alloc([2, NCH, 2], F32, "decs2")
            dS = AR.alloc([2, NCH, 192], F32, "dS")
            Sst = AR.alloc([2, NCH, 192], BF16, "Sst")
            S32 = AR.alloc([2, 2, 96], F32, "S32")
            St = AR.alloc([2, 2, 96], F32, "St")
            cp("dve", decs2[:], decs[:], [decs.r], [decs2.r])
            dma("sp", dS[:], ds_scr, [dsscr_r], [dS.r])
            memset("dve", S32[:], 0.0, [S32.r])
            for s_ in range(NCH):
                cp("act", Sst[:, :, s_, :], S32[:].rearrange("p d a b -> p d (a b)"), [S32.r], [Sst.r])
                if s_ < NCH - 1:
                    tt("dve", St[:], S32[:], decs2[:, :, s_, :].unsqueeze(3).to_broadcast([128, 2, 2, 96]), ALU.mult,
                       [S32.r, decs2.r], [St.r])
                    tt("dve", S32[:], St[:], dS[:, :, s_, :].rearrange("p d (a b) -> p d a b", b=96), ALU.add,
                       [St.r, dS.r], [S32.r])
            G = 3
            ol = [AR.alloc([384], F32, f"ol{i}") for i in range(G)]
            gsb = [AR.alloc([384], BF16, f"gsb{i}") for i in range(G)]
            qbl = [AR.alloc([2, 2, 128], BF16, f"qbl{i}") for i in range(G)]
            osum = [AR.alloc([384], F32, f"osum{i}") for i in range(G)]
            osq = [AR.alloc([384], F32, f"osq{i}") for i in range(G)]
            ss = [AR.alloc([4], F32, f"ss{i}") for i in range(G)]
            otm = [AR.alloc([384], BF16, f"otm{i}") for i in range(G)]
            mst = [AR.alloc([3, 128], BF16, f"mst{i}") for i in range(G)]
            for g0 in range(0, NCH, G):
                chs = list(range(g0, g0 + G))
                for i_, ch in enumerate(chs):
                    dma("sp", qbl[i_][:], qb_scr[:, :, ch, :].rearrange("p d (a b) -> p d a b", b=128), [qbscr_r], [qbl[i_].r])
                    dma("sp", ol[i_][:], of_scr[:, ch, :], [ofscr_r], [ol[i_].r])
                    dma("sp", gsb[i_][:], g_scr[:, ch, :], [gscr_r], [gsb[i_].r])
                pbanks = []
                for i_, ch in enumerate(chs):
                    psE, prE = PS()
                    psO, prO = PS()
                    for e_, (ps_, pr_) in enumerate(((psE, prE), (psO, prO))):
                        n_ = 0
                        for dr in range(2):
                            for p in range(2):
                                base = 64 * e_
                                mm(ps_[:, p * 96:(p + 1) * 96], qbl[i_][base:base + 64, dr, p, :],
                                   Sst[base:base + 64, dr, step_of[dr][ch], p * 96:(p + 1) * 96], n_ == 0, n_ == 3,
                                   [qbl[i_].r, Sst.r], pr_, sig=(n_ == 3))
                                n_ += 1
                    pbanks.append(((psE, prE), (psO, prO)))
                for i_, ch in enumerate(chs):
                    o4 = osum[i_][:].rearrange("p (a e b) -> p a e b", e=2, b=96)
                    l4 = ol[i_][:].rearrange("p (a e b) -> p a e b", e=2, b=96)
                    for e_, (ps_, pr_) in enumerate(pbanks[i_]):
                        tt("dve", o4[:, :, e_, :], l4[:, :, e_, :], ps_[:, 0:192].rearrange("p (a b) -> p a b", b=96), ALU.add,
                           [ol[i_].r, pr_], [osum[i_].r])
                for i_, ch in enumerate(chs):
                    act(osq[i_][:], osum[i_][:], AF.Square, [osum[i_].r], [osq[i_].r])
                for i_, ch in enumerate(chs):
                    treduce(ss[i_][:], osq[i_][:].rearrange("p (a b) -> p a b", b=96), [osq[i_].r], [ss[i_].r])
                for i_, ch in enumerate(chs):
                    act(ss[i_][:], ss[i_][:], AF.Sqrt, [ss[i_].r], [ss[i_].r], scale=1.0 / 96, bias=EPS)
                for i_, ch in enumerate(chs):
                    recip(ss[i_][:], ss[i_][:], [ss[i_].r], [ss[i_].r])
                for i_, ch in enumerate(chs):
                    o3 = osum[i_][:].rearrange("p (a b) -> p a b", b=96)
                    tt("dve", o3, o3, ss[i_][:].unsqueeze(2).to_broadcast([128, 4, 96]), ALU.mult, [osum[i_].r, ss[i_].r], [osum[i_].r])
                for i_, ch in enumerate(chs):
                    tt("dve", otm[i_][:], osum[i_][:], gsb[i_][:], ALU.mult, [osum[i_].r, gsb[i_].r], [otm[i_].r])
                tb = []
                for i_, ch in enumerate(chs):
                    pst_, prt = PS()
                    pstb = pst_[:, :].bitcast(BF16)
                    for k in range(3):
                        petr(pstb[:, k * 128:(k + 1) * 128], otm[i_][:, k * 128:(k + 1) * 128], identb[:], [otm[i_].r, identb.r], prt)
                    tb.append((pstb, prt))
                for i_, ch in enumerate(chs):
                    pstb, prt = tb[i_]
                    cp("act", mst[i_][:], pstb[:, 0:384].rearrange("p (a b) -> p a b", b=128), [prt], [mst[i_].r])
                    dma("sp", m_scr[:, 0:3, ch * 128:(ch + 1) * 128], mst[i_][:], [mst[i_].r], [mscr_r])
            R.barrier()
            AR.release(mkB)

        def stage_att(b, l):
            mk = AR.mark()
            qr = AR.alloc([3, T], BF16, "qr")
            kr = AR.alloc([T], BF16, "kr")
            vaug = AR.alloc([NCH, 2, 66], BF16, "vaug")
            esk = AR.alloc([6], F32, "esk")
            memset("dve", vaug[:], 1.0, [vaug.r])
            dma("sp", esk[:], sink_d[l].partition_broadcast(128), [], [esk.r])
            act(esk[:], esk[:], AF.Exp, [esk.r], [esk.r])
            mk2 = AR.mark()
            Wa = AR.alloc([KT, NFM_A + NTM_A], BF16, "Wa")
            rope = AR.alloc([2, 2048], F32, "rope")
            hb = [AR.alloc([KT, 512], BF16, f"hb{i}") for i in range(2)]
            t1 = AR.alloc([512], F32, "t1")
            t2 = AR.alloc([512], F32, "t2")
            dma("sp", rope[:], crope_d, [], [rope.r])
            load_w(Wa, wfm_d[l].rearrange("(kt p) c -> p kt c", p=128)[:, :, NFM_G:NFM_G + NFM_A], NFM_A)
            dma("pool", Wa[:, :, NFM_A:NFM_A + NTM_A], wtm_d[l].rearrange("(kt p) c -> p kt c", p=128)[:, :, NTM_G:NTM_G + NTM_A],
                [], [Wa.r])
            for bi, (t0, N) in enumerate(BLKS):
                h_ = hb[bi % 2]
                dma("sp", h_[:, :, 0:N], h_scr[:, :, t0:t0 + N], [hscr_r], [h_.r])
                for m in range(4):
                    dst = qr[:, m, t0:t0 + N] if m < 3 else kr[:, t0:t0 + N]
                    dres = qr.r if m < 3 else kr.r
                    wc = m * 128 if m < 3 else 768
                    wcp = (3 + m) * 128 if m < 3 else 896
                    ps, pr = PS()
                    for kt in range(KT):
                        mm(ps[:, 0:N], Wa[:, kt, wc:wc + 128], h_[:, kt, 0:N], kt == 0, kt == KT - 1, [Wa.r, h_.r], pr)
                    if bi == 0:
                        cp("act", dst, ps[:, 0:N], [pr], [dres])
                    else:
                        ps2, pr2 = PS()
                        for kt in range(KT):
                            mm(ps2[:, 0:N], Wa[:, kt, wcp:wcp + 128], h_[:, kt, 0:N], kt == 0, kt == KT - 1, [Wa.r, h_.r], pr2)
                        tl0 = t0 - TCX
                        tt("dve", t1[:, 0:N], ps[:, 0:N], rope[:, 0, tl0:tl0 + N], ALU.mult, [pr, rope.r], [t1.r])
                        tt("dve", t2[:, 0:N], ps2[:, 0:N], rope[:, 1, tl0:tl0 + N], ALU.mult, [pr2, rope.r], [t2.r])
                        tt("pool", dst, t1[:, 0:N], t2[:, 0:N], ALU.add, [t1.r, t2.r], [dres])
                for cc in range(N // 128):
                    ch = t0 // 128 + cc
                    ps, pr = PS()
                    for kt in range(KT):
                        mm(ps[:, 0:128], h_[:, kt, cc * 128:(cc + 1) * 128], Wa[:, kt, NFM_A:NFM_A + 128], kt == 0, kt == KT - 1,
                           [Wa.r, h_.r], pr)
                    cp("act", vaug[:, ch, :, 0:64], ps[:, 0:128].rearrange("p (a b) -> p a b", b=64), [pr], [vaug.r])
            R.barrier()
            AR.release(mk2)
            NPT = 12
            pt = [AR.alloc([384], BF16, f"pt{i}") for i in range(NPT)]
            den = [AR.alloc([2, 3], F32, f"den{i}") for i in range(2)]
            otm = [AR.alloc([384], BF16, f"otm{i}") for i in range(2)]
            mst = [AR.alloc([3, 128], BF16, f"mst{i}") for i in range(2)]
            pti = [0]

            def keys_of(qc):
                keys = [(0, None), (1, None)]
                if qc >= 2:
                    if qc - 1 >= 2:
                        keys.append((qc - 1, 1))
                    keys.append((qc, None))
                    if qc + 1 <= NCH - 1:
                        keys.append((qc + 1, 0))
                return keys

            def att_scores(qc, kv):
                qtok = slice(qc * 128, (qc + 1) * 128)
                base = 64 * kv
                out = []
                for (kc, mk_) in keys_of(qc):
                    ktok_ = slice(kc * 128, (kc + 1) * 128)
                    pss, prs = PS()
                    for hh in range(3):
                        mm(pss[:, hh * 128:(hh + 1) * 128], kr[base:base + 64, ktok_], qr[base:base + 64, hh, qtok],
                           True, True, [kr.r, qr.r], prs)
                    p_ = pt[pti[0] % NPT]
                    pti[0] += 1
                    act(p_[:], pss[:, 0:384], AF.Exp, [prs], [p_.r], scale=0.125)
                    if mk_ is not None:
                        tt("dve", p_[:].rearrange("p (a b) -> p a b", b=128), p_[:].rearrange("p (a b) -> p a b", b=128),
                           maskb[:, mk_, :].unsqueeze(1).to_broadcast([128, 3, 128]), ALU.mult, [p_.r, maskb.r], [p_.r])
                    out.append((kc, p_))
                return out

            def att_pv(qc, kv, plist):
                o_ = otm[qc % 2]
                d_ = den[qc % 2]
                pso, pro = PS()
                for ki, (kc, p_) in enumerate(plist):
                    for hh in range(3):
                        mm(pso[:, hh * 65:(hh + 1) * 65], p_[:, hh * 128:(hh + 1) * 128], vaug[:, kc, kv, 0:65],
                           ki == 0 and hh == 0, ki == len(plist) - 1 and hh == 2, [p_.r, vaug.r], pro, sig=(hh == 2))
                po3 = pso[:, 0:195].rearrange("p (a b) -> p a b", b=65)
                tt("dve", d_[:, kv, :], po3[:, :, 64], esk[:, 3 * kv:3 * kv + 3], ALU.add, [pro, esk.r], [d_.r])
                recip(d_[:, kv, :], d_[:, kv, :], [d_.r], [d_.r])
                tt("dve", o_[:, kv * 192:(kv + 1) * 192].rearrange("p (a b) -> p a b", b=64), po3[:, :, 0:64],
                   d_[:, kv, :].unsqueeze(2).to_broadcast([128, 3, 64]), ALU.mult, [pro, d_.r], [o_.r])

            def att_fin(qc):
                o_ = otm[qc % 2]
                m_ = mst[qc % 2]
                pst_, prt = PS()
                pstb = pst_[:, :].bitcast(BF16)
                for k in range(3):
                    petr(pstb[:, k * 128:(k + 1) * 128], o_[:, k * 128:(k + 1) * 128], identb[:], [o_.r, identb.r], prt)
                cp("act", m_[:], pstb[:, 0:384].rearrange("p (a b) -> p a b", b=128), [prt], [m_.r])
                dma("sp", m_scr[:, 3:6, qc * 128:(qc + 1) * 128], m_[:], [m_.r], [mscr_r])

            units = [(qc, kv) for qc in range(NCH) for kv in range(2)]
            pend = att_scores(*units[0])
            for ui, (qc, kv) in enumerate(units):
                nxt = att_scores(*units[ui + 1]) if ui + 1 < len(units) else None
                att_pv(qc, kv, pend)
                if kv == 1:
                    att_fin(qc)
                pend = nxt
            R.barrier()
            AR.release(mk)

        def sincos(src, K_, F_, dst_s, dst_c):
            cp("dve", K_[:], src[:], [src.r], [K_.r])
            cp("dve", F_[:], K_[:], [K_.r], [F_.r])
            tt("dve", F_[:], src[:], F_[:], ALU.subtract, [src.r, F_.r], [F_.r])
            ts("dve", src[:], F_[:], 0.49999, -0.49999, ALU.min, ALU.max, [F_.r], [src.r])
            act(dst_s[:], src[:], AF.Sin, [src.r], [dst_s.r], scale=TWO_PI)
            ts("dve", F_[:], F_[:], 0.25, None, ALU.add, None, [F_.r], [F_.r])
            cp("dve", K_[:], F_[:], [F_.r], [K_.r])
            cp("dve", src[:], K_[:], [K_.r, dst_s.r], [src.r])
            tt("dve", F_[:], F_[:], src[:], ALU.subtract, [F_.r, src.r], [F_.r])
            ts("dve", F_[:], F_[:], 0.49999, -0.49999, ALU.min, ALU.max, [F_.r], [F_.r])
            act(dst_c[:], F_[:], AF.Sin, [F_.r], [dst_c.r], scale=TWO_PI)

        def stage_s5(b, l):
            NS = 288
            mk = AR.mark()
            uJ = AR.alloc([2, 8, 288], BF16, "uJ")
            yT = AR.alloc([2, T], BF16, "yT")
            mk3 = AR.mark()
            Ws = AR.alloc([KT, NFM_S], BF16, "Ws")
            hb = [AR.alloc([KT, 512], BF16, f"hb{i}") for i in range(2)]
            load_w(Ws, wfm_d[l].rearrange("(kt p) c -> p kt c", p=128)[:, :, NFM_G + NFM_A:NFM_G + NFM_A + NFM_S], NFM_S)
            for bi, (t0, N) in enumerate(BLKS):
                h_ = hb[bi % 2]
                dma("sp", h_[:, :, 0:N], h_scr[:, :, t0:t0 + N], [hscr_r], [h_.r])
                for ct in range(2):
                    ps, pr = PS()
                    for kt in range(KT):
                        mm(ps[:, 0:N], Ws[:, kt, ct * 128:(ct + 1) * 128], h_[:, kt, 0:N], kt == 0, kt == KT - 1, [Ws.r, h_.r], pr)
                    cp("act" if ct == 0 else "dve", uJ[:, ct, :, t0 // 8:(t0 + N) // 8].rearrange("p j n -> p n j"),
                       ps[:, 0:N].rearrange("p (n j) -> p n j", j=8), [pr], [uJ.r])
            R.barrier()
            AR.release(mk3)

            lam = AR.alloc([2, 2, 8], F32, "lam")
            dtt = AR.alloc([2, 8], F32, "dt")
            Bp = AR.alloc([2, 2, 8, 16], F32, "Bp")
            Cp = AR.alloc([2, 2, 8, 16], F32, "Cp")
            dsk = AR.alloc([2], F32, "dsk")
            bd32 = AR.alloc([128], F32, "bd32")
            cpp = AR.alloc([2], F32, "cpp")
            mI = AR.alloc([NS], F32, "mI")
            r8 = AR.alloc([2, 8], F32, "r8")
            phi = AR.alloc([2, 8], F32, "phi")
            dma("sp", lam[:], s5lam_d[l].rearrange("d p c g -> p d c g"), [], [lam.r])
            dma("sp", dtt[:], s5step_d[l].rearrange("d p g -> p d g"), [], [dtt.r])
            dma("sp", Bp[:], s5B_d[l].rearrange("d p c g h -> p d c g h"), [], [Bp.r])
            dma("sp", Cp[:], s5C_d[l].rearrange("d p c g h -> p d c g h"), [], [Cp.r])
            dma("sp", dsk[:], s5d_d[l], [], [dsk.r])
            dma("sp", bd32[:], cbd32_d, [], [bd32.r])
            dma("sp", cpp[:], cpp_d, [], [cpp.r])
            dma("sp", mI[:], cm_d, [], [mI.r])
            ldt = AR.alloc([2, 8], F32, "ldt")
            wdt = AR.alloc([2, 8], F32, "wdt")
            arp = AR.alloc([2, 9, 8], F32, "arp")
            aip = AR.alloc([2, 9, 8], F32, "aip")
            ang = AR.alloc([2, 9, 8], F32, "ang")
            ki = AR.alloc([2, 9, 8], I32, "ki")
            kf = AR.alloc([2, 9, 8], F32, "kf")
            mag = AR.alloc([2, 9, 8], F32, "mag")
            tA = AR.alloc([2, 8], F32, "tA")
            tB = AR.alloc([2, 8], F32, "tB")
            tC = AR.alloc([2, 8], F32, "tC")
            fr = AR.alloc([2, 8], F32, "fr")
            fi = AR.alloc([2, 8], F32, "fi")
            Bbr = AR.alloc([2, 8, 16], F32, "Bbr")
            Bbi = AR.alloc([2, 8, 16], F32, "Bbi")
            u1 = AR.alloc([8, 16], F32, "u1")
            u2 = AR.alloc([8, 16], F32, "u2")
            u3 = AR.alloc([8, 16], F32, "u3")
            tbd = AR.alloc([128], F32, "tbd")
            act(dtt[:], dtt[:], AF.Exp, [dtt.r], [dtt.r])
            ts("dve", lam[:, :, 0, :], lam[:, :, 0, :], -1e-4, None, ALU.min, None, [lam.r], [lam.r])
            tt("dve", ldt[:], lam[:, :, 0, :], dtt[:], ALU.mult, [lam.r, dtt.r], [ldt.r])
            tt("dve", wdt[:], lam[:, :, 1, :], dtt[:], ALU.mult, [lam.r, dtt.r], [wdt.r])
            for tau in range(9):
                act(mag[:, :, tau, :], ldt[:], AF.Exp, [ldt.r], [mag.r], scale=float(tau))
                ts("dve", ang[:, :, tau, :], wdt[:], float(tau) / TWO_PI, None, ALU.mult, None, [wdt.r], [ang.r])
            sincos(ang, ki, kf, aip, arp)
            tt("dve", arp[:], arp[:], mag[:], ALU.mult, [arp.r, mag.r], [arp.r])
            tt("dve", aip[:], aip[:], mag[:], ALU.mult, [aip.r, mag.r], [aip.r])
            cp("dve", r8[:], mag[:, :, 8, :], [mag.r], [r8.r])
            ts("dve", phi[:], wdt[:], 8.0 / TWO_PI, None, ALU.mult, None, [wdt.r], [phi.r])
            cp("dve", ki[:, :, 0, :], phi[:], [phi.r], [ki.r])
            cp("dve", kf[:, :, 0, :], ki[:, :, 0, :], [ki.r], [kf.r])
            tt("dve", phi[:], phi[:], kf[:, :, 0, :], ALU.subtract, [phi.r, kf.r], [phi.r])
            lr_, li_ = lam[:, :, 0, :], lam[:, :, 1, :]
            tt("dve", tA[:], lr_, lr_, ALU.mult, [lam.r], [tA.r])
            tt("dve", tB[:], li_, li_, ALU.mult, [lam.r], [tB.r])
            tt("dve", tA[:], tA[:], tB[:], ALU.add, [tA.r, tB.r], [tA.r])
            recip(tA[:], tA[:], [tA.r], [tA.r])
            ts("dve", tB[:], arp[:, :, 1, :], -1.0, None, ALU.add, None, [arp.r], [tB.r])
            tt("dve", fr[:], tB[:], lr_, ALU.mult, [tB.r, lam.r], [fr.r])
            tt("dve", tC[:], aip[:, :, 1, :], li_, ALU.mult, [aip.r, lam.r], [tC.r])
            tt("dve", fr[:], fr[:], tC[:], ALU.add, [fr.r, tC.r], [fr.r])
            tt("dve", fr[:], fr[:], tA[:], ALU.mult, [fr.r, tA.r], [fr.r])
            tt("dve", fi[:], aip[:, :, 1, :], lr_, ALU.mult, [aip.r, lam.r], [fi.r])
            tt("dve", tC[:], tB[:], li_, ALU.mult, [tB.r, lam.r], [tC.r])
            tt("dve", fi[:], fi[:], tC[:], ALU.subtract, [fi.r, tC.r], [fi.r])
            tt("dve", fi[:], fi[:], tA[:], ALU.mult, [fi.r, tA.r], [fi.r])

            def bc16(ap2):
                return ap2.unsqueeze(2).to_broadcast([128, 8, 16])

            for dr in range(2):
                tt("dve", u1[:], Bp[:, dr, 0], bc16(fr[:, dr, :]), ALU.mult, [Bp.r, fr.r], [u1.r])
                tt("dve", u2[:], Bp[:, dr, 1], bc16(fi[:, dr, :]), ALU.mult, [Bp.r, fi.r], [u2.r])
                tt("dve", Bbr[:, dr], u1[:], u2[:], ALU.subtract, [u1.r, u2.r], [Bbr.r])
                tt("dve", u1[:], Bp[:, dr, 1], bc16(fr[:, dr, :]), ALU.mult, [Bp.r, fr.r], [u1.r])
                tt("dve", u2[:], Bp[:, dr, 0], bc16(fi[:, dr, :]), ALU.mult, [Bp.r, fi.r], [u2.r])
                tt("dve", Bbi[:, dr], u1[:], u2[:], ALU.add, [u1.r, u2.r], [Bbi.r])

            for ct in range(2):
                mkc = AR.mark()
                pq = slice(ct * 4, ct * 4 + 4)
                BDT = AR.alloc([2, 8, 128], BF16, "BDT")
                CmI = AR.alloc([4, 32, 32], BF16, "CmI")
                Dst = AR.alloc([16, NS], F32, "Dst")
                memset("dve", CmI[:], 0.0, [CmI.r])
                mkb = AR.mark()
                BmJ = AR.alloc([64, 128], BF16, "BmJ")
                CAMa = AR.alloc([9, 2, 128], F32, "CAMa")
                BbM = AR.alloc([2, 2, 128], F32, "BbM")
                WMa = AR.alloc([8, 2, 128], F32, "WMa")
                V1 = AR.alloc([9, 4, 16], F32, "V1")
                V2 = AR.alloc([9, 4, 16], F32, "V2")
                V3 = AR.alloc([9, 4, 16], F32, "V3")
                memset("dve", CAMa[:], 0.0, [CAMa.r])
                memset("dve", BbM[:], 0.0, [BbM.r])
                memset("dve", WMa[:], 0.0, [WMa.r])

                def bmj(pp, dr, j, ri):
                    return BmJ[:, ((pp * 2 + dr) * 8 + j) * 2 + ri, :]

                cam6 = CAMa[:].rearrange("p t r (g a h) -> p t r g a h", a=2, h=16)
                wm6 = WMa[:].rearrange("p t r (g a h) -> p t r g a h", a=2, h=16)
                for dr in range(2):
                    bm4 = BbM[:, dr].rearrange("p r (g a h) -> p r g a h", a=2, h=16)
                    for g2 in range(2):
                        hs = slice(64 * g2, 64 * g2 + 64)
                        cp("dve", bm4[hs, 0, :, g2, :], Bbr[hs, dr, pq, :], [Bbr.r], [BbM.r])
                        cp("dve", bm4[hs, 1, :, g2, :], Bbi[hs, dr, pq, :], [Bbi.r], [BbM.r])

                def b_h(ap3, n):
                    return ap3.unsqueeze(3).to_broadcast([128, n, 4, 16])

                def b_t(ap3, n):
                    return ap3.unsqueeze(1).to_broadcast([128, n, 4, 16])

                for dr in range(2):
                    ar9, ai9 = b_h(arp[:, dr, :, pq], 9), b_h(aip[:, dr, :, pq], 9)
                    cr9, ci9 = b_t(Cp[:, dr, 0, pq, :], 9), b_t(Cp[:, dr, 1, pq, :], 9)
                    tt("dve", V1[:], cr9, ar9, ALU.mult, [Cp.r, arp.r], [V1.r])
                    tt("dve", V2[:], ci9, ai9, ALU.mult, [Cp.r, aip.r], [V2.r])
                    tt("dve", V3[:], V1[:], V2[:], ALU.subtract, [V1.r, V2.r], [V3.r])
                    for g2 in range(2):
                        hs = slice(64 * g2, 64 * g2 + 64)
                        cp("dve", cam6[hs, :, 0, :, g2, :], V3[hs], [V3.r], [CAMa.r])
                    tt("dve", V1[:], cr9, ai9, ALU.mult, [Cp.r, aip.r], [V1.r])
                    tt("dve", V2[:], ci9, ar9, ALU.mult, [Cp.r, arp.r], [V2.r])
                    stt("dve", V3[:], V1[:], -1.0, V2[:], ALU.mult, ALU.subtract, [V1.r, V2.r], [V3.r])
                    for g2 in range(2):
                        hs = slice(64 * g2, 64 * g2 + 64)
                        cp("dve", cam6[hs, :, 1, :, g2, :], V3[hs], [V3.r], [CAMa.r])
                    for ri in range(2):
                        src = CAMa[:, 1:9, ri, :] if dr == 0 else CAMa[:, 8:0:-1, ri, :]
                        cp("act", CmI[:, :, dr * 16 + ri:dr * 16 + 16:2, :], src.rearrange("p t (g c) -> p g t c", c=32),
                           [CAMa.r], [CmI.r])
                    for k4 in range(2):
                        ps, pr = PS()
                        for t4 in range(4):
                            tau = k4 * 4 + t4
                            for ri in range(2):
                                mm(ps[:, t4 * 128:(t4 + 1) * 128], BbM[:, dr, ri, :], CAMa[:, tau, ri, :], ri == 0, ri == 1,
                                   [BbM.r, CAMa.r], pr)
                        tt("dve", BDT[:, dr, k4 * 4:k4 * 4 + 4, :], ps[:, :].rearrange("p (t c) -> p t c", c=128),
                           bd32[:].unsqueeze(1).to_broadcast([128, 4, 128]), ALU.mult, [pr, bd32.r], [BDT.r])
                    if dr == 0:
                        stt("dve", BDT[:, 0, 0, :], ident[:], dsk[:, ct:ct + 1], BDT[:, 0, 0, :], ALU.mult, ALU.add,
                            [ident.r, dsk.r, BDT.r], [BDT.r])
                    if dr == 0:
                        ar8, ai8 = b_h(arp[:, dr, 7::-1, pq], 8), b_h(aip[:, dr, 7::-1, pq], 8)
                    else:
                        ar8, ai8 = b_h(arp[:, dr, 0:8, pq], 8), b_h(aip[:, dr, 0:8, pq], 8)
                    br8, bi8 = b_t(Bbr[:, dr, pq, :], 8), b_t(Bbi[:, dr, pq, :], 8)
                    tt("dve", V1[:, 0:8], br8, ar8, ALU.mult, [Bbr.r, arp.r], [V1.r])
                    tt("dve", V2[:, 0:8], bi8, ai8, ALU.mult, [Bbi.r, aip.r], [V2.r])
                    tt("dve", V3[:, 0:8], V1[:, 0:8], V2[:, 0:8], ALU.subtract, [V1.r, V2.r], [V3.r])
                    for g2 in range(2):
                        hs = slice(64 * g2, 64 * g2 + 64)
                        cp("dve", wm6[hs, :, 0, :, g2, :], V3[hs, 0:8], [V3.r], [WMa.r])
                    tt("dve", V1[:, 0:8], bi8, ar8, ALU.mult, [Bbi.r, arp.r], [V1.r])
                    tt("dve", V2[:, 0:8], br8, ai8, ALU.mult, [Bbr.r, aip.r], [V2.r])
                    tt("dve", V3[:, 0:8], V1[:, 0:8], V2[:, 0:8], ALU.add, [V1.r, V2.r], [V3.r])
                    for g2 in range(2):
                        hs = slice(64 * g2, 64 * g2 + 64)
                        cp("dve", wm6[hs, :, 1, :, g2, :], V3[hs, 0:8], [V3.r], [WMa.r])
                    for k4 in range(4):
                        ps, pr = PS()
                        for jj in range(2):
                            for ri in range(2):
                                c_ = (jj * 2 + ri) * 128
                                petr(ps[:, c_:c_ + 128], WMa[:, k4 * 2 + jj, ri, :], ident[:], [WMa.r, ident.r], pr)
                        for pp in range(2):
                            s0 = ((pp * 2 + dr) * 8 + k4 * 2) * 2
                            ts("dve", BmJ[:, s0:s0 + 4, :], ps[:, :].rearrange("p (t c) -> p t c", c=128), cpp[:, pp:pp + 1], None,
                               ALU.mult, None, [pr, cpp.r], [BmJ.r])
                for p4 in range(4):
                    half, pp = p4 // 2, p4 % 2
                    hs = slice(64 * half, 64 * half + 64)
                    for dr in range(2):
                        for ri in range(2):
                            ps, pr = PS()
                            for j in range(8):
                                mm(ps[:, 0:NS], bmj(pp, dr, j, ri)[hs, :], uJ[hs, ct, j, :], j == 0, j == 7, [BmJ.r, uJ.r], pr)
                            slot = dr * 8 + ri * 4 + p4
                            if dr == 0:
                                cp("act", Dst[:, slot, :], ps[:, 0:NS], [pr], [Dst.r])
                            else:
                                cp("act", Dst[:, slot, 0:32], ps[:, 0:32][:, ::-1], [pr], [Dst.r])
                                cp("dve", Dst[:, slot, 32:NS], ps[:, 32:NS][:, ::-1], [pr], [Dst.r])
                R.barrier()
                AR.release(mkb)
                Xbf = AR.alloc([16, NS], BF16, "Xbf")
                Ec = AR.alloc([4, NS], F32, "Ec")
                Es = AR.alloc([4, NS], F32, "Es")
                pk = AR.alloc([4, NS], I32, "pk")
                pf = AR.alloc([4, NS], F32, "pf")
                pa = AR.alloc([4, NS], F32, "pa")
                w1_ = AR.alloc([4, NS], F32, "w1_")
                w2_ = AR.alloc([4, NS], F32, "w2_")
                w3_ = AR.alloc([4, NS], F32, "w3_")
                w4_ = AR.alloc([4, NS], F32, "w4_")
                memset("dve", Xbf[:], 0.0, [Xbf.r])
                for dr in range(2):
                    tt("dve", pa[:], mI[:].unsqueeze(1).to_broadcast([128, 4, NS]),
                       phi[:, dr, pq].unsqueeze(2).to_broadcast([128, 4, NS]), ALU.mult, [mI.r, phi.r], [pa.r])
                    sincos(pa, pk, pf, Es, Ec)
                    Dr_ = Dst[:, dr * 8:dr * 8 + 4, :]
                    Di_ = Dst[:, dr * 8 + 4:dr * 8 + 8, :]
                    tt("dve", w1_[:], Dr_, Ec[:], ALU.mult, [Dst.r, Ec.r], [w1_.r])
                    tt("pool", w2_[:], Di_, Es[:], ALU.mult, [Dst.r, Es.r], [w2_.r])
                    tt("dve", w3_[:], Di_, Ec[:], ALU.mult, [Dst.r, Ec.r], [w3_.r])
                    tt("pool", w4_[:], Dr_, Es[:], ALU.mult, [Dst.r, Es.r], [w4_.r])
                    tt("dve", Dr_, w1_[:], w2_[:], ALU.add, [w1_.r, w2_.r], [Dst.r])
                    tt("dve", Di_, w3_[:], w4_[:], ALU.subtract, [w3_.r, w4_.r], [Dst.r])
                    for ri in range(2):
                        for p4 in range(4):
                            sl = dr * 8 + ri * 4 + p4
                            tscan(Dst[:, sl, :], r8[:, dr, ct * 4 + p4:ct * 4 + p4 + 1].to_broadcast([128, NS]), Dst[:, sl, :],
                                  [Dst.r, r8.r], [Dst.r])
                    M1 = NS - 1
                    Sr, Si = Dst[:, dr * 8:dr * 8 + 4, 0:M1], Dst[:, dr * 8 + 4:dr * 8 + 8, 0:M1]
                    cc_, ss_ = Ec[:, :, 0:M1], Es[:, :, 0:M1]
                    tt("dve", w1_[:, :, 0:M1], Sr, cc_, ALU.mult, [Dst.r, Ec.r], [w1_.r])
                    tt("pool", w2_[:, :, 0:M1], Si, ss_, ALU.mult, [Dst.r, Es.r], [w2_.r])
                    tt("dve", w3_[:, :, 0:M1], Si, cc_, ALU.mult, [Dst.r, Ec.r], [w3_.r])
                    tt("pool", w4_[:, :, 0:M1], Sr, ss_, ALU.mult, [Dst.r, Es.r], [w4_.r])
                    for (xo, a_, b_, op) in ((Xbf[:, dr * 8:dr * 8 + 4, :], w1_, w2_, ALU.subtract),
                                             (Xbf[:, dr * 8 + 4:dr * 8 + 8, :], w3_, w4_, ALU.add)):
                        if dr == 0:
                            tt("dve", xo[:, :, 1:NS], a_[:, :, 0:M1], b_[:, :, 0:M1], op, [a_.r, b_.r], [Xbf.r])
                        else:
                            tt("dve", xo[:, :, 0:31][:, :, ::-1], a_[:, :, 0:31], b_[:, :, 0:31], op, [a_.r, b_.r], [Xbf.r])
                            tt("dve", xo[:, :, 32:NS][:, :, ::-1], a_[:, :, 31:M1], b_[:, :, 31:M1], op, [a_.r, b_.r], [Xbf.r])
                for i in range(8):
                    ps, pr = PS()
                    for p4 in range(4):
                        cnt = 0
                        for dr in range(2):
                            for ri in range(2):
                                slot = dr * 8 + ri * 4 + p4
                                kw = dict(tile_position=(0, 96)) if p4 == 3 else {}
                                mm(ps[32 * p4:32 * p4 + 32, 0:NS], CmI[:, p4, (dr * 8 + i) * 2 + ri, :], Xbf[:, slot, :],
                                   cnt == 0, False, [CmI.r, Xbf.r], pr, sig=False, **kw)
                                cnt += 1
                    terms = []
                    for dr in range(2):
                        js = range(0, i + 1) if dr == 0 else range(i, 8)
                        for j in js:
                            terms.append((dr, j, i - j if dr == 0 else j - i))
                    for n_, (dr, j, tau) in enumerate(terms):
                        lastt = n_ == len(terms) - 1
                        mm(ps[:, 0:NS], BDT[:, dr, tau, :], uJ[:, ct, j, :], False, lastt, [BDT.r, uJ.r], pr, sig=lastt)
                    act(yT[:, ct, i:T:8], ps[:, 0:NS], AF.Gelu, [pr], [yT.r])
                R.barrier()
                AR.release(mkc)

            Wgl = AR.alloc([2, 512], BF16, "Wgl")
            gb = AR.alloc([4], F32, "gb")
            sg = [AR.alloc([512], F32, f"sg{i}") for i in range(2)]
            mst = [AR.alloc([2, 512], BF16, f"mst{i}") for i in range(2)]
            dma("pool", Wgl[:], gluw_d[l].rearrange("(k p) c -> p k c", p=128), [], [Wgl.r])
            dma("sp", gb[:], glub_d[l], [], [gb.r])
            for bi, (t0, N) in enumerate(BLKS):
                ms_ = mst[bi % 2]
                for mt in range(2):
                    psa, pra = PS()
                    psg, prg = PS()
                    for k in range(2):
                        mm(psa[:, 0:N], Wgl[:, k, mt * 128:(mt + 1) * 128], yT[:, k, t0:t0 + N], k == 0, k == 1, [Wgl.r, yT.r], pra)
                    for k in range(2):
                        mm(psg[:, 0:N], Wgl[:, k, 256 + mt * 128:256 + (mt + 1) * 128], yT[:, k, t0:t0 + N], k == 0, k == 1,
                           [Wgl.r, yT.r], prg)
                    s_ = sg[mt]
                    act(s_[:, 0:N], psg[:, 0:N], AF.Sigmoid, [prg, gb.r], [s_.r], bias=gb[:, 2 + mt:3 + mt])
                    stt("dve", ms_[:, mt, 0:N], psa[:, 0:N], gb[:, mt:mt + 1], s_[:, 0:N], ALU.add, ALU.mult, [pra, gb.r, s_.r], [ms_.r])
                dma("sp", m_scr[:, 6:8, t0:t0 + N], ms_[:, :, 0:N], [ms_.r], [mscr_r])
            R.barrier()
            AR.release(mk)

        def stage_wout(b, l):
            mk = AR.mark()
            Wo = AR.alloc([KT, D], BF16, "Wo")
            mb = [AR.alloc([KT, 512], BF16, f"mb{i}") for i in range(2)]
            load_w(Wo, wout_d[l].rearrange("(kt p) c -> p kt c", p=128), D)
            for bi, (t0, N) in enumerate(BLKS):
                j = 2 if bi == 0 else b
                m_ = mb[bi % 2]
                dma("sp", m_[:, :, 0:N], m_scr[:, :, t0:t0 + N], [mscr_r], [m_.r])
                for mt in range(KT):
                    ps, pr = PS()
                    for kt in range(KT):
                        mm(ps[:, 0:N], Wo[:, kt, mt * 128:(mt + 1) * 128], m_[:, kt, 0:N], kt == 0, kt == KT - 1, [Wo.r, m_.r], pr)
                    stt("dve", x[:, mt, t0:t0 + N], ps[:, 0:N], G1(l, mt, j), x[:, mt, t0:t0 + N], ALU.mult, ALU.add,
                        [pr, modt.r, xres[bi]], [xres[bi]])
            R.barrier()
            AR.release(mk)

        def stage_mlp(b, l):
            mk0 = AR.mark()
            W1 = [AR.alloc([KT, 1024], BF16, f"W1{i}") for i in range(2)]
            W2 = [AR.alloc([KT, 1024], BF16, f"W2{i}") for i in range(2)]
            w1v = w1_d[l].rearrange("(kt p) c -> p kt c", p=128)
            w2v = w2_d[l].rearrange("(kt p) c -> p kt c", p=128)

            def load_q(q):
                W1_, W2_ = W1[q % 2], W2[q % 2]
                for c0 in range(0, 1024, 512):
                    dma("pool", W1_[:, :, c0:c0 + 512], w1v[:, :, q * 1024 + c0:q * 1024 + c0 + 512], [], [W1_.r])
                for c0 in range(0, 1024, 512):
                    dma("pool", W2_[:, :, c0:c0 + 512], w2v[:, q * 8:(q + 1) * 8, c0:c0 + 512], [], [W2_.r])

            load_q(0)
            load_q(1)
            mk = AR.mark()
            hb = [AR.alloc([KT, 512], BF16, f"hb{i}") for i in range(2)]
            sq = AR.alloc([KT, 512], BF16, "sq")
            rstd = AR.alloc([512], F32, "rstd")
            tmpf = [AR.alloc([512], F32, f"tf{i}") for i in range(2)]
            for bi, (t0, N) in enumerate(BLKS):
                j = 2 if bi == 0 else b
                h_ = hb[bi % 2]
                rms_block(bi, lambda kt: A2[:, l, kt, j:j + 1], lambda kt: SH2(l, kt, j), h_, sq, rstd, tmpf)
                dma("sp", h_scr[:, :, t0:t0 + N], h_[:, :, 0:N], [h_.r], [hscr_r])
            R.barrier()
            AR.release(mk)
            hb = [AR.alloc([KT, 512], BF16, f"hb{i}") for i in range(2)]
            hid = [AR.alloc([KT, 512], BF16, f"hid{i}") for i in range(2)]
            rl = [AR.alloc([512], F32, f"rl{i}") for i in range(2)]
            for q in range(4):
                W1_, W2_ = W1[q % 2], W2[q % 2]
                if q >= 2:
                    load_q(q)
                for bi, (t0, N) in enumerate(BLKS):
                    j = 2 if bi == 0 else b
                    h_ = hb[bi % 2]
                    hd = hid[bi % 2]
                    dma("sp", h_[:, :, 0:N], h_scr[:, :, t0:t0 + N], [hscr_r], [h_.r])
                    for mt in range(8):
                        ps, pr = PS()
                        for kt in range(KT):
                            mm(ps[:, 0:N], W1_[:, kt, mt * 128:(mt + 1) * 128], h_[:, kt, 0:N], kt == 0, kt == KT - 1, [W1_.r, h_.r], pr)
                        r_ = rl[mt % 2]
                        act(r_[:, 0:N], ps[:, 0:N], AF.Relu, [pr], [r_.r])
                        tt("pool" if mt % 2 else "dve", hd[:, mt, 0:N], r_[:, 0:N], r_[:, 0:N], ALU.mult, [r_.r], [hd.r])
                    for mt in range(KT):
                        ps, pr = PS()
                        for kt in range(8):
                            mm(ps[:, 0:N], W2_[:, kt, mt * 128:(mt + 1) * 128], hd[:, kt, 0:N], kt == 0, kt == 7, [W2_.r, hd.r], pr)
                        stt("dve", x[:, mt, t0:t0 + N], ps[:, 0:N], G2(l, mt, j), x[:, mt, t0:t0 + N], ALU.mult, ALU.add,
                            [pr, modt.r, xres[bi]], [xres[bi]])
            R.barrier()
            AR.release(mk0)

        def stage_final(b):
            mk = AR.mark()
            sq = AR.alloc([KT, 512], BF16, "sq")
            rstd = AR.alloc([512], F32, "rstd")
            ob = [AR.alloc([KT, 512], F32, f"ob{i}") for i in range(2)]
            for bi, (t0, N) in enumerate(BLKS):
                if bi == 0:
                    continue
                o_ = ob[bi % 2]
                xr = xres[bi]
                act(sq[:, :, 0:N], x[:, :, t0:t0 + N], AF.Square, [xr], [sq.r])
                ps, pr = PS()
                for kt in range(KT):
                    mm(ps[:, 0:N], onesb[:], sq[:, kt, 0:N], kt == 0, kt == KT - 1, [onesb.r, sq.r], pr)
                act(rstd[:, 0:N], ps[:, 0:N], AF.Sqrt, [pr], [rstd.r], scale=1.0 / D, bias=EPS)
                recip(rstd[:, 0:N], rstd[:, 0:N], [rstd.r], [rstd.r])
                for kt in range(KT):
                    stt("dve", o_[:, kt, 0:N], x[:, kt, t0:t0 + N], nrm[:, 8, kt:kt + 1], rstd[:, 0:N], ALU.mult, ALU.mult,
                        [xr, nrm.r, rstd.r], [o_.r])
                dma("sp", out_d[b, :, :, t0 - TCX:t0 - TCX + N], o_[:, :, 0:N], [o_.r], [])
            R.barrier()
            AR.release(mk)

        for b in range(n_b):
            for bi, (t0, N) in enumerate(BLKS):
                dma("sp", x[:, :, t0:t0 + N], xin[b, :, :, t0:t0 + N], [], [xres[bi]])
            for l in range(n_layers):
                if "n" in stages:
                    stage_norm1(b, l)
                if "g" in stages:
                    stage_gla(b, l)
                if "a" in stages:
                    stage_att(b, l)
                if "s" in stages:
                    stage_s5(b, l)
                if "w" in stages:
                    stage_wout(b, l)
                if dbg and b == 0 and l == 0:
                    R.barrier()
                    for bi_, (t0_, N_) in enumerate(BLKS):
                        dma("sp", dbg_xa[:, :, t0_:t0_ + N_], x[:, :, t0_:t0_ + N_], [xres[bi_]], [])
                if "m" in stages:
                    stage_mlp(b, l)
                if dbg and b == 0 and l == 0:
                    R.barrier()
                    for bi_, (t0_, N_) in enumerate(BLKS):
                        dma("sp", dbg_xb[:, :, t0_:t0_ + N_], x[:, :, t0_:t0_ + N_], [xres[bi_]], [])
            stage_final(b)
        R.barrier()
        print("ops", R.n_ops, "waits", R.n_waits, "sems", len(R.sems), "arena peak", AR.peak, {k: len(v) for k, v in R.prog.items()}, flush=True)
        R.emit()
    return nc


def _perm64():
    p = np.zeros(64, np.int64)
    for d in range(64):
        q = d // 16
        p[d] = d + 16 if q % 2 == 0 else d - 16
    return p


def _w_in_layouts(w_in):
    Lc = w_in.shape[0]
    oQ, oK, oV, oG, oZF, oZB, oAQ, oAK, oAV, oU = 0, 192, 384, 768, 1152, 1168, 1184, 1568, 1696, 1824
    fm = np.full(1920, -1, np.int64)
    for pr in range(2):
        for hh in range(2):
            h = 2 * pr + hh
            fm[pr * 128 + hh * 64: pr * 128 + hh * 64 + 48] = oQ + 48 * h + np.arange(48)
            fm[256 + pr * 128 + hh * 64: 256 + pr * 128 + hh * 64 + 48] = oK + 48 * h + np.arange(48)
    fm[512:528] = oZF + np.arange(16)
    fm[544:560] = oZB + np.arange(16)
    perm = _perm64()
    base = 640
    for m, (ha, hb) in enumerate([(0, 3), (1, 4), (2, 5)]):
        for s, h in enumerate((ha, hb)):
            fm[base + m * 128 + s * 64: base + m * 128 + (s + 1) * 64] = oAQ + 64 * h + np.arange(64)
            fm[base + (3 + m) * 128 + s * 64: base + (3 + m) * 128 + (s + 1) * 64] = oAQ + 64 * h + perm
    for s in range(2):
        fm[base + 768 + s * 64: base + 768 + (s + 1) * 64] = oAK + 64 * s + np.arange(64)
        fm[base + 896 + s * 64: base + 896 + (s + 1) * 64] = oAK + 64 * s + perm
    fm[1664:1920] = oU + np.arange(256)
    tm = np.full(1152, -1, np.int64)
    for h in range(4):
        tm[h * 64:h * 64 + 48] = oK + 48 * h + np.arange(48)
    tm[256:640] = oV + np.arange(384)
    tm[640:1024] = oG + np.arange(384)
    tm[1024:1152] = oAV + np.arange(128)

    def gather(idx):
        out = np.zeros((Lc, w_in.shape[1], idx.size), np.float32)
        sel = idx >= 0
        out[:, :, sel] = w_in[:, :, idx[sel]]
        return out
    return gather(fm), gather(tm)


def _constants():
    j = np.arange(128)[:, None]
    i = np.arange(128)[None, :]
    c = {}
    tri = np.zeros((128, 4, 128), np.float32)
    tri[:, 0, :] = (j <= i) * (-1.0 / 16)
    tri[:, 1, :] = (j >= i) * (-1.0 / 16)
    tri[:, 2, :] = (j > i) * (-1.0 / 16)
    tri[:, 3, :] = (j < i) * (-1.0 / 16)
    c["c_tri"] = tri
    msk = np.zeros((128, 2, 128), np.float32)
    msk[:, 0, :] = (j <= i)
    msk[:, 1, :] = (j >= i)
    c["c_mask"] = msk
    c["c_ident"] = np.eye(128, dtype=np.float32)
    rows = TLAT // 64
    row = np.repeat(np.arange(rows, dtype=np.float32), 64)
    col = np.tile(np.arange(64, dtype=np.float32), rows)
    inv = (10000.0 ** (-np.arange(16, dtype=np.float32) / 16)).astype(np.float32)
    ang = np.concatenate([row[:, None] * inv, row[:, None] * inv, col[:, None] * inv, col[:, None] * inv], axis=-1)
    sign = np.concatenate([-np.ones(16), np.ones(16), -np.ones(16), np.ones(16)]).astype(np.float32)
    cosT = np.cos(ang).T.astype(np.float32)
    sinT = (np.sin(ang) * sign[None, :]).T.astype(np.float32)
    rope = np.zeros((128, 2, TLAT), np.float32)
    rope[:, 0, :] = np.concatenate([cosT, cosT], 0)
    rope[:, 1, :] = np.concatenate([sinT, sinT], 0)
    c["c_rope"] = rope
    r = np.arange(128)
    c["c_bd32"] = (r[:, None] // 32 == r[None, :] // 32).astype(np.float32)
    cpp = np.zeros((128, 2), np.float32)
    for pp in range(2):
        cpp[:, pp] = ((r % 64) // 32 == pp)
    c["c_pp"] = cpp
    c["c_m"] = np.broadcast_to(np.arange(1, 289, dtype=np.float32)[None, :], (128, 288)).copy()
    return c


def _col_layout(v):
    m = v.shape[-1] // 128
    return np.ascontiguousarray(np.swapaxes(v.reshape(v.shape[:-1] + (m, 128)), -1, -2))


def _prep_shared(inp):
    f = lambda k: np.asarray(inp[k], np.float32)
    sh = {}
    sh["w_mod"] = f("w_mod")
    sh["bmod"] = _col_layout(f("b_mod"))
    nr = np.zeros((128, 9, KT), np.float32)
    n1, n2 = _col_layout(f("norm1_w")), _col_layout(f("norm2_w"))
    for l in range(L):
        nr[:, l, :] = n1[l]
        nr[:, 4 + l, :] = n2[l]
    nr[:, 8, :] = _col_layout(f("final_norm_w"))
    sh["nrm"] = nr
    sh["wfm"], sh["wtm"] = _w_in_layouts(f("w_in"))
    wa = np.zeros((L, 48, 512), np.float32)
    ba = np.zeros((L, 512), np.float32)
    for dr, (wk, bk) in enumerate((("gla_wa_f", "gla_ba_f"), ("gla_wa_b", "gla_ba_b"))):
        w_, b_ = f(wk), f(bk)
        for h in range(4):
            wa[:, 32 * dr:32 * dr + 16, dr * 256 + h * 64:dr * 256 + h * 64 + 48] = w_[:, :, h * 48:(h + 1) * 48]
            ba[:, dr * 256 + h * 64: dr * 256 + h * 64 + 48] = b_[:, h * 48:(h + 1) * 48]
    sh["wa"], sh["ba"] = wa, ba
    sh["gnw"] = np.tile(f("gla_norm_w"), (1, 4))
    sh["sink"] = f("attn_sink")
    sh["w_out"], sh["mlp_w1"], sh["mlp_w2"] = f("w_out"), f("mlp_w1"), f("mlp_w2")
    sh["glu_w"] = f("glu_w")
    sh["glub"] = _col_layout(f("glu_b"))
    lam = np.zeros((L, 2, 128, 2, 8), np.float32)
    stp = np.zeros((L, 2, 128, 8), np.float32)
    Bm = np.zeros((L, 2, 128, 2, 8, 16), np.float32)
    Cm = np.zeros((L, 2, 128, 2, 8, 16), np.float32)
    for dr, tg in enumerate(("f", "b")):
        lre, lim, ls = f("s5_lam_re_" + tg), f("s5_lam_im_" + tg), f("s5_log_step_" + tg)
        bre, bim, cre, cim = f("s5_b_re_" + tg), f("s5_b_im_" + tg), f("s5_c_re_" + tg), f("s5_c_im_" + tg)
        for g2 in range(2):
            ps_ = slice(64 * g2, 64 * g2 + 64)
            gsel = np.arange(8) * 2 + g2
            lam[:, dr, ps_, 0, :] = np.transpose(lre[:, gsel, :], (0, 2, 1))
            lam[:, dr, ps_, 1, :] = np.transpose(lim[:, gsel, :], (0, 2, 1))
            stp[:, dr, ps_, :] = ls[:, None, gsel]
            Bm[:, dr, ps_, 0] = np.transpose(bre[:, gsel], (0, 2, 1, 3))
            Bm[:, dr, ps_, 1] = np.transpose(bim[:, gsel], (0, 2, 1, 3))
            Cm[:, dr, ps_, 0] = np.transpose(cre[:, gsel], (0, 3, 1, 2))
            Cm[:, dr, ps_, 1] = np.transpose(cim[:, gsel], (0, 3, 1, 2))
    sh["s5lam"], sh["s5step"], sh["s5B"], sh["s5C"] = lam, stp, Bm, Cm
    sh["s5d"] = _col_layout(f("s5_d"))
    sh.update(_constants())
    return sh


def _prep_core(inp, core):
    x, ctx, c, c_ctx = (np.asarray(inp[k], np.float32) for k in ("x", "ctx", "c", "c_ctx"))
    xin = np.zeros((NBC, 128, KT, T), np.float32)
    cT = np.zeros((128, KT, 3), np.float32)
    for bb in range(NBC):
        b = core * NBC + bb
        seq = np.concatenate([ctx[b], x[b]], axis=0)
        xin[bb] = np.transpose(seq.T.reshape(KT, 128, T), (1, 0, 2))
        cT[:, :, bb] = c[b].reshape(KT, 128).T
    cT[:, :, 2] = c_ctx.reshape(KT, 128).T
    return {"xin": xin, "cT": cT}


_NC_CACHE = {}


def kernel(**inputs):
    n = 8
    shared = _prep_shared(inputs)
    in_maps = []
    for core in range(n):
        m = dict(shared)
        m.update(_prep_core(inputs, core))
        in_maps.append(m)
    if "nc" not in _NC_CACHE:
        _NC_CACHE["nc"] = build_program()
    res = run_bass_kernel_spmd(_NC_CACHE["nc"], in_maps, core_ids=list(range(n)))
    B = np.asarray(inputs["x"]).shape[0]
    out = np.zeros((B, TLAT, D), np.float32)
    for core in range(n):
        o = np.asarray(res.results[core]["out"])
        for bb in range(NBC):
            out[core * NBC + bb] = np.transpose(o[bb], (2, 1, 0)).reshape(TLAT, D)
    return out
```

```python
import numpy as np
import concourse.bass as bass
import concourse.mybir as mybir
from concourse.bass_utils import run_bass_kernel_spmd
from contextlib import ExitStack

F32 = mybir.dt.float32
BF16 = mybir.dt.bfloat16
I32 = mybir.dt.int32
U8 = mybir.dt.uint8
AF = mybir.ActivationFunctionType
ALU = mybir.AluOpType
AX = mybir.AxisListType

D = 1024
KT = 8
T = 2304
TCX = 256
TLAT = 2048
NCH = 18
L = 4
NBC = 2
BLKS = [(0, 256), (256, 512), (768, 512), (1280, 512), (1792, 512)]
EPS = 1e-6
NFM_G, NTM_G = 640, 1024
NFM_A, NTM_A = 1024, 128
NFM_S = 256
TWO_PI = 2.0 * np.pi


class Res:
    __slots__ = ("name", "w", "r", "excl")

    def __init__(self, name="", excl=False):
        self.name = name
        self.w = None
        self.r = {}
        self.excl = excl


class Rec:
    EPOCH = 30000
    SAME_ENGINE_SYNC = True

    def __init__(self, nc, es, n_dma_sems=12):
        self.nc = nc
        self.es = es
        self.prog = {k: [] for k in ("pe", "act", "dve", "pool", "sp")}
        self.sems = []
        self.csem = {}
        self.ccnt = {}
        self.waited = {k: {} for k in self.prog}
        for k in ("pe", "act", "dve", "pool"):
            self._new_csem(k)
        self.dsem = {}
        self.drr = {}
        for q in ("sp", "pool", "act"):
            self.dsem[q] = [[self._new_sem(f"d_{q}_{i}"), 0] for i in range(n_dma_sems)]
            self.drr[q] = 0
        self.n_ops = 0
        self.n_waits = 0

    def _new_sem(self, name):
        s = self.es.enter_context(self.nc.semaphore(name))
        self.sems.append(s)
        return len(self.sems) - 1

    def _new_csem(self, e):
        self.csem[e] = self._new_sem(f"c_{e}_{len(self.sems)}")
        self.ccnt[e] = 0

    def _waits(self, e, deps):
        wd = self.waited[e]
        best = {}
        for (si, val, src) in deps:
            if src == e and (e == "pe" or not self.SAME_ENGINE_SYNC):
                continue
            if wd.get(si, 0) >= val:
                continue
            if best.get(si, 0) < val:
                best[si] = val
        for si, val in best.items():
            wd[si] = val
            self.prog[e].append(("wait", si, val))
            self.n_waits += 1

    def _deps(self, reads, writes, e=None):
        deps = []
        for r in reads:
            if r.w is not None:
                deps.append(r.w)
            if r.excl:
                for si, (val, src) in r.r.items():
                    if src != e:
                        deps.append((si, val, src))
        for w in writes:
            if w.w is not None:
                deps.append(w.w)
            for si, (val, src) in w.r.items():
                deps.append((si, val, src))
        return deps

    def _mark(self, tok, reads, writes):
        si, val, src = tok
        for r in reads:
            r.r[si] = (val, src)
        for w in writes:
            w.w = tok
            w.r = {}

    def op(self, e, fn, reads=(), writes=(), sig=True):
        self._waits(e, self._deps(reads, writes, e))
        if not sig and self.ccnt[e] >= self.EPOCH - 1:
            sig = True
        if sig and self.ccnt[e] >= self.EPOCH:
            self._new_csem(e)
        si = self.csem[e]
        if sig:
            self.ccnt[e] += 1
            tok = (si, self.ccnt[e], e)
            self.prog[e].append(("op", fn, si, 1))
        else:
            tok = (si, self.ccnt[e] + 1, e)
            self.prog[e].append(("op", fn, None, 0))
        self._mark(tok, reads, writes)
        self.n_ops += 1
        return tok

    def dma(self, q, fn, reads=(), writes=()):
        deps = self._deps(reads, writes)
        slot = self.dsem[q][self.drr[q]]
        self.drr[q] = (self.drr[q] + 1) % len(self.dsem[q])
        if slot[1] > 0:
            deps.append((slot[0], slot[1], "dma"))
        self._waits(q, deps)
        slot[1] += 16
        tok = (slot[0], slot[1], "dma")
        self.prog[q].append(("op", fn, slot[0], 16))
        self._mark(tok, reads, writes)
        self.n_ops += 1
        return tok

    def all_tokens(self):
        toks = [(self.csem[e], self.ccnt[e], e) for e in ("pe", "act", "dve", "pool") if self.ccnt[e] > 0]
        for q in self.dsem:
            toks += [(s[0], s[1], "dma") for s in self.dsem[q] if s[1] > 0]
        return toks

    def barrier(self, force=False):
        if not force and not getattr(self, 'USE_BARRIERS', False):
            return
        toks = self.all_tokens()
        for e in self.prog:
            self._waits(e, [t for t in toks if not (t[2] == e and e == "pe")])

    def emit(self):
        nc = self.nc
        sems = self.sems
        prog = self.prog

        def replay(name, eng):
            for it in prog[name]:
                if it[0] == "wait":
                    eng.wait_ge(sems[it[1]], it[2])
                else:
                    ins = it[1](eng)
                    if it[3]:
                        ins.then_inc(sems[it[2]], it[3])

        with nc.Block() as block:
            @block.sync
            def _(e):
                replay("sp", e)

            @block.tensor
            def _(e):
                replay("pe", e)

            @block.scalar
            def _(e):
                replay("act", e)

            @block.vector
            def _(e):
                replay("dve", e)

            @block.gpsimd
            def _(e):
                replay("pool", e)


class Tl:
    def __init__(self, ap, name=""):
        self.ap = ap
        self.r = Res(name)

    def __getitem__(self, k):
        return self.ap[k]


class Arena:
    def __init__(self, nc, es, nbytes):
        self.t = es.enter_context(nc.sbuf_tensor("arena", [128, nbytes], U8))
        self.off = 0
        self.cap = nbytes
        self.peak = 0
        self.hist = []

    def alloc(self, shape, dtype, name=""):
        sz = mybir.dt.size(dtype)
        n = int(np.prod(shape))
        nb = (n * sz + 63) // 64 * 64
        assert self.off + nb <= self.cap, ("SBUF arena overflow", name, self.off, nb, self.cap)
        ap = self.t[:, self.off:self.off + n * sz].bitcast(dtype)
        lo, hi = self.off, self.off + nb
        inherit = {}
        keep = []
        for (a_, b_, r_) in self.hist:
            if a_ < hi and lo < b_:
                toks = dict(r_.r)
                if r_.w is not None:
                    si, val, src = r_.w
                    if toks.get(si, (0, None))[0] < val:
                        toks[si] = (val, src)
                for si, (val, src) in toks.items():
                    if inherit.get(si, (0, None))[0] < val:
                        inherit[si] = (val, src)
                if a_ >= lo and b_ <= hi:
                    continue
            keep.append((a_, b_, r_))
        self.hist = keep
        self.off += nb
        self.peak = max(self.peak, self.off)
        if len(shape) == 2:
            ap = ap.rearrange("p (a b) -> p a b", b=shape[1])
        elif len(shape) == 3:
            ap = ap.rearrange("p (a b c) -> p a b c", b=shape[1], c=shape[2])
        elif len(shape) == 4:
            ap = ap.rearrange("p (a b c d) -> p a b c d", b=shape[1], c=shape[2], d=shape[3])
        t = Tl(ap, name)
        t.r.r = inherit
        self.hist.append((lo, hi, t.r))
        return t

    def mark(self):
        return self.off

    def release(self, m):
        self.off = m


def build_program(n_layers=L, n_b=NBC, dbg=False, stages="ngaswm", prologue=True):
    nc = bass.Bass("TRN2", target_bir_lowering=False)

    def din(name, shape, dt=F32):
        return nc.dram_tensor(name, list(shape), dt, kind="ExternalInput").ap()

    xin = din("xin", [NBC, 128, KT, T])
    cT_d = din("cT", [128, KT, 3])
    wmod_d = din("w_mod", [L, D, 6 * D])
    bmod_d = din("bmod", [L, 128, 48])
    nrm_d = din("nrm", [128, 9, KT])
    wfm_d = din("wfm", [L, D, 1920])
    wtm_d = din("wtm", [L, D, 1152])
    wa_d = din("wa", [L, 48, 512])
    ba_d = din("ba", [L, 512])
    gnw_d = din("gnw", [L, 384])
    sink_d = din("sink", [L, 6])
    wout_d = din("w_out", [L, D, D])
    w1_d = din("mlp_w1", [L, D, 4 * D])
    w2_d = din("mlp_w2", [L, 4 * D, D])
    gluw_d = din("glu_w", [L, 256, 512])
    glub_d = din("glub", [L, 128, 4])
    s5lam_d = din("s5lam", [L, 2, 128, 2, 8])
    s5step_d = din("s5step", [L, 2, 128, 8])
    s5B_d = din("s5B", [L, 2, 128, 2, 8, 16])
    s5C_d = din("s5C", [L, 2, 128, 2, 8, 16])
    s5d_d = din("s5d", [L, 128, 2])
    ctri_d = din("c_tri", [128, 4, 128])
    cmask_d = din("c_mask", [128, 2, 128])
    cident_d = din("c_ident", [128, 128])
    crope_d = din("c_rope", [128, 2, 2048])
    cbd32_d = din("c_bd32", [128, 128])
    cpp_d = din("c_pp", [128, 2])
    cm_d = din("c_m", [128, 288])
    out_d = nc.dram_tensor("out", [NBC, 128, KT, TLAT], F32, kind="ExternalOutput").ap()
    h_scr = nc.dram_tensor("dbg_h" if dbg else "h_scr", [128, KT, T], BF16, kind="ExternalOutput" if dbg else "Internal").ap()
    m_scr = nc.dram_tensor("dbg_m" if dbg else "m_scr", [128, KT, T], BF16, kind="ExternalOutput" if dbg else "Internal").ap()
    g_scr = nc.dram_tensor("g_scr", [128, NCH, 384], BF16, kind="Internal").ap()
    of_scr = nc.dram_tensor("of_scr", [128, NCH, 384], F32, kind="Internal").ap()
    qb_scr = nc.dram_tensor("qb_scr", [128, 2, NCH, 256], BF16, kind="Internal").ap()
    ds_scr = nc.dram_tensor("ds_scr", [128, 2, NCH, 192], F32, kind="Internal").ap()
    if dbg:
        dbg_xa = nc.dram_tensor("dbg_xa", [128, KT, T], F32, kind="ExternalOutput").ap()
        dbg_xb = nc.dram_tensor("dbg_xb", [128, KT, T], F32, kind="ExternalOutput").ap()

    es = ExitStack()
    with es:
        R = Rec(nc, es)
        AR = Arena(nc, es, 196000)
        pst = [es.enter_context(nc.psum_tensor(f"ps{i}", [128, 512], F32)) for i in range(8)]
        psr = [Res(f"ps{i}", excl=True) for i in range(8)]
        psi = [0]

        def PS():
            k = psi[0]
            psi[0] = (k + 1) % 8
            return pst[k], psr[k]

        def mm(out, lhsT, rhs, start, stop, reads, wres, sig=None, **kw):
            if sig is None:
                sig = stop
            R.op("pe", lambda e: e.matmul(out, lhsT=lhsT, rhs=rhs, start=start, stop=stop, **kw),
                 reads=reads, writes=[wres], sig=sig)

        def act(out, in_, func, reads, writes, **kw):
            R.op("act", lambda e: e.activation(out=out, in_=in_, func=func, **kw), reads=reads, writes=writes)

        def tt(eng, out, in0, in1, op, reads, writes):
            R.op(eng, lambda e: e.tensor_tensor(out=out, in0=in0, in1=in1, op=op), reads=reads, writes=writes)

        def stt(eng, out, in0, scalar, in1, op0, op1, reads, writes):
            R.op(eng, lambda e: e.scalar_tensor_tensor(out=out, in0=in0, scalar=scalar, in1=in1, op0=op0, op1=op1),
                 reads=reads, writes=writes)

        def ts(eng, out, in0, s1, s2, op0, op1, reads, writes):
            if s2 is None:
                R.op(eng, lambda e: e.tensor_scalar(out=out, in0=in0, scalar1=s1, scalar2=None, op0=op0),
                     reads=reads, writes=writes)
            else:
                R.op(eng, lambda e: e.tensor_scalar(out=out, in0=in0, scalar1=s1, scalar2=s2, op0=op0, op1=op1),
                     reads=reads, writes=writes)

        def cp(eng, out, in_, reads, writes):
            if eng == "act":
                act(out, in_, AF.Copy, reads, writes)
            else:
                R.op(eng, lambda e: e.tensor_copy(out=out, in_=in_), reads=reads, writes=writes)

        def memset(eng, out, val, writes):
            R.op(eng, lambda e: e.memset(out, val), writes=writes)

        def dma(q, out, in_, reads, writes):
            R.dma(q, lambda e: e.dma_start(out=out, in_=in_), reads=reads, writes=writes)

        def recip(out, in_, reads, writes):
            R.op("dve", lambda e: e.reciprocal(out=out, in_=in_), reads=reads, writes=writes)

        def treduce(out, in_, reads, writes):
            R.op("dve", lambda e: e.tensor_reduce(out=out, in_=in_, axis=AX.X, op=ALU.add), reads=reads, writes=writes)

        def petr(out, in_, idn, reads, wres):
            R.op("pe", lambda e: e.transpose(out=out, in_=in_, identity=idn), reads=reads, writes=[wres])

        def tscan(out, d0, d1, reads, writes):
            R.op("dve", lambda e: e.tensor_tensor_scan(out=out, data0=d0, data1=d1, initial=0.0, op0=ALU.mult, op1=ALU.add),
                 reads=reads, writes=writes)

        x = AR.alloc([KT, T], F32, "x")
        xres = [Res(f"x{b}") for b in range(len(BLKS))]
        ident = AR.alloc([128], F32, "ident")
        identb = AR.alloc([128], BF16, "identb")
        onesb = AR.alloc([128], BF16, "onesb")
        maskb = AR.alloc([2, 128], BF16, "maskb")
        nrm = AR.alloc([9, KT], F32, "nrm")
        modt = AR.alloc([L, 48, 3], F32, "mod")
        A1 = AR.alloc([L, KT, 3], F32, "A1")
        A2 = AR.alloc([L, KT, 3], F32, "A2")
        dma("sp", ident[:], cident_d, [], [ident.r])
        dma("pool", identb[:], cident_d, [], [identb.r])
        dma("pool", maskb[:], cmask_d, [], [maskb.r])
        dma("sp", nrm[:], nrm_d, [], [nrm.r])
        memset("dve", onesb[:], 1.0, [onesb.r])

        mk = AR.mark()
        c32 = AR.alloc([KT, 3], F32, "c32")
        cact = AR.alloc([KT, 3], BF16, "cact")
        bmod = AR.alloc([L, 48], F32, "bmod")
        wm = [AR.alloc([KT, 768], BF16, f"wm{i}") for i in range(2)]
        dma("sp", c32[:], cT_d, [], [c32.r])
        dma("sp", bmod[:], bmod_d.rearrange("l p m -> p l m"), [], [bmod.r])
        act(cact[:], c32[:], AF.Silu, [c32.r], [cact.r])
        for l in range(n_layers if prologue else 0):
            ps, pr = PS()
            wv = wmod_d[l].rearrange("(kt p) c -> p kt c", p=128)
            for ch in range(8):
                w_ = wm[ch % 2]
                dma("pool", w_[:], wv[:, :, ch * 768:(ch + 1) * 768], [], [w_.r])
                for m in range(6):
                    col = (ch * 6 + m) * 3
                    for kt in range(KT):
                        mm(ps[:, col:col + 3], w_[:, kt, m * 128:(m + 1) * 128], cact[:, kt, :], kt == 0, kt == KT - 1,
                           [w_.r, cact.r], pr)
            tt("dve", modt[:, l], ps[:, 0:144].rearrange("p (m j) -> p m j", j=3),
               bmod[:, l, :].unsqueeze(2).to_broadcast([128, 48, 3]), ALU.add, [pr, bmod.r], [modt.r])
            stt("dve", A1[:, l], modt[:, l, 8:16, :], 1.0, nrm[:, l, :].unsqueeze(2).to_broadcast([128, KT, 3]),
                ALU.add, ALU.mult, [modt.r, nrm.r], [A1.r])
            stt("dve", A2[:, l], modt[:, l, 32:40, :], 1.0, nrm[:, 4 + l, :].unsqueeze(2).to_broadcast([128, KT, 3]),
                ALU.add, ALU.mult, [modt.r, nrm.r], [A2.r])
        R.barrier()
        AR.release(mk)

        def SH1(l, kt, j): return modt[:, l, 0 + kt, j:j + 1]
        def G1(l, kt, j): return modt[:, l, 16 + kt, j:j + 1]
        def SH2(l, kt, j): return modt[:, l, 24 + kt, j:j + 1]
        def G2(l, kt, j): return modt[:, l, 40 + kt, j:j + 1]

        def rms_block(bi, Asc, Bsh, hb, tmp_sq, rstd, tmpf):
            t0, N = BLKS[bi]
            xr = xres[bi]
            act(tmp_sq[:, :, 0:N], x[:, :, t0:t0 + N], AF.Square, [xr], [tmp_sq.r])
            ps, pr = PS()
            for kt in range(KT):
                mm(ps[:, 0:N], onesb[:], tmp_sq[:, kt, 0:N], kt == 0, kt == KT - 1, [onesb.r, tmp_sq.r], pr)
            act(rstd[:, 0:N], ps[:, 0:N], AF.Sqrt, [pr], [rstd.r], scale=1.0 / D, bias=EPS)
            recip(rstd[:, 0:N], rstd[:, 0:N], [rstd.r], [rstd.r])
            for kt in range(KT):
                tf = tmpf[kt % 2]
                tt("dve", tf[:, 0:N], x[:, kt, t0:t0 + N], rstd[:, 0:N], ALU.mult, [xr, rstd.r], [tf.r])
                act(hb[:, kt, 0:N], tf[:, 0:N], AF.Identity, [tf.r, modt.r, A1.r, A2.r], [hb.r],
                    scale=Asc(kt), bias=Bsh(kt))

        def load_w(tile, src_view, ncols, step=512):
            for c0 in range(0, ncols, step):
                c1 = min(ncols, c0 + step)
                dma("pool", tile[:, :, c0:c1], src_view[:, :, c0:c1], [], [tile.r])

        def stage_norm1(b, l):
            mk = AR.mark()
            hb = [AR.alloc([KT, 512], BF16, f"hb{i}") for i in range(2)]
            sq = AR.alloc([KT, 512], BF16, "sq")
            rstd = AR.alloc([512], F32, "rstd")
            tmpf = [AR.alloc([512], F32, f"tf{i}") for i in range(2)]
            for bi, (t0, N) in enumerate(BLKS):
                j = 2 if bi == 0 else b
                h_ = hb[bi % 2]
                rms_block(bi, lambda kt: A1[:, l, kt, j:j + 1], lambda kt: SH1(l, kt, j), h_, sq, rstd, tmpf)
                dma("sp", h_scr[:, :, t0:t0 + N], h_[:, :, 0:N], [h_.r], [hscr_r])
            R.barrier()
            AR.release(mk)

        hscr_r = Res("h_scr")
        mscr_r = Res("m_scr")

        gscr_r = Res("g_scr")

        def stage_gla(b, l):
            mk = AR.mark()
            qT = AR.alloc([2, T], BF16, "qT")
            kT = AR.alloc([2, T], BF16, "kT")
            ktok = AR.alloc([NCH, 256], BF16, "ktok")
            vtok = AR.alloc([NCH, 384], BF16, "vtok")
            la = AR.alloc([NCH, 2, 256], F32, "la")
            tri = AR.alloc([4, 128], F32, "tri")
            nwb = AR.alloc([384], F32, "nwb")
            dma("sp", tri[:], ctri_d, [], [tri.r])
            dma("sp", nwb[:], gnw_d[l].partition_broadcast(128), [], [nwb.r])
            mk2 = AR.mark()
            zT = AR.alloc([T], BF16, "zT")
            wa = AR.alloc([512], BF16, "wa")
            bab = AR.alloc([512], F32, "bab")
            dma("pool", wa[0:48, :], wa_d[l], [], [wa.r])
            dma("sp", bab[:], ba_d[l].partition_broadcast(128), [], [bab.r])
            hb = [AR.alloc([KT, 256], BF16, f"hb{i}") for i in range(2)]
            GBL = [(t_, 256) for t_ in range(0, T, 256)]
            mk3 = AR.mark()
            Wg = AR.alloc([KT, NFM_G], BF16, "Wgf")
            load_w(Wg, wfm_d[l].rearrange("(kt p) c -> p kt c", p=128)[:, :, 0:NFM_G], NFM_G, 640)
            for bi, (t0, N) in enumerate(GBL):
                h_ = hb[bi % 2]
                dma("sp", h_[:, :, 0:N], h_scr[:, :, t0:t0 + N], [hscr_r], [h_.r])
                for m in range(5):
                    ps, pr = PS()
                    for kt in range(KT):
                        mm(ps[:, 0:N], Wg[:, kt, m * 128:(m + 1) * 128], h_[:, kt, 0:N], kt == 0, kt == KT - 1,
                           [Wg.r, h_.r], pr)
                    if m < 2:
                        cp("act", qT[:, m, t0:t0 + N], ps[:, 0:N], [pr], [qT.r])
                    elif m < 4:
                        cp("dve", kT[:, m - 2, t0:t0 + N], ps[:, 0:N], [pr], [kT.r])
                    else:
                        cp("act", zT[:, t0:t0 + N], ps[:, 0:N], [pr], [zT.r])
            R.barrier()
            AR.release(mk3)
            Wg = AR.alloc([KT, 512], BF16, "Wgt")
            tg = AR.alloc([384], F32, "tg")
            tl = AR.alloc([512], F32, "tl")
            gst = [AR.alloc([384], BF16, f"gst{i}") for i in range(2)]
            wtv = wtm_d[l].rearrange("(kt p) c -> p kt c", p=128)
            for half in range(2):
                dma("pool", Wg[:], wtv[:, :, half * 512:(half + 1) * 512], [], [Wg.r])
                for bi, (t0, N) in enumerate(GBL):
                    h_ = hb[bi % 2]
                    dma("sp", h_[:, :, 0:N], h_scr[:, :, t0:t0 + N], [hscr_r], [h_.r])
                    for cc in range(N // 128):
                        ch = t0 // 128 + cc
                        tok = slice(cc * 128, (cc + 1) * 128)
                        psA, prA = PS()
                        for kt in range(KT):
                            mm(psA[:, :], h_[:, kt, tok], Wg[:, kt, :], kt == 0, kt == KT - 1, [Wg.r, h_.r], prA)
                        if half == 0:
                            cp("dve", ktok[:, ch, :], psA[:, 0:256], [prA], [ktok.r])
                            cp("act", vtok[:, ch, 0:256], psA[:, 256:512], [prA], [vtok.r])
                            psL, prL = PS()
                            gt = slice(t0 + cc * 128, t0 + (cc + 1) * 128)
                            mm(psL[:, :], zT[0:48, gt], wa[0:48, :], True, True, [zT.r, wa.r], prL)
                            tt("dve", tl[:], psL[:, :], bab[:], ALU.add, [prL, bab.r], [tl.r])
                            act(tl[:], tl[:], AF.Exp, [tl.r], [tl.r], scale=-1.0)
                            act(la[:, ch].rearrange("p a b -> p (a b)"), tl[:], AF.Ln, [tl.r], [la.r], bias=1.0)
                        else:
                            cp("dve", vtok[:, ch, 256:384], psA[:, 0:128], [prA], [vtok.r])
                            act(tg[:], psA[:, 128:512], AF.Silu, [prA], [tg.r])
                            g_ = gst[ch % 2]
                            tt("dve", g_[:], tg[:], nwb[:], ALU.mult, [tg.r, nwb.r], [g_.r])
                            dma("sp", g_scr[:, ch, :], g_[:], [g_.r], [gscr_r])
            R.barrier()
            AR.release(mk2)
            order = [list(range(NCH)), [1, 0] + list(range(17, 1, -1))]
            step_of = [{ch: s_ for s_, ch in enumerate(order[d_])} for d_ in range(2)]
            QS = 48 ** -0.5
            ofscr_r = Res("of_scr")
            qbscr_r = Res("qb_scr")
            dsscr_r = Res("ds_scr")
            decs = AR.alloc([2, NCH, 2], F32, "decs")
            mkA = AR.mark()
            E1 = [AR.alloc([2, 128], F32, f"E1{d}") for d in range(4)]
            E2 = [AR.alloc([2, 128], F32, f"E2{d}") for d in range(4)]
            Ek = [AR.alloc([256], F32, f"Ek{d}") for d in range(4)]
            qb = [AR.alloc([2, 128], BF16, f"qb{d}") for d in range(6)]
            kb = [AR.alloc([2, 128], BF16, f"kb{d}") for d in range(6)]
            kd = [AR.alloc([256], BF16, f"kd{d}") for d in range(6)]
            scm = [AR.alloc([4, 128], BF16, f"scm{d}") for d in range(4)]
            dst = [AR.alloc([2, 2, 96], F32, f"dst{d}") for d in range(2)]
            ofs = [AR.alloc([384], F32, f"ofs{i}") for i in range(2)]

            def front(ch):
                tok = slice(ch * 128, (ch + 1) * 128)
                for dr in range(2):
                    k_ = (ch % 2) * 2 + dr
                    k3 = (ch % 3) * 2 + dr
                    s_ = step_of[dr][ch]
                    last = 127 if dr == 0 else 0
                    psb, prb = PS()
                    for p in range(2):
                        mm(psb[:, p * 128:(p + 1) * 128], la[:, ch, dr, p * 128:(p + 1) * 128], tri[:, dr, :], True, True,
                           [la.r, tri.r], prb)
                    mm(psb[:, 256:512], tri[:, 2 + dr, :], la[:, ch, dr, :], True, True, [la.r, tri.r], prb)
                    pb3 = psb[:, 0:256].rearrange("p (a b) -> p a b", b=128)
                    act(E1[k_][:], pb3, AF.Exp, [prb], [E1[k_].r])
                    act(E2[k_][:], pb3, AF.Exp, [prb], [E2[k_].r], scale=-1.0)
                    act(Ek[k_][:], psb[:, 256:512], AF.Exp, [prb], [Ek[k_].r])
                    act(decs[:, dr, s_, :], pb3[:, :, last], AF.Exp, [prb], [decs.r])
                    stt("dve", qb[k3][:], qT[:, :, tok], QS, E1[k_][:], ALU.mult, ALU.mult, [qT.r, E1[k_].r], [qb[k3].r])
                    tt("dve", kb[k3][:], kT[:, :, tok], E2[k_][:], ALU.mult, [kT.r, E2[k_].r], [kb[k3].r])
                    tt("dve", kd[k3][:], ktok[:, ch, :], Ek[k_][:], ALU.mult, [ktok.r, Ek[k_].r], [kd[k3].r])
                    dma("sp", qb_scr[:, dr, ch, :], qb[k3][:].rearrange("p a b -> p (a b)"), [qb[k3].r], [qbscr_r])

            def back1(ch):
                banks = []
                psd, prd = PS()
                for dr in range(2):
                    k3 = (ch % 3) * 2 + dr
                    pssE, prsE = PS()
                    pssO, prsO = PS()
                    for h in range(4):
                        p, base = h // 2, 64 * (h % 2)
                        pss_, prs_ = (pssE, prsE) if h % 2 == 0 else (pssO, prsO)
                        mm(pss_[:, p * 128:(p + 1) * 128], kb[k3][base:base + 64, p, :], qb[k3][base:base + 64, p, :],
                           True, True, [kb[k3].r, qb[k3].r], prs_)
                    banks.append(((pssE, prsE), (pssO, prsO)))
                    for h in range(4):
                        p, base = h // 2, 64 * (h % 2)
                        c0 = (dr * 2 + p) * 96
                        mm(psd[base:base + 64, c0:c0 + 96], kd[k3][:, h * 64:(h + 1) * 64],
                           vtok[:, ch, h * 96:(h + 1) * 96], True, True, [kd[k3].r, vtok.r], prd)
                d_ = dst[ch % 2]
                cp("act", d_[:], psd[:, 0:384].rearrange("p (d a b) -> p d a b", d=2, b=96), [prd], [d_.r])
                for dr in range(2):
                    dma("sp", ds_scr[:, dr, step_of[dr][ch], :], d_[:, dr].rearrange("p a b -> p (a b)"), [d_.r], [dsscr_r])
                for dr in range(2):
                    k_ = (ch % 2) * 2 + dr
                    scm4 = scm[k_][:].rearrange("p (a e) b -> p a e b", e=2)
                    for e_, (pss_, prs_) in enumerate(banks[dr]):
                        tt("dve", scm4[:, :, e_, :], pss_[:, 0:256].rearrange("p (a b) -> p a b", b=128),
                           maskb[:, dr, :].unsqueeze(1).to_broadcast([128, 2, 128]), ALU.mult, [prs_, maskb.r], [scm[k_].r])

            def back2(ch):
                pso, pro = PS()
                n_ = 0
                for dr in range(2):
                    k_ = (ch % 2) * 2 + dr
                    for h in range(4):
                        mm(pso[:, h * 96:(h + 1) * 96], scm[k_][:, h, :], vtok[:, ch, h * 96:(h + 1) * 96], n_ == 0, n_ == 7,
                           [scm[k_].r, vtok.r], pro, sig=(n_ == 7))
                        n_ += 1
                os_ = ofs[ch % 2]
                cp("act", os_[:], pso[:, 0:384], [pro], [os_.r])
                dma("sp", of_scr[:, ch, :], os_[:], [os_.r], [ofscr_r])

            for c_ in range(NCH + 2):
                if c_ < NCH:
                    front(c_)
                if 1 <= c_ <= NCH:
                    back1(c_ - 1)
                if c_ >= 2:
                    back2(c_ - 2)
            R.barrier()
            AR.release(mk)
            mkB = AR.mark()
            decs2 = AR.alloc([2, NCH, 2], F32, "decs2")
            dS = AR.alloc([2, NCH, 192], F32, "dS")
            Sst = AR.alloc([2, NCH, 192], BF16, "Sst")
            S32 = AR.alloc([2, 2, 96], F32, "S32")
            St = AR.alloc([2, 2, 96], F32, "St")
            cp("dve", decs2[:], decs[:], [decs.r], [decs2.r])
            dma("sp", dS[:], ds_scr, [dsscr_r], [dS.r])
            memset("dve", S32[:], 0.0, [S32.r])
            for s_ in range(NCH):
                cp("act", Sst[:, :, s_, :], S32[:].rearrange("p d a b -> p d (a b)"), [S32.r], [Sst.r])
                if s_ < NCH - 1:
                    tt("dve", St[:], S32[:], decs2[:, :, s_, :].unsqueeze(3).to_broadcast([128, 2, 2, 96]), ALU.mult,
                       [S32.r, decs2.r], [St.r])
                    tt("dve", S32[:], St[:], dS[:, :, s_, :].rearrange("p d (a b) -> p d a b", b=96), ALU.add,
                       [St.r, dS.r], [S32.r])
            G = 3
            ol = [AR.alloc([384], F32, f"ol{i}") for i in range(G)]
            gsb = [AR.alloc([384], BF16, f"gsb{i}") for i in range(G)]
            qbl = [AR.alloc([2, 2, 128], BF16, f"qbl{i}") for i in range(G)]
            osum = [AR.alloc([384], F32, f"osum{i}") for i in range(G)]
            osq = [AR.alloc([384], F32, f"osq{i}") for i in range(G)]
            ss = [AR.alloc([4], F32, f"ss{i}") for i in range(G)]
            otm = [AR.alloc([384], BF16, f"otm{i}") for i in range(G)]
            mst = [AR.alloc([3, 128], BF16, f"mst{i}") for i in range(G)]
            for g0 in range(0, NCH, G):
                chs = list(range(g0, g0 + G))
                for i_, ch in enumerate(chs):
                    dma("sp", qbl[i_][:], qb_scr[:, :, ch, :].rearrange("p d (a b) -> p d a b", b=128), [qbscr_r], [qbl[i_].r])
                    dma("sp", ol[i_][:], of_scr[:, ch, :], [ofscr_r], [ol[i_].r])
                    dma("sp", gsb[i_][:], g_scr[:, ch, :], [gscr_r], [gsb[i_].r])
                pbanks = []
                for i_, ch in enumerate(chs):
                    psE, prE = PS()
                    psO, prO = PS()
                    for e_, (ps_, pr_) in enumerate(((psE, prE), (psO, prO))):
                        n_ = 0
                        for dr in range(2):
                            for p in range(2):
                                base = 64 * e_
                                mm(ps_[:, p * 96:(p + 1) * 96], qbl[i_][base:base + 64, dr, p, :],
                                   Sst[base:base + 64, dr, step_of[dr][ch], p * 96:(p + 1) * 96], n_ == 0, n_ == 3,
                                   [qbl[i_].r, Sst.r], pr_, sig=(n_ == 3))
                                n_ += 1
                    pbanks.append(((psE, prE), (psO, prO)))
                for i_, ch in enumerate(chs):
                    o4 = osum[i_][:].rearrange("p (a e b) -> p a e b", e=2, b=96)
                    l4 = ol[i_][:].rearrange("p (a e b) -> p a e b", e=2, b=96)
                    for e_, (ps_, pr_) in enumerate(pbanks[i_]):
                        tt("dve", o4[:, :, e_, :], l4[:, :, e_, :], ps_[:, 0:192].rearrange("p (a b) -> p a b", b=96), ALU.add,
                           [ol[i_].r, pr_], [osum[i_].r])
                for i_, ch in enumerate(chs):
                    act(osq[i_][:], osum[i_][:], AF.Square, [osum[i_].r], [osq[i_].r])
                for i_, ch in enumerate(chs):
                    treduce(ss[i_][:], osq[i_][:].rearrange("p (a b) -> p a b", b=96), [osq[i_].r], [ss[i_].r])
                for i_, ch in enumerate(chs):
                    act(ss[i_][:], ss[i_][:], AF.Sqrt, [ss[i_].r], [ss[i_].r], scale=1.0 / 96, bias=EPS)
                for i_, ch in enumerate(chs):
                    recip(ss[i_][:], ss[i_][:], [ss[i_].r], [ss[i_].r])
                for i_, ch in enumerate(chs):
                    o3 = osum[i_][:].rearrange("p (a b) -> p a b", b=96)
                    tt("dve", o3, o3, ss[i_][:].unsqueeze(2).to_broadcast([128, 4, 96]), ALU.mult, [osum[i_].r, ss[i_].r], [osum[i_].r])
                for i_, ch in enumerate(chs):
                    tt("dve", otm[i_][:], osum[i_][:], gsb[i_][:], ALU.mult, [osum[i_].r, gsb[i_].r], [otm[i_].r])
                tb = []
                for i_, ch in enumerate(chs):
                    pst_, prt = PS()
                    pstb = pst_[:, :].bitcast(BF16)
                    for k in range(3):
                        petr(pstb[:, k * 128:(k + 1) * 128], otm[i_][:, k * 128:(k + 1) * 128], identb[:], [otm[i_].r, identb.r], prt)
                    tb.append((pstb, prt))
                for i_, ch in enumerate(chs):
                    pstb, prt = tb[i_]
                    cp("act", mst[i_][:], pstb[:, 0:384].rearrange("p (a b) -> p a b", b=128), [prt], [mst[i_].r])
                    dma("sp", m_scr[:, 0:3, ch * 128:(ch + 1) * 128], mst[i_][:], [mst[i_].r], [mscr_r])
            R.barrier()
            AR.release(mkB)

        def stage_att(b, l):
            mk = AR.mark()
            qr = AR.alloc([3, T], BF16, "qr")
            kr = AR.alloc([T], BF16, "kr")
            vaug = AR.alloc([NCH, 2, 66], BF16, "vaug")
            esk = AR.alloc([6], F32, "esk")
            memset("dve", vaug[:], 1.0, [vaug.r])
            dma("sp", esk[:], sink_d[l].partition_broadcast(128), [], [esk.r])
            act(esk[:], esk[:], AF.Exp, [esk.r], [esk.r])
            mk2 = AR.mark()
            Wa = AR.alloc([KT, NFM_A + NTM_A], BF16, "Wa")
            rope = AR.alloc([2, 2048], F32, "rope")
            hb = [AR.alloc([KT, 512], BF16, f"hb{i}") for i in range(2)]
            t1 = AR.alloc([512], F32, "t1")
            t2 = AR.alloc([512], F32, "t2")
            dma("sp", rope[:], crope_d, [], [rope.r])
            load_w(Wa, wfm_d[l].rearrange("(kt p) c -> p kt c", p=128)[:, :, NFM_G:NFM_G + NFM_A], NFM_A)
            dma("pool", Wa[:, :, NFM_A:NFM_A + NTM_A], wtm_d[l].rearrange("(kt p) c -> p kt c", p=128)[:, :, NTM_G:NTM_G + NTM_A],
                [], [Wa.r])
            for bi, (t0, N) in enumerate(BLKS):
                h_ = hb[bi % 2]
                dma("sp", h_[:, :, 0:N], h_scr[:, :, t0:t0 + N], [hscr_r], [h_.r])
                for m in range(4):
                    dst = qr[:, m, t0:t0 + N] if m < 3 else kr[:, t0:t0 + N]
                    dres = qr.r if m < 3 else kr.r
                    wc = m * 128 if m < 3 else 768
                    wcp = (3 + m) * 128 if m < 3 else 896
                    ps, pr = PS()
                    for kt in range(KT):
                        mm(ps[:, 0:N], Wa[:, kt, wc:wc + 128], h_[:, kt, 0:N], kt == 0, kt == KT - 1, [Wa.r, h_.r], pr)
                    if bi == 0:
                        cp("act", dst, ps[:, 0:N], [pr], [dres])
                    else:
                        ps2, pr2 = PS()
                        for kt in range(KT):
                            mm(ps2[:, 0:N], Wa[:, kt, wcp:wcp + 128], h_[:, kt, 0:N], kt == 0, kt == KT - 1, [Wa.r, h_.r], pr2)
                        tl0 = t0 - TCX
                        tt("dve", t1[:, 0:N], ps[:, 0:N], rope[:, 0, tl0:tl0 + N], ALU.mult, [pr, rope.r], [t1.r])
                        tt("dve", t2[:, 0:N], ps2[:, 0:N], rope[:, 1, tl0:tl0 + N], ALU.mult, [pr2, rope.r], [t2.r])
                        tt("pool", dst, t1[:, 0:N], t2[:, 0:N], ALU.add, [t1.r, t2.r], [dres])
                for cc in range(N // 128):
                    ch = t0 // 128 + cc
                    ps, pr = PS()
                    for kt in range(KT):
                        mm(ps[:, 0:128], h_[:, kt, cc * 128:(cc + 1) * 128], Wa[:, kt, NFM_A:NFM_A + 128], kt == 0, kt == KT - 1,
                           [Wa.r, h_.r], pr)
                    cp("act", vaug[:, ch, :, 0:64], ps[:, 0:128].rearrange("p (a b) -> p a b", b=64), [pr], [vaug.r])
            R.barrier()
            AR.release(mk2)
            NPT = 12
            pt = [AR.alloc([384], BF16, f"pt{i}") for i in range(NPT)]
            den = [AR.alloc([2, 3], F32, f"den{i}") for i in range(2)]
            otm = [AR.alloc([384], BF16, f"otm{i}") for i in range(2)]
            mst = [AR.alloc([3, 128], BF16, f"mst{i}") for i in range(2)]
            pti = [0]

            def keys_of(qc):
                keys = [(0, None), (1, None)]
                if qc >= 2:
                    if qc - 1 >= 2:
                        keys.append((qc - 1, 1))
                    keys.append((qc, None))
                    if qc + 1 <= NCH - 1:
                        keys.append((qc + 1, 0))
                return keys

            def att_scores(qc, kv):
                qtok = slice(qc * 128, (qc + 1) * 128)
                base = 64 * kv
                out = []
                for (kc, mk_) in keys_of(qc):
                    ktok_ = slice(kc * 128, (kc + 1) * 128)
                    pss, prs = PS()
                    for hh in range(3):
                        mm(pss[:, hh * 128:(hh + 1) * 128], kr[base:base + 64, ktok_], qr[base:base + 64, hh, qtok],
                           True, True, [kr.r, qr.r], prs)
                    p_ = pt[pti[0] % NPT]
                    pti[0] += 1
                    act(p_[:], pss[:, 0:384], AF.Exp, [prs], [p_.r], scale=0.125)
                    if mk_ is not None:
                        tt("dve", p_[:].rearrange("p (a b) -> p a b", b=128), p_[:].rearrange("p (a b) -> p a b", b=128),
                           maskb[:, mk_, :].unsqueeze(1).to_broadcast([128, 3, 128]), ALU.mult, [p_.r, maskb.r], [p_.r])
                    out.append((kc, p_))
                return out

            def att_pv(qc, kv, plist):
                o_ = otm[qc % 2]
                d_ = den[qc % 2]
                pso, pro = PS()
                for ki, (kc, p_) in enumerate(plist):
                    for hh in range(3):
                        mm(pso[:, hh * 65:(hh + 1) * 65], p_[:, hh * 128:(hh + 1) * 128], vaug[:, kc, kv, 0:65],
                           ki == 0 and hh == 0, ki == len(plist) - 1 and hh == 2, [p_.r, vaug.r], pro, sig=(hh == 2))
                po3 = pso[:, 0:195].rearrange("p (a b) -> p a b", b=65)
                tt("dve", d_[:, kv, :], po3[:, :, 64], esk[:, 3 * kv:3 * kv + 3], ALU.add, [pro, esk.r], [d_.r])
                recip(d_[:, kv, :], d_[:, kv, :], [d_.r], [d_.r])
                tt("dve", o_[:, kv * 192:(kv + 1) * 192].rearrange("p (a b) -> p a b", b=64), po3[:, :, 0:64],
                   d_[:, kv, :].unsqueeze(2).to_broadcast([128, 3, 64]), ALU.mult, [pro, d_.r], [o_.r])

            def att_fin(qc):
                o_ = otm[qc % 2]
                m_ = mst[qc % 2]
                pst_, prt = PS()
                pstb = pst_[:, :].bitcast(BF16)
                for k in range(3):
                    petr(pstb[:, k * 128:(k + 1) * 128], o_[:, k * 128:(k + 1) * 128], identb[:], [o_.r, identb.r], prt)
                cp("act", m_[:], pstb[:, 0:384].rearrange("p (a b) -> p a b", b=128), [prt], [m_.r])
                dma("sp", m_scr[:, 3:6, qc * 128:(qc + 1) * 128], m_[:], [m_.r], [mscr_r])

            units = [(qc, kv) for qc in range(NCH) for kv in range(2)]
            pend = att_scores(*units[0])
            for ui, (qc, kv) in enumerate(units):
                nxt = att_scores(*units[ui + 1]) if ui + 1 < len(units) else None
                att_pv(qc, kv, pend)
                if kv == 1:
                    att_fin(qc)
                pend = nxt
            R.barrier()
            AR.release(mk)

        def sincos(src, K_, F_, dst_s, dst_c):
            cp("dve", K_[:], src[:], [src.r], [K_.r])
            cp("dve", F_[:], K_[:], [K_.r], [F_.r])
            tt("dve", F_[:], src[:], F_[:], ALU.subtract, [src.r, F_.r], [F_.r])
            ts("dve", src[:], F_[:], 0.49999, -0.49999, ALU.min, ALU.max, [F_.r], [src.r])
            act(dst_s[:], src[:], AF.Sin, [src.r], [dst_s.r], scale=TWO_PI)
            ts("dve", F_[:], F_[:], 0.25, None, ALU.add, None, [F_.r], [F_.r])
            cp("dve", K_[:], F_[:], [F_.r], [K_.r])
            cp("dve", src[:], K_[:], [K_.r, dst_s.r], [src.r])
            tt("dve", F_[:], F_[:], src[:], ALU.subtract, [F_.r, src.r], [F_.r])
            ts("dve", F_[:], F_[:], 0.49999, -0.49999, ALU.min, ALU.max, [F_.r], [F_.r])
            act(dst_c[:], F_[:], AF.Sin, [F_.r], [dst_c.r], scale=TWO_PI)

        def stage_s5(b, l):
            NS = 288
            mk = AR.mark()
            uJ = AR.alloc([2, 8, 288], BF16, "uJ")
            yT = AR.alloc([2, T], BF16, "yT")
            mk3 = AR.mark()
            Ws = AR.alloc([KT, NFM_S], BF16, "Ws")
            hb = [AR.alloc([KT, 512], BF16, f"hb{i}") for i in range(2)]
            load_w(Ws, wfm_d[l].rearrange("(kt p) c -> p kt c", p=128)[:, :, NFM_G + NFM_A:NFM_G + NFM_A + NFM_S], NFM_S)
            for bi, (t0, N) in enumerate(BLKS):
                h_ = hb[bi % 2]
                dma("sp", h_[:, :, 0:N], h_scr[:, :, t0:t0 + N], [hscr_r], [h_.r])
                for ct in range(2):
                    ps, pr = PS()
                    for kt in range(KT):
                        mm(ps[:, 0:N], Ws[:, kt, ct * 128:(ct + 1) * 128], h_[:, kt, 0:N], kt == 0, kt == KT - 1, [Ws.r, h_.r], pr)
                    cp("act" if ct == 0 else "dve", uJ[:, ct, :, t0 // 8:(t0 + N) // 8].rearrange("p j n -> p n j"),
                       ps[:, 0:N].rearrange("p (n j) -> p n j", j=8), [pr], [uJ.r])
            R.barrier()
            AR.release(mk3)

            lam = AR.alloc([2, 2, 8], F32, "lam")
            dtt = AR.alloc([2, 8], F32, "dt")
            Bp = AR.alloc([2, 2, 8, 16], F32, "Bp")
            Cp = AR.alloc([2, 2, 8, 16], F32, "Cp")
            dsk = AR.alloc([2], F32, "dsk")
            bd32 = AR.alloc([128], F32, "bd32")
            cpp = AR.alloc([2], F32, "cpp")
            mI = AR.alloc([NS], F32, "mI")
            r8 = AR.alloc([2, 8], F32, "r8")
            phi = AR.alloc([2, 8], F32, "phi")
            dma("sp", lam[:], s5lam_d[l].rearrange("d p c g -> p d c g"), [], [lam.r])
            dma("sp", dtt[:], s5step_d[l].rearrange("d p g -> p d g"), [], [dtt.r])
            dma("sp", Bp[:], s5B_d[l].rearrange("d p c g h -> p d c g h"), [], [Bp.r])
            dma("sp", Cp[:], s5C_d[l].rearrange("d p c g h -> p d c g h"), [], [Cp.r])
            dma("sp", dsk[:], s5d_d[l], [], [dsk.r])
            dma("sp", bd32[:], cbd32_d, [], [bd32.r])
            dma("sp", cpp[:], cpp_d, [], [cpp.r])
            dma("sp", mI[:], cm_d, [], [mI.r])
            ldt = AR.alloc([2, 8], F32, "ldt")
            wdt = AR.alloc([2, 8], F32, "wdt")
            arp = AR.alloc([2, 9, 8], F32, "arp")
            aip = AR.alloc([2, 9, 8], F32, "aip")
            ang = AR.alloc([2, 9, 8], F32, "ang")
            ki = AR.alloc([2, 9, 8], I32, "ki")
            kf = AR.alloc([2, 9, 8], F32, "kf")
            mag = AR.alloc([2, 9, 8], F32, "mag")
            tA = AR.alloc([2, 8], F32, "tA")
            tB = AR.alloc([2, 8], F32, "tB")
            tC = AR.alloc([2, 8], F32, "tC")
            fr = AR.alloc([2, 8], F32, "fr")
            fi = AR.alloc([2, 8], F32, "fi")
            Bbr = AR.alloc([2, 8, 16], F32, "Bbr")
            Bbi = AR.alloc([2, 8, 16], F32, "Bbi")
            u1 = AR.alloc([8, 16], F32, "u1")
            u2 = AR.alloc([8, 16], F32, "u2")
            u3 = AR.alloc([8, 16], F32, "u3")
            tbd = AR.alloc([128], F32, "tbd")
            act(dtt[:], dtt[:], AF.Exp, [dtt.r], [dtt.r])
            ts("dve", lam[:, :, 0, :], lam[:, :, 0, :], -1e-4, None, ALU.min, None, [lam.r], [lam.r])
            tt("dve", ldt[:], lam[:, :, 0, :], dtt[:], ALU.mult, [lam.r, dtt.r], [ldt.r])
            tt("dve", wdt[:], lam[:, :, 1, :], dtt[:], ALU.mult, [lam.r, dtt.r], [wdt.r])
            for tau in range(9):
                act(mag[:, :, tau, :], ldt[:], AF.Exp, [ldt.r], [mag.r], scale=float(tau))
                ts("dve", ang[:, :, tau, :], wdt[:], float(tau) / TWO_PI, None, ALU.mult, None, [wdt.r], [ang.r])
            sincos(ang, ki, kf, aip, arp)
            tt("dve", arp[:], arp[:], mag[:], ALU.mult, [arp.r, mag.r], [arp.r])
            tt("dve", aip[:], aip[:], mag[:], ALU.mult, [aip.r, mag.r], [aip.r])
            cp("dve", r8[:], mag[:, :, 8, :], [mag.r], [r8.r])
            ts("dve", phi[:], wdt[:], 8.0 / TWO_PI, None, ALU.mult, None, [wdt.r], [phi.r])
            cp("dve", ki[:, :, 0, :], phi[:], [phi.r], [ki.r])
            cp("dve", kf[:, :, 0, :], ki[:, :, 0, :], [ki.r], [kf.r])
            tt("dve", phi[:], phi[:], kf[:, :, 0, :], ALU.subtract, [phi.r, kf.r], [phi.r])
            lr_, li_ = lam[:, :, 0, :], lam[:, :, 1, :]
            tt("dve", tA[:], lr_, lr_, ALU.mult, [lam.r], [tA.r])
            tt("dve", tB[:], li_, li_, ALU.mult, [lam.r], [tB.r])
            tt("dve", tA[:], tA[:], tB[:], ALU.add, [tA.r, tB.r], [tA.r])
            recip(tA[:], tA[:], [tA.r], [tA.r])
            ts("dve", tB[:], arp[:, :, 1, :], -1.0, None, ALU.add, None, [arp.r], [tB.r])
            tt("dve", fr[:], tB[:], lr_, ALU.mult, [tB.r, lam.r], [fr.r])
            tt("dve", tC[:], aip[:, :, 1, :], li_, ALU.mult, [aip.r, lam.r], [tC.r])
            tt("dve", fr[:], fr[:], tC[:], ALU.add, [fr.r, tC.r], [fr.r])
            tt("dve", fr[:], fr[:], tA[:], ALU.mult, [fr.r, tA.r], [fr.r])
            tt("dve", fi[:], aip[:, :, 1, :], lr_, ALU.mult, [aip.r, lam.r], [fi.r])
            tt("dve", tC[:], tB[:], li_, ALU.mult, [tB.r, lam.r], [tC.r])
            tt("dve", fi[:], fi[:], tC[:], ALU.subtract, [fi.r, tC.r], [fi.r])
            tt("dve", fi[:], fi[:], tA[:], ALU.mult, [fi.r, tA.r], [fi.r])

            def bc16(ap2):
                return ap2.unsqueeze(2).to_broadcast([128, 8, 16])

            for dr in range(2):
                tt("dve", u1[:], Bp[:, dr, 0], bc16(fr[:, dr, :]), ALU.mult, [Bp.r, fr.r], [u1.r])
                tt("dve", u2[:], Bp[:, dr, 1], bc16(fi[:, dr, :]), ALU.mult, [Bp.r, fi.r], [u2.r])
                tt("dve", Bbr[:, dr], u1[:], u2[:], ALU.subtract, [u1.r, u2.r], [Bbr.r])
                tt("dve", u1[:], Bp[:, dr, 1], bc16(fr[:, dr, :]), ALU.mult, [Bp.r, fr.r], [u1.r])
                tt("dve", u2[:], Bp[:, dr, 0], bc16(fi[:, dr, :]), ALU.mult, [Bp.r, fi.r], [u2.r])
                tt("dve", Bbi[:, dr], u1[:], u2[:], ALU.add, [u1.r, u2.r], [Bbi.r])

            for ct in range(2):
                mkc = AR.mark()
                pq = slice(ct * 4, ct * 4 + 4)
                BDT = AR.alloc([2, 8, 128], BF16, "BDT")
                CmI = AR.alloc([4, 32, 32], BF16, "CmI")
                Dst = AR.alloc([16, NS], F32, "Dst")
                memset("dve", CmI[:], 0.0, [CmI.r])
                mkb = AR.mark()
                BmJ = AR.alloc([64, 128], BF16, "BmJ")
                CAMa = AR.alloc([9, 2, 128], F32, "CAMa")
                BbM = AR.alloc([2, 2, 128], F32, "BbM")
                WMa = AR.alloc([8, 2, 128], F32, "WMa")
                V1 = AR.alloc([9, 4, 16], F32, "V1")
                V2 = AR.alloc([9, 4, 16], F32, "V2")
                V3 = AR.alloc([9, 4, 16], F32, "V3")
                memset("dve", CAMa[:], 0.0, [CAMa.r])
                memset("dve", BbM[:], 0.0, [BbM.r])
                memset("dve", WMa[:], 0.0, [WMa.r])

                def bmj(pp, dr, j, ri):
                    return BmJ[:, ((pp * 2 + dr) * 8 + j) * 2 + ri, :]

                cam6 = CAMa[:].rearrange("p t r (g a h) -> p t r g a h", a=2, h=16)
                wm6 = WMa[:].rearrange("p t r (g a h) -> p t r g a h", a=2, h=16)
                for dr in range(2):
                    bm4 = BbM[:, dr].rearrange("p r (g a h) -> p r g a h", a=2, h=16)
                    for g2 in range(2):
                        hs = slice(64 * g2, 64 * g2 + 64)
                        cp("dve", bm4[hs, 0, :, g2, :], Bbr[hs, dr, pq, :], [Bbr.r], [BbM.r])
                        cp("dve", bm4[hs, 1, :, g2, :], Bbi[hs, dr, pq, :], [Bbi.r], [BbM.r])

                def b_h(ap3, n):
                    return ap3.unsqueeze(3).to_broadcast([128, n, 4, 16])

                def b_t(ap3, n):
                    return ap3.unsqueeze(1).to_broadcast([128, n, 4, 16])

                for dr in range(2):
                    ar9, ai9 = b_h(arp[:, dr, :, pq], 9), b_h(aip[:, dr, :, pq], 9)
                    cr9, ci9 = b_t(Cp[:, dr, 0, pq, :], 9), b_t(Cp[:, dr, 1, pq, :], 9)
                    tt("dve", V1[:], cr9, ar9, ALU.mult, [Cp.r, arp.r], [V1.r])
                    tt("dve", V2[:], ci9, ai9, ALU.mult, [Cp.r, aip.r], [V2.r])
                    tt("dve", V3[:], V1[:], V2[:], ALU.subtract, [V1.r, V2.r], [V3.r])
                    for g2 in range(2):
                        hs = slice(64 * g2, 64 * g2 + 64)
                        cp("dve", cam6[hs, :, 0, :, g2, :], V3[hs], [V3.r], [CAMa.r])
                    tt("dve", V1[:], cr9, ai9, ALU.mult, [Cp.r, aip.r], [V1.r])
                    tt("dve", V2[:], ci9, ar9, ALU.mult, [Cp.r, arp.r], [V2.r])
                    stt("dve", V3[:], V1[:], -1.0, V2[:], ALU.mult, ALU.subtract, [V1.r, V2.r], [V3.r])
                    for g2 in range(2):
                        hs = slice(64 * g2, 64 * g2 + 64)
                        cp("dve", cam6[hs, :, 1, :, g2, :], V3[hs], [V3.r], [CAMa.r])
                    for ri in range(2):
                        src = CAMa[:, 1:9, ri, :] if dr == 0 else CAMa[:, 8:0:-1, ri, :]
                        cp("act", CmI[:, :, dr * 16 + ri:dr * 16 + 16:2, :], src.rearrange("p t (g c) -> p g t c", c=32),
                           [CAMa.r], [CmI.r])
                    for k4 in range(2):
                        ps, pr = PS()
                        for t4 in range(4):
                            tau = k4 * 4 + t4
                            for ri in range(2):
                                mm(ps[:, t4 * 128:(t4 + 1) * 128], BbM[:, dr, ri, :], CAMa[:, tau, ri, :], ri == 0, ri == 1,
                                   [BbM.r, CAMa.r], pr)
                        tt("dve", BDT[:, dr, k4 * 4:k4 * 4 + 4, :], ps[:, :].rearrange("p (t c) -> p t c", c=128),
                           bd32[:].unsqueeze(1).to_broadcast([128, 4, 128]), ALU.mult, [pr, bd32.r], [BDT.r])
                    if dr == 0:
                        stt("dve", BDT[:, 0, 0, :], ident[:], dsk[:, ct:ct + 1], BDT[:, 0, 0, :], ALU.mult, ALU.add,
                            [ident.r, dsk.r, BDT.r], [BDT.r])
                    if dr == 0:
                        ar8, ai8 = b_h(arp[:, dr, 7::-1, pq], 8), b_h(aip[:, dr, 7::-1, pq], 8)
                    else:
                        ar8, ai8 = b_h(arp[:, dr, 0:8, pq], 8), b_h(aip[:, dr, 0:8, pq], 8)
                    br8, bi8 = b_t(Bbr[:, dr, pq, :], 8), b_t(Bbi[:, dr, pq, :], 8)
                    tt("dve", V1[:, 0:8], br8, ar8, ALU.mult, [Bbr.r, arp.r], [V1.r])
                    tt("dve", V2[:, 0:8], bi8, ai8, ALU.mult, [Bbi.r, aip.r], [V2.r])
                    tt("dve", V3[:, 0:8], V1[:, 0:8], V2[:, 0:8], ALU.subtract, [V1.r, V2.r], [V3.r])
                    for g2 in range(2):
                        hs = slice(64 * g2, 64 * g2 + 64)
                        cp("dve", wm6[hs, :, 0, :, g2, :], V3[hs, 0:8], [V3.r], [WMa.r])
                    tt("dve", V1[:, 0:8], bi8, ar8, ALU.mult, [Bbi.r, arp.r], [V1.r])
                    tt("dve", V2[:, 0:8], br8, ai8, ALU.mult, [Bbr.r, aip.r], [V2.r])
                    tt("dve", V3[:, 0:8], V1[:, 0:8], V2[:, 0:8], ALU.add, [V1.r, V2.r], [V3.r])
                    for g2 in range(2):
                        hs = slice(64 * g2, 64 * g2 + 64)
                        cp("dve", wm6[hs, :, 1, :, g2, :], V3[hs, 0:8], [V3.r], [WMa.r])
                    for k4 in range(4):
                        ps, pr = PS()
                        for jj in range(2):
                            for ri in range(2):
                                c_ = (jj * 2 + ri) * 128
                                petr(ps[:, c_:c_ + 128], WMa[:, k4 * 2 + jj, ri, :], ident[:], [WMa.r, ident.r], pr)
                        for pp in range(2):
                            s0 = ((pp * 2 + dr) * 8 + k4 * 2) * 2
                            ts("dve", BmJ[:, s0:s0 + 4, :], ps[:, :].rearrange("p (t c) -> p t c", c=128), cpp[:, pp:pp + 1], None,
                               ALU.mult, None, [pr, cpp.r], [BmJ.r])
                for p4 in range(4):
                    half, pp = p4 // 2, p4 % 2
                    hs = slice(64 * half, 64 * half + 64)
                    for dr in range(2):
                        for ri in range(2):
                            ps, pr = PS()
                            for j in range(8):
                                mm(ps[:, 0:NS], bmj(pp, dr, j, ri)[hs, :], uJ[hs, ct, j, :], j == 0, j == 7, [BmJ.r, uJ.r], pr)
                            slot = dr * 8 + ri * 4 + p4
                            if dr == 0:
                                cp("act", Dst[:, slot, :], ps[:, 0:NS], [pr], [Dst.r])
                            else:
                                cp("act", Dst[:, slot, 0:32], ps[:, 0:32][:, ::-1], [pr], [Dst.r])
                                cp("dve", Dst[:, slot, 32:NS], ps[:, 32:NS][:, ::-1], [pr], [Dst.r])
                R.barrier()
                AR.release(mkb)
                Xbf = AR.alloc([16, NS], BF16, "Xbf")
                Ec = AR.alloc([4, NS], F32, "Ec")
                Es = AR.alloc([4, NS], F32, "Es")
                pk = AR.alloc([4, NS], I32, "pk")
                pf = AR.alloc([4, NS], F32, "pf")
                pa = AR.alloc([4, NS], F32, "pa")
                w1_ = AR.alloc([4, NS], F32, "w1_")
                w2_ = AR.alloc([4, NS], F32, "w2_")
                w3_ = AR.alloc([4, NS], F32, "w3_")
                w4_ = AR.alloc([4, NS], F32, "w4_")
                memset("dve", Xbf[:], 0.0, [Xbf.r])
                for dr in range(2):
                    tt("dve", pa[:], mI[:].unsqueeze(1).to_broadcast([128, 4, NS]),
                       phi[:, dr, pq].unsqueeze(2).to_broadcast([128, 4, NS]), ALU.mult, [mI.r, phi.r], [pa.r])
                    sincos(pa, pk, pf, Es, Ec)
                    Dr_ = Dst[:, dr * 8:dr * 8 + 4, :]
                    Di_ = Dst[:, dr * 8 + 4:dr * 8 + 8, :]
                    tt("dve", w1_[:], Dr_, Ec[:], ALU.mult, [Dst.r, Ec.r], [w1_.r])
                    tt("pool", w2_[:], Di_, Es[:], ALU.mult, [Dst.r, Es.r], [w2_.r])
                    tt("dve", w3_[:], Di_, Ec[:], ALU.mult, [Dst.r, Ec.r], [w3_.r])
                    tt("pool", w4_[:], Dr_, Es[:], ALU.mult, [Dst.r, Es.r], [w4_.r])
                    tt("dve", Dr_, w1_[:], w2_[:], ALU.add, [w1_.r, w2_.r], [Dst.r])
                    tt("dve", Di_, w3_[:], w4_[:], ALU.subtract, [w3_.r, w4_.r], [Dst.r])
                    for ri in range(2):
                        for p4 in range(4):
                            sl = dr * 8 + ri * 4 + p4
                            tscan(Dst[:, sl, :], r8[:, dr, ct * 4 + p4:ct * 4 + p4 + 1].to_broadcast([128, NS]), Dst[:, sl, :],
                                  [Dst.r, r8.r], [Dst.r])
                    M1 = NS - 1
                    Sr, Si = Dst[:, dr * 8:dr * 8 + 4, 0:M1], Dst[:, dr * 8 + 4:dr * 8 + 8, 0:M1]
                    cc_, ss_ = Ec[:, :, 0:M1], Es[:, :, 0:M1]
                    tt("dve", w1_[:, :, 0:M1], Sr, cc_, ALU.mult, [Dst.r, Ec.r], [w1_.r])
                    tt("pool", w2_[:, :, 0:M1], Si, ss_, ALU.mult, [Dst.r, Es.r], [w2_.r])
                    tt("dve", w3_[:, :, 0:M1], Si, cc_, ALU.mult, [Dst.r, Ec.r], [w3_.r])
                    tt("pool", w4_[:, :, 0:M1], Sr, ss_, ALU.mult, [Dst.r, Es.r], [w4_.r])
                    for (xo, a_, b_, op) in ((Xbf[:, dr * 8:dr * 8 + 4, :], w1_, w2_, ALU.subtract),
                                             (Xbf[:, dr * 8 + 4:dr * 8 + 8, :], w3_, w4_, ALU.add)):
                        if dr == 0:
                            tt("dve", xo[:, :, 1:NS], a_[:, :, 0:M1], b_[:, :, 0:M1], op, [a_.r, b_.r], [Xbf.r])
                        else:
                            tt("dve", xo[:, :, 0:31][:, :, ::-1], a_[:, :, 0:31], b_[:, :, 0:31], op, [a_.r, b_.r], [Xbf.r])
                            tt("dve", xo[:, :, 32:NS][:, :, ::-1], a_[:, :, 31:M1], b_[:, :, 31:M1], op, [a_.r, b_.r], [Xbf.r])
                for i in range(8):
                    ps, pr = PS()
                    for p4 in range(4):
                        cnt = 0
                        for dr in range(2):
                            for ri in range(2):
                                slot = dr * 8 + ri * 4 + p4
                                kw = dict(tile_position=(0, 96)) if p4 == 3 else {}
                                mm(ps[32 * p4:32 * p4 + 32, 0:NS], CmI[:, p4, (dr * 8 + i) * 2 + ri, :], Xbf[:, slot, :],
                                   cnt == 0, False, [CmI.r, Xbf.r], pr, sig=False, **kw)
                                cnt += 1
                    terms = []
                    for dr in range(2):
                        js = range(0, i + 1) if dr == 0 else range(i, 8)
                        for j in js:
                            terms.append((dr, j, i - j if dr == 0 else j - i))
                    for n_, (dr, j, tau) in enumerate(terms):
                        lastt = n_ == len(terms) - 1
                        mm(ps[:, 0:NS], BDT[:, dr, tau, :], uJ[:, ct, j, :], False, lastt, [BDT.r, uJ.r], pr, sig=lastt)
                    act(yT[:, ct, i:T:8], ps[:, 0:NS], AF.Gelu, [pr], [yT.r])
                R.barrier()
                AR.release(mkc)

            Wgl = AR.alloc([2, 512], BF16, "Wgl")
            gb = AR.alloc([4], F32, "gb")
            sg = [AR.alloc([512], F32, f"sg{i}") for i in range(2)]
            mst = [AR.alloc([2, 512], BF16, f"mst{i}") for i in range(2)]
            dma("pool", Wgl[:], gluw_d[l].rearrange("(k p) c -> p k c", p=128), [], [Wgl.r])
            dma("sp", gb[:], glub_d[l], [], [gb.r])
            for bi, (t0, N) in enumerate(BLKS):
                ms_ = mst[bi % 2]
                for mt in range(2):
                    psa, pra = PS()
                    psg, prg = PS()
                    for k in range(2):
                        mm(psa[:, 0:N], Wgl[:, k, mt * 128:(mt + 1) * 128], yT[:, k, t0:t0 + N], k == 0, k == 1, [Wgl.r, yT.r], pra)
                    for k in range(2):
                        mm(psg[:, 0:N], Wgl[:, k, 256 + mt * 128:256 + (mt + 1) * 128], yT[:, k, t0:t0 + N], k == 0, k == 1,
                           [Wgl.r, yT.r], prg)
                    s_ = sg[mt]
                    act(s_[:, 0:N], psg[:, 0:N], AF.Sigmoid, [prg, gb.r], [s_.r], bias=gb[:, 2 + mt:3 + mt])
                    stt("dve", ms_[:, mt, 0:N], psa[:, 0:N], gb[:, mt:mt + 1], s_[:, 0:N], ALU.add, ALU.mult, [pra, gb.r, s_.r], [ms_.r])
                dma("sp", m_scr[:, 6:8, t0:t0 + N], ms_[:, :, 0:N], [ms_.r], [mscr_r])
            R.barrier()
            AR.release(mk)

        def stage_wout(b, l):
            mk = AR.mark()
            Wo = AR.alloc([KT, D], BF16, "Wo")
            mb = [AR.alloc([KT, 512], BF16, f"mb{i}") for i in range(2)]
            load_w(Wo, wout_d[l].rearrange("(kt p) c -> p kt c", p=128), D)
            for bi, (t0, N) in enumerate(BLKS):
                j = 2 if bi == 0 else b
                m_ = mb[bi % 2]
                dma("sp", m_[:, :, 0:N], m_scr[:, :, t0:t0 + N], [mscr_r], [m_.r])
                for mt in range(KT):
                    ps, pr = PS()
                    for kt in range(KT):
                        mm(ps[:, 0:N], Wo[:, kt, mt * 128:(mt + 1) * 128], m_[:, kt, 0:N], kt == 0, kt == KT - 1, [Wo.r, m_.r], pr)
                    stt("dve", x[:, mt, t0:t0 + N], ps[:, 0:N], G1(l, mt, j), x[:, mt, t0:t0 + N], ALU.mult, ALU.add,
                        [pr, modt.r, xres[bi]], [xres[bi]])
            R.barrier()
            AR.release(mk)

        def stage_mlp(b, l):
            mk0 = AR.mark()
            W1 = [AR.alloc([KT, 1024], BF16, f"W1{i}") for i in range(2)]
            W2 = [AR.alloc([KT, 1024], BF16, f"W2{i}") for i in range(2)]
            w1v = w1_d[l].rearrange("(kt p) c -> p kt c", p=128)
            w2v = w2_d[l].rearrange("(kt p) c -> p kt c", p=128)

            def load_q(q):
                W1_, W2_ = W1[q % 2], W2[q % 2]
                for c0 in range(0, 1024, 512):
                    dma("pool", W1_[:, :, c0:c0 + 512], w1v[:, :, q * 1024 + c0:q * 1024 + c0 + 512], [], [W1_.r])
                for c0 in range(0, 1024, 512):
                    dma("pool", W2_[:, :, c0:c0 + 512], w2v[:, q * 8:(q + 1) * 8, c0:c0 + 512], [], [W2_.r])

            load_q(0)
            load_q(1)
            mk = AR.mark()
            hb = [AR.alloc([KT, 512], BF16, f"hb{i}") for i in range(2)]
            sq = AR.alloc([KT, 512], BF16, "sq")
            rstd = AR.alloc([512], F32, "rstd")
            tmpf = [AR.alloc([512], F32, f"tf{i}") for i in range(2)]
            for bi, (t0, N) in enumerate(BLKS):
                j = 2 if bi == 0 else b
                h_ = hb[bi % 2]
                rms_block(bi, lambda kt: A2[:, l, kt, j:j + 1], lambda kt: SH2(l, kt, j), h_, sq, rstd, tmpf)
                dma("sp", h_scr[:, :, t0:t0 + N], h_[:, :, 0:N], [h_.r], [hscr_r])
            R.barrier()
            AR.release(mk)
            hb = [AR.alloc([KT, 512], BF16, f"hb{i}") for i in range(2)]
            hid = [AR.alloc([KT, 512], BF16, f"hid{i}") for i in range(2)]
            rl = [AR.alloc([512], F32, f"rl{i}") for i in range(2)]
            for q in range(4):
                W1_, W2_ = W1[q % 2], W2[q % 2]
                if q >= 2:
                    load_q(q)
                for bi, (t0, N) in enumerate(BLKS):
                    j = 2 if bi == 0 else b
                    h_ = hb[bi % 2]
                    hd = hid[bi % 2]
                    dma("sp", h_[:, :, 0:N], h_scr[:, :, t0:t0 + N], [hscr_r], [h_.r])
                    for mt in range(8):
                        ps, pr = PS()
                        for kt in range(KT):
                            mm(ps[:, 0:N], W1_[:, kt, mt * 128:(mt + 1) * 128], h_[:, kt, 0:N], kt == 0, kt == KT - 1, [W1_.r, h_.r], pr)
                        r_ = rl[mt % 2]
                        act(r_[:, 0:N], ps[:, 0:N], AF.Relu, [pr], [r_.r])
                        tt("pool" if mt % 2 else "dve", hd[:, mt, 0:N], r_[:, 0:N], r_[:, 0:N], ALU.mult, [r_.r], [hd.r])
                    for mt in range(KT):
                        ps, pr = PS()
                        for kt in range(8):
                            mm(ps[:, 0:N], W2_[:, kt, mt * 128:(mt + 1) * 128], hd[:, kt, 0:N], kt == 0, kt == 7, [W2_.r, hd.r], pr)
                        stt("dve", x[:, mt, t0:t0 + N], ps[:, 0:N], G2(l, mt, j), x[:, mt, t0:t0 + N], ALU.mult, ALU.add,
                            [pr, modt.r, xres[bi]], [xres[bi]])
            R.barrier()
            AR.release(mk0)

        def stage_final(b):
            mk = AR.mark()
            sq = AR.alloc([KT, 512], BF16, "sq")
            rstd = AR.alloc([512], F32, "rstd")
            ob = [AR.alloc([KT, 512], F32, f"ob{i}") for i in range(2)]
            for bi, (t0, N) in enumerate(BLKS):
                if bi == 0:
                    continue
                o_ = ob[bi % 2]
                xr = xres[bi]
                act(sq[:, :, 0:N], x[:, :, t0:t0 + N], AF.Square, [xr], [sq.r])
                ps, pr = PS()
                for kt in range(KT):
                    mm(ps[:, 0:N], onesb[:], sq[:, kt, 0:N], kt == 0, kt == KT - 1, [onesb.r, sq.r], pr)
                act(rstd[:, 0:N], ps[:, 0:N], AF.Sqrt, [pr], [rstd.r], scale=1.0 / D, bias=EPS)
                recip(rstd[:, 0:N], rstd[:, 0:N], [rstd.r], [rstd.r])
                for kt in range(KT):
                    stt("dve", o_[:, kt, 0:N], x[:, kt, t0:t0 + N], nrm[:, 8, kt:kt + 1], rstd[:, 0:N], ALU.mult, ALU.mult,
                        [xr, nrm.r, rstd.r], [o_.r])
                dma("sp", out_d[b, :, :, t0 - TCX:t0 - TCX + N], o_[:, :, 0:N], [o_.r], [])
            R.barrier()
            AR.release(mk)

        for b in range(n_b):
            for bi, (t0, N) in enumerate(BLKS):
                dma("sp", x[:, :, t0:t0 + N], xin[b, :, :, t0:t0 + N], [], [xres[bi]])
            for l in range(n_layers):
                if "n" in stages:
                    stage_norm1(b, l)
                if "g" in stages:
                    stage_gla(b, l)
                if "a" in stages:
                    stage_att(b, l)
                if "s" in stages:
                    stage_s5(b, l)
                if "w" in stages:
                    stage_wout(b, l)
                if dbg and b == 0 and l == 0:
                    R.barrier(force=True)
                    for bi_, (t0_, N_) in enumerate(BLKS):
                        dma("sp", dbg_xa[:, :, t0_:t0_ + N_], x[:, :, t0_:t0_ + N_], [xres[bi_]], [])
                if "m" in stages:
                    stage_mlp(b, l)
                if dbg and b == 0 and l == 0:
                    R.barrier(force=True)
                    for bi_, (t0_, N_) in enumerate(BLKS):
                        dma("sp", dbg_xb[:, :, t0_:t0_ + N_], x[:, :, t0_:t0_ + N_], [xres[bi_]], [])
            stage_final(b)
        R.barrier(force=True)
        print("ops", R.n_ops, "waits", R.n_waits, "sems", len(R.sems), "arena peak", AR.peak, {k: len(v) for k, v in R.prog.items()}, flush=True)
        R.emit()
    return nc


def _perm64():
    p = np.zeros(64, np.int64)
    for d in range(64):
        q = d // 16
        p[d] = d + 16 if q % 2 == 0 else d - 16
    return p


def _w_in_layouts(w_in):
    Lc = w_in.shape[0]
    oQ, oK, oV, oG, oZF, oZB, oAQ, oAK, oAV, oU = 0, 192, 384, 768, 1152, 1168, 1184, 1568, 1696, 1824
    fm = np.full(1920, -1, np.int64)
    for pr in range(2):
        for hh in range(2):
            h = 2 * pr + hh
            fm[pr * 128 + hh * 64: pr * 128 + hh * 64 + 48] = oQ + 48 * h + np.arange(48)
            fm[256 + pr * 128 + hh * 64: 256 + pr * 128 + hh * 64 + 48] = oK + 48 * h + np.arange(48)
    fm[512:528] = oZF + np.arange(16)
    fm[544:560] = oZB + np.arange(16)
    perm = _perm64()
    base = 640
    for m, (ha, hb) in enumerate([(0, 3), (1, 4), (2, 5)]):
        for s, h in enumerate((ha, hb)):
            fm[base + m * 128 + s * 64: base + m * 128 + (s + 1) * 64] = oAQ + 64 * h + np.arange(64)
            fm[base + (3 + m) * 128 + s * 64: base + (3 + m) * 128 + (s + 1) * 64] = oAQ + 64 * h + perm
    for s in range(2):
        fm[base + 768 + s * 64: base + 768 + (s + 1) * 64] = oAK + 64 * s + np.arange(64)
        fm[base + 896 + s * 64: base + 896 + (s + 1) * 64] = oAK + 64 * s + perm
    fm[1664:1920] = oU + np.arange(256)
    tm = np.full(1152, -1, np.int64)
    for h in range(4):
        tm[h * 64:h * 64 + 48] = oK + 48 * h + np.arange(48)
    tm[256:640] = oV + np.arange(384)
    tm[640:1024] = oG + np.arange(384)
    tm[1024:1152] = oAV + np.arange(128)

    def gather(idx):
        out = np.zeros((Lc, w_in.shape[1], idx.size), np.float32)
        sel = idx >= 0
        out[:, :, sel] = w_in[:, :, idx[sel]]
        return out
    return gather(fm), gather(tm)


def _constants():
    j = np.arange(128)[:, None]
    i = np.arange(128)[None, :]
    c = {}
    tri = np.zeros((128, 4, 128), np.float32)
    tri[:, 0, :] = (j <= i) * (-1.0 / 16)
    tri[:, 1, :] = (j >= i) * (-1.0 / 16)
    tri[:, 2, :] = (j > i) * (-1.0 / 16)
    tri[:, 3, :] = (j < i) * (-1.0 / 16)
    c["c_tri"] = tri
    msk = np.zeros((128, 2, 128), np.float32)
    msk[:, 0, :] = (j <= i)
    msk[:, 1, :] = (j >= i)
    c["c_mask"] = msk
    c["c_ident"] = np.eye(128, dtype=np.float32)
    rows = TLAT // 64
    row = np.repeat(np.arange(rows, dtype=np.float32), 64)
    col = np.tile(np.arange(64, dtype=np.float32), rows)
    inv = (10000.0 ** (-np.arange(16, dtype=np.float32) / 16)).astype(np.float32)
    ang = np.concatenate([row[:, None] * inv, row[:, None] * inv, col[:, None] * inv, col[:, None] * inv], axis=-1)
    sign = np.concatenate([-np.ones(16), np.ones(16), -np.ones(16), np.ones(16)]).astype(np.float32)
    cosT = np.cos(ang).T.astype(np.float32)
    sinT = (np.sin(ang) * sign[None, :]).T.astype(np.float32)
    rope = np.zeros((128, 2, TLAT), np.float32)
    rope[:, 0, :] = np.concatenate([cosT, cosT], 0)
    rope[:, 1, :] = np.concatenate([sinT, sinT], 0)
    c["c_rope"] = rope
    r = np.arange(128)
    c["c_bd32"] = (r[:, None] // 32 == r[None, :] // 32).astype(np.float32)
    cpp = np.zeros((128, 2), np.float32)
    for pp in range(2):
        cpp[:, pp] = ((r % 64) // 32 == pp)
    c["c_pp"] = cpp
    c["c_m"] = np.broadcast_to(np.arange(1, 289, dtype=np.float32)[None, :], (128, 288)).copy()
    return c


def _col_layout(v):
    m = v.shape[-1] // 128
    return np.ascontiguousarray(np.swapaxes(v.reshape(v.shape[:-1] + (m, 128)), -1, -2))


def _prep_shared(inp):
    f = lambda k: np.asarray(inp[k], np.float32)
    sh = {}
    sh["w_mod"] = f("w_mod")
    sh["bmod"] = _col_layout(f("b_mod"))
    nr = np.zeros((128, 9, KT), np.float32)
    n1, n2 = _col_layout(f("norm1_w")), _col_layout(f("norm2_w"))
    for l in range(L):
        nr[:, l, :] = n1[l]
        nr[:, 4 + l, :] = n2[l]
    nr[:, 8, :] = _col_layout(f("final_norm_w"))
    sh["nrm"] = nr
    sh["wfm"], sh["wtm"] = _w_in_layouts(f("w_in"))
    wa = np.zeros((L, 48, 512), np.float32)
    ba = np.zeros((L, 512), np.float32)
    for dr, (wk, bk) in enumerate((("gla_wa_f", "gla_ba_f"), ("gla_wa_b", "gla_ba_b"))):
        w_, b_ = f(wk), f(bk)
        for h in range(4):
            wa[:, 32 * dr:32 * dr + 16, dr * 256 + h * 64:dr * 256 + h * 64 + 48] = w_[:, :, h * 48:(h + 1) * 48]
            ba[:, dr * 256 + h * 64: dr * 256 + h * 64 + 48] = b_[:, h * 48:(h + 1) * 48]
    sh["wa"], sh["ba"] = wa, ba
    sh["gnw"] = np.tile(f("gla_norm_w"), (1, 4))
    sh["sink"] = f("attn_sink")
    sh["w_out"], sh["mlp_w1"], sh["mlp_w2"] = f("w_out"), f("mlp_w1"), f("mlp_w2")
    sh["glu_w"] = f("glu_w")
    sh["glub"] = _col_layout(f("glu_b"))
    lam = np.zeros((L, 2, 128, 2, 8), np.float32)
    stp = np.zeros((L, 2, 128, 8), np.float32)
    Bm = np.zeros((L, 2, 128, 2, 8, 16), np.float32)
    Cm = np.zeros((L, 2, 128, 2, 8, 16), np.float32)
    for dr, tg in enumerate(("f", "b")):
        lre, lim, ls = f("s5_lam_re_" + tg), f("s5_lam_im_" + tg), f("s5_log_step_" + tg)
        bre, bim, cre, cim = f("s5_b_re_" + tg), f("s5_b_im_" + tg), f("s5_c_re_" + tg), f("s5_c_im_" + tg)
        for g2 in range(2):
            ps_ = slice(64 * g2, 64 * g2 + 64)
            gsel = np.arange(8) * 2 + g2
            lam[:, dr, ps_, 0, :] = np.transpose(lre[:, gsel, :], (0, 2, 1))
            lam[:, dr, ps_, 1, :] = np.transpose(lim[:, gsel, :], (0, 2, 1))
            stp[:, dr, ps_, :] = ls[:, None, gsel]
            Bm[:, dr, ps_, 0] = np.transpose(bre[:, gsel], (0, 2, 1, 3))
            Bm[:, dr, ps_, 1] = np.transpose(bim[:, gsel], (0, 2, 1, 3))
            Cm[:, dr, ps_, 0] = np.transpose(cre[:, gsel], (0, 3, 1, 2))
            Cm[:, dr, ps_, 1] = np.transpose(cim[:, gsel], (0, 3, 1, 2))
    sh["s5lam"], sh["s5step"], sh["s5B"], sh["s5C"] = lam, stp, Bm, Cm
    sh["s5d"] = _col_layout(f("s5_d"))
    sh.update(_constants())
    return sh


def _prep_core(inp, core):
    x, ctx, c, c_ctx = (np.asarray(inp[k], np.float32) for k in ("x", "ctx", "c", "c_ctx"))
    xin = np.zeros((NBC, 128, KT, T), np.float32)
    cT = np.zeros((128, KT, 3), np.float32)
    for bb in range(NBC):
        b = core * NBC + bb
        seq = np.concatenate([ctx[b], x[b]], axis=0)
        xin[bb] = np.transpose(seq.T.reshape(KT, 128, T), (1, 0, 2))
        cT[:, :, bb] = c[b].reshape(KT, 128).T
    cT[:, :, 2] = c_ctx.reshape(KT, 128).T
    return {"xin": xin, "cT": cT}


_NC_CACHE = {}


def kernel(**inputs):
    n = 8
    shared = _prep_shared(inputs)
    in_maps = []
    for core in range(n):
        m = dict(shared)
        m.update(_prep_core(inputs, core))
        in_maps.append(m)
    if "nc" not in _NC_CACHE:
        _NC_CACHE["nc"] = build_program()
    res = run_bass_kernel_spmd(_NC_CACHE["nc"], in_maps, core_ids=list(range(n)))
    B = np.asarray(inputs["x"]).shape[0]
    out = np.zeros((B, TLAT, D), np.float32)
    for core in range(n):
        o = np.asarray(res.results[core]["out"])
        for bb in range(NBC):
            out[core * NBC + bb] = np.transpose(o[bb], (2, 1, 0)).reshape(TLAT, D)
    return out
```
